# Optimizing a Trainium2 kernel written in Bass

```python
import jax, jax.numpy as jnp
from jax import lax
import numpy as np

D_MODEL = 2048
BATCH = 4
SEQ = 2048
DEPTH = 2
DEC_BATCH = 128
DEC_SEQ = 8
PAST_LEN = 16384
PAGE_SIZE = 128

N_META = 16
N_AB = (DEPTH + 1) // 2
N_C = DEPTH // 2
MIX_A = D_MODEL // 2
NH_A = 4
HD_A = MIX_A // NH_A
CHUNK_A = 128
RG_W = D_MODEL // 2
RG_BLOCKS = 8
RG_BW = RG_W // RG_BLOCKS
CONV_W = 4
RG_C = 8.0
PROJ_AB = 4 * MIX_A + 2 * NH_A + 2 * RG_W
HD_C = 64
NH_C = D_MODEL // HD_C
DECAY_LORA = 96
AAA_LORA = 96
GATE_LORA = 256
D_FF = 4 * D_MODEL
LN_EPS = 1e-5
GN_EPS = 64e-5
ALPHA = (2.0 * DEPTH) ** 0.25
BETA = (8.0 * DEPTH) ** -0.25

kernel_name = 'hybrid_mlstm_rglru_rwkv7_step'


def layer_norm(x, g, b, eps=LN_EPS):
    xf = x.astype(jnp.float32)
    mu = jnp.mean(xf, axis=-1, keepdims=True)
    var = jnp.mean(jnp.square(xf - mu), axis=-1, keepdims=True)
    return ((xf - mu) * lax.rsqrt(var + eps) * g + b).astype(x.dtype)


def sq_relu_mlp(x, w_up, w_down):
    return jnp.square(jax.nn.relu(x @ w_up)) @ w_down


def mlstm_chunk(carry, inp):
    C, n, m = carry
    q, k, v, ig, lf = inp
    L = q.shape[1]
    Fh = jnp.swapaxes(jnp.cumsum(lf, axis=1), 1, 2)
    igh = jnp.swapaxes(ig, 1, 2)
    logD = Fh[..., :, None] - Fh[..., None, :] + igh[..., None, :]
    logD = jnp.where(jnp.tril(jnp.ones((L, L), bool)), logD, -jnp.inf)
    b = Fh + m[..., None]
    m_row = jnp.maximum(b, jnp.max(logD, axis=-1))
    inter = jnp.exp(b - m_row)
    s = jnp.einsum('blhd,bmhd->bhlm', q, k) * jnp.exp(logD - m_row[..., None])
    num = jnp.einsum('bhlm,bmhd->bhld', s, v) + inter[..., None] * jnp.einsum('blhk,bhkv->bhlv', q, C)
    den = jnp.sum(s, axis=-1) + inter * jnp.einsum('blhk,bhk->bhl', q, n)
    h = num / jnp.maximum(jnp.abs(den), jnp.exp(-m_row))[..., None]
    FL = Fh[..., -1]
    wj = FL[..., None] - Fh + igh
    m_new = jnp.maximum(FL + m, jnp.max(wj, axis=-1))
    wexp = jnp.exp(wj - m_new[..., None])
    dec = jnp.exp(FL + m - m_new)
    C_new = dec[..., None, None] * C + jnp.einsum('bhl,blhk,blhv->bhkv', wexp, k, v)
    n_new = dec[..., None] * n + jnp.einsum('bhl,blhk->bhk', wexp, k)
    return (C_new, n_new, m_new), jnp.swapaxes(h, 1, 2)


def mlstm_seq(state, q, k, v, ig, lf, lead):
    B, T = q.shape[:2]
    ins = (q, k, v, ig, lf)
    if lead > 0:
        state, h_lead = mlstm_chunk(state, tuple(a[:, :lead] for a in ins))
        ins = tuple(a[:, lead:] for a in ins)
    rest = T - lead
    if rest % CHUNK_A == 0 and rest >= CHUNK_A:
        nc = rest // CHUNK_A
        split = lambda a: jnp.swapaxes(a.reshape((B, nc, CHUNK_A) + a.shape[2:]), 0, 1)
        state, h = lax.scan(mlstm_chunk, state, tuple(split(a) for a in ins))
        h = jnp.swapaxes(h, 0, 1)
        h = h.reshape((B, rest) + h.shape[3:])
    else:
        state, h = mlstm_chunk(state, ins)
    if lead > 0:
        h = jnp.concatenate([h_lead, h], axis=1)
    return state, h


def causal_conv(xb, buf, w, b):
    T = xb.shape[1]
    xp = jnp.concatenate([buf.astype(xb.dtype), xb], axis=1)
    out = b + sum(xp[:, j:j + T] * w[j] for j in range(CONV_W))
    return out, xp[:, T:]


def block_diag(x, w, b):
    xb = x.reshape(x.shape[:-1] + (RG_BLOCKS, RG_BW))
    return jnp.einsum('btni,nij->btnj', xb, w).reshape(x.shape) + b


def _lin_combine(e1, e2):
    return (e1[0] * e2[0], e2[0] * e1[1] + e2[1])


def rglru(x, h0, wa, ba, wx, bx, lam):
    r = jax.nn.sigmoid(block_diag(x, wa, ba))
    gi = jax.nn.sigmoid(block_diag(x, wx, bx))
    log_a = -RG_C * r * jax.nn.softplus(-lam.astype(jnp.float32))
    a = jnp.exp(log_a)
    u = jnp.sqrt(-jnp.expm1(2.0 * log_a)) * (gi * x)
    u = u.at[:, 0].add(a[:, 0] * h0)
    _, h = lax.associative_scan(_lin_combine, (a, u), axis=1)
    return h, h[:, -1]


def ab_mixer(x, C0, n0, m0, h0, conv0, p, i, lead):
    B, T, _ = x.shape
    z = x @ p['w_in_ab'][i]
    cuts = [MIX_A, 2 * MIX_A, 3 * MIX_A, 4 * MIX_A, 4 * MIX_A + 2 * NH_A, 4 * MIX_A + 2 * NH_A + RG_W]
    q, k, v, o, gates, xr, gr = jnp.split(z, cuts, axis=-1)
    heads = lambda t: t.astype(jnp.float32).reshape(B, T, NH_A, HD_A)
    gates = gates.astype(jnp.float32) + p['b_if_ab'][i].astype(jnp.float32)
    ig, lf = gates[..., :NH_A], jax.nn.log_sigmoid(gates[..., NH_A:])
    st0 = (C0.astype(jnp.float32), n0.astype(jnp.float32), m0.astype(jnp.float32))
    (C, n, m), hm = mlstm_seq(st0, heads(q), heads(k) * HD_A ** -0.5, heads(v), ig, lf, lead)
    mu = jnp.mean(hm, axis=-1, keepdims=True)
    var = jnp.mean(jnp.square(hm - mu), axis=-1, keepdims=True)
    hm = (hm - mu) * lax.rsqrt(var + LN_EPS) * p['mlstm_norm_g'][i].reshape(NH_A, HD_A)
    hm = hm.reshape(B, T, MIX_A) * jax.nn.sigmoid(o.astype(jnp.float32))
    xc, conv_new = causal_conv(xr, conv0, p['rg_conv_w'][i], p['rg_conv_b'][i])
    hr, h_last = rglru(xc.astype(jnp.float32), h0.astype(jnp.float32), p['rg_wa'][i], p['rg_ba'][i],
                       p['rg_wx'][i], p['rg_bx'][i], p['rg_lambda'][i])
    yr = hr * jax.nn.gelu(gr.astype(jnp.float32))
    y = jnp.concatenate([hm, yr], axis=-1).astype(x.dtype) @ p['w_out_ab'][i]
    return y.astype(x.dtype), C, n, m, h_last, conv_new


def rwkv_step(S, inp):
    r, w, k, v, a, b = inp
    sa = jnp.einsum('bhvk,bhk->bhv', S, a)
    S = S * w[:, :, None, :] + sa[..., None] * b[:, :, None, :] + v[..., None] * k[:, :, None, :]
    return S, jnp.einsum('bhvk,bhk->bhv', S, r)


def rwkv_mixer(x, S0, x_last, p, i):
    B, T, D = x.shape
    xf = x.astype(jnp.float32)
    x_prev = jnp.concatenate([x_last.astype(jnp.float32)[:, None], xf[:, :-1]], axis=1)
    xs = xf[:, :, None] + (x_prev - xf)[:, :, None] * p['rw_mu'][i]
    xr, xw, xk, xv, xa, xg = (xs[:, :, j] for j in range(6))
    r = xr @ p['rw_wr'][i]
    k = xk @ p['rw_wk'][i]
    v = xv @ p['rw_wv'][i]
    w = -jax.nn.softplus(-(p['rw_w0'][i] + jnp.tanh(xw @ p['rw_w1'][i]) @ p['rw_w2'][i])) - 0.5
    a = jax.nn.sigmoid(p['rw_a0'][i] + (xa @ p['rw_a1'][i]) @ p['rw_a2'][i])
    g = jax.nn.sigmoid(xg @ p['rw_g1'][i]) @ p['rw_g2'][i]
    heads = lambda t: t.astype(jnp.float32).reshape(B, T, NH_C, HD_C)
    kk = heads(k * p['rw_kk'][i])
    kk = kk / jnp.maximum(jnp.sqrt(jnp.sum(jnp.square(kk), axis=-1, keepdims=True)), 1e-12)
    k = k * (1.0 + (a - 1.0) * p['rw_ka'][i])
    rh, kh, vh = heads(r), heads(k), heads(v)
    seq = (rh, heads(jnp.exp(-jnp.exp(w))), kh, vh, -kk, kk * heads(a))
    S, y = lax.scan(rwkv_step, S0.astype(jnp.float32), tuple(jnp.moveaxis(t, 1, 0) for t in seq))
    y = jnp.moveaxis(y, 0, 1)
    mu = jnp.mean(y, axis=-1, keepdims=True)
    var = jnp.mean(jnp.square(y - mu), axis=-1, keepdims=True)
    yn = ((y - mu) * lax.rsqrt(var + GN_EPS)).reshape(B, T, D) * p['rw_lnx_g'][i] + p['rw_lnx_b'][i]
    bonus = jnp.sum(rh * kh * p['rw_rk'][i], axis=-1, keepdims=True) * vh
    out = ((yn + bonus.reshape(B, T, D)) * g).astype(x.dtype) @ p['rw_wo'][i]
    return out.astype(x.dtype), S, x[:, -1]


def trunk(h, states, p, lead):
    mC, mn, mm, rgh, rgc, rwS, rwx = states
    out = [[] for _ in range(7)]
    for layer in range(DEPTH):
        i = layer // 2
        if layer % 2 == 0:
            y, *new = ab_mixer(h, mC[i], mn[i], mm[i], rgh[i], rgc[i], p, i, lead)
            for lst, s in zip(out[:5], new):
                lst.append(s)
        else:
            y, *new = rwkv_mixer(h, rwS[i], rwx[i], p, i)
            for lst, s in zip(out[5:], new):
                lst.append(s)
        h = layer_norm(ALPHA * h + y, p['ln1_g'][layer], p['ln1_b'][layer])
        h = layer_norm(ALPHA * h + sq_relu_mlp(h, p['w_up'][layer], p['w_down'][layer]),
                       p['ln2_g'][layer], p['ln2_b'][layer])
    return h, [jnp.stack(lst) for lst in out]


def setup_inputs(seed: int = 0) -> dict:
    key = jax.random.key(seed)
    keys = list(jax.random.split(key, 64))
    nrm = lambda shape, scale: scale * jax.random.normal(keys.pop(), shape, jnp.float32)
    unif = lambda shape, lo, hi: jax.random.uniform(keys.pop(), shape, jnp.float32, lo, hi)
    D = D_MODEL
    s = unif((N_AB, RG_W), 0.9, 0.999) ** (1.0 / RG_C)
    return {
        'x_prompt': nrm((BATCH, SEQ, D), 1.0),
        'x_sample': nrm((DEC_BATCH, DEC_SEQ, D), 1.0),
        'state_mlstm_C': nrm((N_AB, DEC_BATCH, NH_A, HD_A, HD_A), 0.05),
        'state_mlstm_n': nrm((N_AB, DEC_BATCH, NH_A, HD_A), 0.05),
        'state_mlstm_m': nrm((N_AB, DEC_BATCH, NH_A), 1.0),
        'state_rglru_h': nrm((N_AB, DEC_BATCH, RG_W), 0.5),
        'state_rglru_conv': nrm((N_AB, DEC_BATCH, CONV_W - 1, RG_W), 1.0),
        'state_rwkv_S': nrm((N_C, DEC_BATCH, NH_C, HD_C, HD_C), 0.1),
        'state_rwkv_shift': nrm((N_C, DEC_BATCH, D), 1.0),
        'meta_tokens': nrm((N_META, D), 1.0),
        'w_in_ab': nrm((N_AB, D, PROJ_AB), D ** -0.5),
        'b_if_ab': jnp.concatenate([nrm((N_AB, NH_A), 0.1), 3.0 + unif((N_AB, NH_A), 0.0, 3.0)], axis=-1),
        'mlstm_norm_g': 1.0 + nrm((N_AB, MIX_A), 0.02),
        'rg_conv_w': nrm((N_AB, CONV_W, RG_W), CONV_W ** -0.5),
        'rg_conv_b': nrm((N_AB, RG_W), 0.02),
        'rg_wa': nrm((N_AB, RG_BLOCKS, RG_BW, RG_BW), RG_BW ** -0.5),
        'rg_ba': nrm((N_AB, RG_W), 0.02),
        'rg_wx': nrm((N_AB, RG_BLOCKS, RG_BW, RG_BW), RG_BW ** -0.5),
        'rg_bx': nrm((N_AB, RG_W), 0.02),
        'rg_lambda': jnp.log(s) - jnp.log1p(-s),
        'w_out_ab': nrm((N_AB, MIX_A + RG_W, D), (MIX_A + RG_W) ** -0.5 * BETA),
        'rw_mu': unif((N_C, 6, D), 0.0, 1.0),
        'rw_wr': nrm((N_C, D, D), D ** -0.5),
        'rw_wk': nrm((N_C, D, D), D ** -0.5),
        'rw_wv': nrm((N_C, D, D), D ** -0.5),
        'rw_wo': nrm((N_C, D, D), D ** -0.5 * BETA),
        'rw_w0': unif((N_C, D), -6.0, 0.5),
        'rw_w1': nrm((N_C, D, DECAY_LORA), D ** -0.5),
        'rw_w2': nrm((N_C, DECAY_LORA, D), 0.1 * DECAY_LORA ** -0.5),
        'rw_a0': nrm((N_C, D), 0.1),
        'rw_a1': nrm((N_C, D, AAA_LORA), D ** -0.5),
        'rw_a2': nrm((N_C, AAA_LORA, D), 0.1 * AAA_LORA ** -0.5),
        'rw_g1': nrm((N_C, D, GATE_LORA), D ** -0.5),
        'rw_g2': nrm((N_C, GATE_LORA, D), GATE_LORA ** -0.5),
        'rw_kk': 0.85 + nrm((N_C, D), 0.02),
        'rw_ka': 1.0 + nrm((N_C, D), 0.02),
        'rw_rk': nrm((N_C, NH_C, HD_C), 0.1),
        'rw_lnx_g': 1.0 + nrm((N_C, D), 0.02),
        'rw_lnx_b': nrm((N_C, D), 0.02),
        'ln1_g': 1.0 + nrm((DEPTH, D), 0.02),
        'ln1_b': nrm((DEPTH, D), 0.02),
        'ln2_g': 1.0 + nrm((DEPTH, D), 0.02),
        'ln2_b': nrm((DEPTH, D), 0.02),
        'w_up': nrm((DEPTH, D, D_FF), D ** -0.5),
        'w_down': nrm((DEPTH, D_FF, D), D_FF ** -0.5 * BETA),
    }


def reference(x_prompt, x_sample, state_mlstm_C, state_mlstm_n, state_mlstm_m, state_rglru_h,
              state_rglru_conv, state_rwkv_S, state_rwkv_shift, meta_tokens, w_in_ab, b_if_ab,
              mlstm_norm_g, rg_conv_w, rg_conv_b, rg_wa, rg_ba, rg_wx, rg_bx, rg_lambda, w_out_ab,
              rw_mu, rw_wr, rw_wk, rw_wv, rw_wo, rw_w0, rw_w1, rw_w2, rw_a0, rw_a1, rw_a2, rw_g1,
              rw_g2, rw_kk, rw_ka, rw_rk, rw_lnx_g, rw_lnx_b, ln1_g, ln1_b, ln2_g, ln2_b, w_up, w_down):
    p = dict(w_in_ab=w_in_ab, b_if_ab=b_if_ab, mlstm_norm_g=mlstm_norm_g, rg_conv_w=rg_conv_w,
             rg_conv_b=rg_conv_b, rg_wa=rg_wa, rg_ba=rg_ba, rg_wx=rg_wx, rg_bx=rg_bx,
             rg_lambda=rg_lambda, w_out_ab=w_out_ab, rw_mu=rw_mu, rw_wr=rw_wr, rw_wk=rw_wk,
             rw_wv=rw_wv, rw_wo=rw_wo, rw_w0=rw_w0, rw_w1=rw_w1, rw_w2=rw_w2, rw_a0=rw_a0,
             rw_a1=rw_a1, rw_a2=rw_a2, rw_g1=rw_g1, rw_g2=rw_g2, rw_kk=rw_kk, rw_ka=rw_ka,
             rw_rk=rw_rk, rw_lnx_g=rw_lnx_g, rw_lnx_b=rw_lnx_b, ln1_g=ln1_g, ln1_b=ln1_b,
             ln2_g=ln2_g, ln2_b=ln2_b, w_up=w_up, w_down=w_down)
    B = x_prompt.shape[0]
    dt = x_prompt.dtype
    zero_states = (
        jnp.zeros((N_AB, B, NH_A, HD_A, HD_A), jnp.float32),
        jnp.zeros((N_AB, B, NH_A, HD_A), jnp.float32),
        jnp.zeros((N_AB, B, NH_A), jnp.float32),
        jnp.zeros((N_AB, B, RG_W), jnp.float32),
        jnp.zeros((N_AB, B, CONV_W - 1, RG_W), dt),
        jnp.zeros((N_C, B, NH_C, HD_C, HD_C), jnp.float32),
        jnp.zeros((N_C, B, D_MODEL), dt),
    )
    meta = jnp.broadcast_to(meta_tokens.astype(dt)[None], (B, N_META, D_MODEL))
    hp, ps = trunk(jnp.concatenate([meta, x_prompt], axis=1), zero_states, p, N_META)
    sample_states = (state_mlstm_C, state_mlstm_n, state_mlstm_m, state_rglru_h, state_rglru_conv,
                     state_rwkv_S, state_rwkv_shift)
    hs, ss = trunk(x_sample, sample_states, p, 0)
    return (hp[:, N_META:], hs, ps[0], ps[1], ps[2], ps[3], ps[4], ps[5], ps[6],
            ss[0], ss[1], ss[2], ss[3], ss[4], ss[5], ss[6])
```

```python
import os
import numpy as np
import concourse.bass as bass
import concourse.mybir as mybir
from concourse.bass_utils import run_bass_kernel_spmd

F32 = mybir.dt.float32
BF16 = mybir.dt.bfloat16
AF = mybir.ActivationFunctionType
ALU = mybir.AluOpType
AX = mybir.AxisListType

LN_EPS = 1e-5
GN_EPS = 64e-5
NEG = -30000.0


class Cfg:
    def __init__(self, D=2048, NHA=4, nchunks=16, NS=16, LS=8, DEPTH=2, lora_w=96, lora_a=96, lora_g=256):
        self.D = D
        self.NC = D // 128
        self.MIXA = D // 2
        self.NHA = NHA
        self.HDA = self.MIXA // NHA
        self.HKC = self.HDA // 128
        self.MC = self.MIXA // 128
        self.RGW = D // 2
        self.RGB = self.RGW // 128
        self.NHC = D // 64
        self.DFF = 4 * D
        self.FC = self.DFF // 128
        self.nchunks = nchunks
        self.TP = 16 + 128 * nchunks
        self.NS = NS
        self.LS = LS
        self.TS = NS * LS
        self.LW, self.LA, self.LG = lora_w, lora_a, lora_g
        self.ALPHA = (2.0 * DEPTH) ** 0.25
        tiles = [("p", 0, 16)] + [("p", 16 + 128 * i, 128) for i in range(nchunks)]
        blocks = []
        cur = []
        n = 0
        for t in tiles:
            if n + t[2] > 512:
                blocks.append(cur)
                cur, n = [], 0
            cur.append(t)
            n += t[2]
        if n + self.TS > 512:
            blocks.append(cur)
            cur = []
        cur.append(("s", 0, self.TS))
        blocks.append(cur)
        self.blocks = blocks


class T:
    __slots__ = ("ap", "name", "w", "r", "dsem", "dcount")

    def __init__(self, ap, name):
        self.ap, self.name = ap, name
        self.w, self.r = {}, {}
        self.dsem, self.dcount = None, 0

    def __getitem__(self, k):
        return V(self, self.ap[k])

    @property
    def v(self):
        return V(self, self.ap)


class V:
    __slots__ = ("t", "ap", "ts")

    def __init__(self, t, ap, ts=None):
        self.t, self.ap, self.ts = t, ap, ts

    def __getitem__(self, k):
        return V(self.t, self.ap[k], self.ts)

    def bc(self, shape):
        return V(self.t, self.ap.to_broadcast(list(shape)), self.ts)

    def re(self, pat, **kw):
        return V(self.t, self.ap.rearrange(pat, **kw), self.ts)

    def un(self, axis):
        return V(self.t, self.ap.unsqueeze(axis), self.ts)

    def tiles(self):
        return self.ts if self.ts is not None else [self.t]


def _ap(x):
    return x.ap if isinstance(x, V) else x


class Fw:
    ENG = ("pe", "dve", "act", "pool", "sp")

    def __init__(self, nc, dry=False):
        self.nc, self.dry = nc, dry
        self.eng = {"pe": nc.tensor, "dve": nc.vector, "act": nc.scalar, "pool": nc.gpsimd, "sp": nc.sync}
        self.cnt = {e: 0 for e in self.ENG}
        self.waited = {e: {} for e in self.ENG}
        self.dsems = {}
        self.sem = {}
        if not dry:
            self.sem = {e: nc.alloc_semaphore("s_" + e) for e in self.ENG}
        self.n_inst = 0
        self._uid = 0
        self.dma_keys = {}
        self.free_dsems = []
        self.scopes = []
        self.freed = {}
        self.min_free = 1 << 30

    def scope(self):
        fw = self

        class _S:
            def __enter__(s):
                s.tiles = []
                s.guards = []
                fw.scopes.append(s)
                return s

            def __exit__(s, *a):
                fw.scopes.pop()
                for t in s.tiles:
                    for d in (t.w, t.r):
                        for k, v in d.items():
                            if fw.freed.get(k, 0) < v:
                                fw.freed[k] = v
                    if t.dsem is not None:
                        fw.free_dsems.append(t.dsem)
                for g in reversed(s.guards):
                    g.__exit__(None, None, None)
                return False
        return _S()

    def sb(self, shape, dtype=F32, name=None):
        self._uid += 1
        name = (name or "sb") + f"_{self._uid}"
        if self.scopes:
            g = self.nc.sbuf_tensor(name, list(shape), dtype)
            h = g.__enter__()
            t = T(h.ap(), name)
            t.w = dict(self.freed)
            self.scopes[-1].tiles.append(t)
            self.scopes[-1].guards.append(g)
        else:
            t = T(self.nc.alloc_sbuf_tensor(name, list(shape), dtype).ap(), name)
        self.min_free = min(self.min_free, self.nc.sbuf_bytes_remaining)
        return t

    def ps(self, shape, dtype=F32, name=None):
        self._uid += 1
        name = (name or "ps") + f"_{self._uid}"
        return self.nc.alloc_psum_tensor(name, list(shape), dtype).ap()

    def _collect(self, reads, writes, eng, skip=None):
        deps = {}
        for t in reads:
            for k, v in t.w.items():
                if k != skip and deps.get(k, 0) < v:
                    deps[k] = v
        for t in writes:
            for k, v in t.w.items():
                if (k == eng and eng == "pe") or k == skip:
                    continue
                if deps.get(k, 0) < v:
                    deps[k] = v
            for k, v in t.r.items():
                if k == skip:
                    continue
                if deps.get(k, 0) < v:
                    deps[k] = v
        return deps

    def _waits(self, eng, deps):
        e = self.eng[eng]
        wd = self.waited[eng]
        for k, v in deps.items():
            if wd.get(k, 0) >= v:
                continue
            sem = self.sem[k] if k in self.sem else self.dsems[k]
            e.wait_ge(sem, v)
            wd[k] = v

    def op(self, eng, fn, ins, outs):
        self.n_inst += 1
        if self.dry:
            return
        reads = []
        for x in ins:
            if isinstance(x, V):
                for t in x.tiles():
                    if t not in reads:
                        reads.append(t)
        writes = []
        for x in outs:
            for t in x.tiles():
                if t not in writes:
                    writes.append(t)
        deps = self._collect(reads, writes, eng)
        self._waits(eng, deps)
        ins_ = fn()
        self.cnt[eng] += 1
        idx = self.cnt[eng]
        ins_.then_inc(self.sem[eng], 1)
        for t in writes:
            t.w = {eng: idx}
            t.r = {}
        for t in reads:
            if t not in writes:
                t.r[eng] = idx

    def dma(self, q, out, in_, join=False):
        self.n_inst += 1
        if self.dry:
            return
        reads = in_.tiles() if isinstance(in_, V) else []
        writes = out.tiles() if isinstance(out, V) else []
        st = (writes[0] if writes else reads[0])
        if st.dsem is None:
            if self.free_dsems:
                st.dsem = self.free_dsems.pop()
            else:
                st.dsem = f"d{len(self.dsems)}"
                self.dsems[st.dsem] = self.nc.alloc_semaphore(st.dsem)
                self.dma_keys[st.dsem] = 0
            prev = self.dma_keys[st.dsem]
            if prev > 0:
                self._waits(q, {st.dsem: prev})
        deps = self._collect(reads, writes, q, skip=(st.dsem if join else None))
        self._waits(q, deps)
        ins_ = self.eng[q].dma_start(out=_ap(out), in_=_ap(in_))
        self.dma_keys[st.dsem] += 16
        cnt = self.dma_keys[st.dsem]
        ins_.then_inc(self.dsems[st.dsem], 16)
        for t in writes:
            if join and st.dsem in t.w:
                t.w[st.dsem] = cnt
            else:
                t.w = {st.dsem: cnt}
                t.r = {}
        for t in reads:
            t.r[st.dsem] = cnt

    def final_wait(self, eng="sp"):
        if self.dry:
            return
        for k, c in self.dma_keys.items():
            if c > 0:
                self.eng[eng].wait_ge(self.dsems[k], c)

    def _e(self, eng):
        return self.eng[eng]

    def tt(self, eng, out, a, b, op):
        self.op(eng, lambda: self._e(eng).tensor_tensor(out.ap, a.ap, b.ap, op), [a, b], [out])

    def ts(self, eng, out, a, s1, s2, op0, op1=None):
        if op1 is None:
            self.op(eng, lambda: self._e(eng).tensor_scalar(out.ap, a.ap, _ap(s1), None, op0), [a, s1], [out])
        else:
            self.op(eng, lambda: self._e(eng).tensor_scalar(out.ap, a.ap, _ap(s1), _ap(s2), op0, op1),
                    [a, s1, s2], [out])

    def stt(self, out, a, s, b, op0, op1):
        self.op("dve", lambda: self.nc.vector.scalar_tensor_tensor(out.ap, a.ap, _ap(s), b.ap, op0, op1),
                [a, s, b], [out])

    def scan(self, out, d0, d1, init, op0, op1):
        self.op("dve", lambda: self.nc.vector.tensor_tensor_scan(out.ap, d0.ap, d1.ap, _ap(init), op0, op1),
                [d0, d1, init], [out])

    def copy(self, eng, out, a):
        if eng == "act":
            self.op(eng, lambda: self.nc.scalar.copy(out.ap, a.ap), [a], [out])
        else:
            self.op(eng, lambda: self._e(eng).tensor_copy(out.ap, a.ap), [a], [out])

    def act(self, out, a, func, bias=0.0, scale=1.0):
        self.op("act", lambda: self.nc.scalar.activation(out.ap, a.ap, func, bias=_ap(bias), scale=_ap(scale)),
                [a, bias, scale], [out])

    def recip(self, out, a):
        self.op("dve", lambda: self.nc.vector.reciprocal(out.ap, a.ap), [a], [out])

    def memset(self, eng, out, val):
        self.op(eng, lambda: self._e(eng).memset(out.ap, val), [], [out])

    def mm(self, out, lhsT, rhs, start=True, stop=True):
        self.op("pe", lambda: self.nc.tensor.matmul(out.ap, lhsT.ap, rhs.ap, start=start, stop=stop),
                [lhsT, rhs], [out])

    def tr(self, out, a, ident):
        self.op("pe", lambda: self.nc.tensor.transpose(out.ap, a.ap, ident.ap), [a, ident], [out])


def _panels(W, pw):
    K, N = W.shape
    assert K % 128 == 0 and N % pw == 0
    return np.ascontiguousarray(W.reshape(K // 128, 128, N // pw, pw).transpose(2, 1, 0, 3))


def _fm(vec):
    v = np.asarray(vec, np.float32).reshape(-1)
    return np.ascontiguousarray(v.reshape(-1, 128).T)


def make_consts(cfg):
    cols = {}
    parts = []

    def add(name, arr):
        arr = np.asarray(arr, np.float32)
        assert arr.shape[0] == 128
        cols[name] = (sum(p.shape[1] for p in parts), arr.shape[1])
        parts.append(arr)

    I = np.arange(128)
    add("ident", np.eye(128))
    add("ones", np.ones((128, 128)))
    add("blk", (I[:, None] // 64 == I[None, :] // 64).astype(np.float32))
    add("hm", np.stack([(I < 64), (I >= 64)], 1).astype(np.float32))
    for kind, ns, L in (("p", 1, 128), ("s", cfg.NS, cfg.LS)):
        seg = I // L
        same = seg[:, None] == seg[None, :]
        le = I[:, None] <= I[None, :]
        lt = I[:, None] < I[None, :]
        add(kind + "_maskb", np.where(same & le, 0.0, NEG))
        strict = (same & lt).astype(np.float32)
        incl = (same & le).astype(np.float32)
        add(kind + "_m2", np.concatenate([strict, incl], 1))
        add(kind + "_strictT", strict.T.copy())
        start = (I % L == 0)
        add(kind + "_rmask", np.tile(np.where(start, 0.0, 1.0)[None, :], (128, 1)))
        add(kind + "_rbias", np.tile(np.where(start, -1e30, 0.0)[None, :], (128, 1)))
        rowmask = (seg[:, None] == np.arange(ns)[None, :]).astype(np.float32)
        add(kind + "_rowmask", rowmask)
    seg = I // cfg.LS
    rowmask = (seg[:, None] == np.arange(cfg.NS)[None, :]).astype(np.float32)
    colmask = np.tile(rowmask.T.reshape(1, cfg.NS * 128), (128, 1)).astype(np.float32)
    return np.concatenate(parts, 1), cols, colmask


def vec_cols(cfg):
    NC, MC, RGB = cfg.NC, cfg.MC, cfg.RGB
    names = []
    for l in range(2):
        names += [(f"ln1g{l}", NC), (f"ln1b{l}", NC), (f"ln2g{l}", NC), (f"ln2b{l}", NC)]
    names += [("mng", MC)] + [(f"cw{j}", RGB) for j in range(4)] + [("cb", RGB), ("ba", RGB), ("bx", RGB), ("lam", RGB)]
    names += [(f"mu{j}", NC) for j in range(6)]
    names += [(n, NC) for n in ("w0", "a0", "kk", "ka", "rk", "lxg", "lxb")]
    cols, o = {}, 0
    for n, c in names:
        cols[n] = (o, c)
        o += c
    return cols, o


def make_vecs(cfg, p):
    cols, n = vec_cols(cfg)
    out = np.zeros((128, n), np.float32)

    def put(name, vec):
        o, c = cols[name]
        a = _fm(vec)
        assert a.shape == (128, c), (name, a.shape, c)
        out[:, o:o + c] = a

    for l in range(2):
        put(f"ln1g{l}", p["ln1_g"][l]); put(f"ln1b{l}", p["ln1_b"][l])
        put(f"ln2g{l}", p["ln2_g"][l]); put(f"ln2b{l}", p["ln2_b"][l])
    put("mng", p["mlstm_norm_g"][0])
    for j in range(4):
        put(f"cw{j}", p["rg_conv_w"][0, j])
    put("cb", p["rg_conv_b"][0]); put("ba", p["rg_ba"][0]); put("bx", p["rg_bx"][0]); put("lam", p["rg_lambda"][0])
    for j in range(6):
        put(f"mu{j}", p["rw_mu"][0, j])
    put("w0", p["rw_w0"][0]); put("a0", p["rw_a0"][0]); put("kk", p["rw_kk"][0]); put("ka", p["rw_ka"][0])
    put("rk", np.asarray(p["rw_rk"][0]).reshape(-1)); put("lxg", p["rw_lnx_g"][0]); put("lxb", p["rw_lnx_b"][0])
    return out


def weight_specs(cfg):
    D, NC, MIXA, RGB, DFF, FC = cfg.D, cfg.NC, cfg.MIXA, cfg.RGB, cfg.DFF, cfg.FC
    PW = 256
    s = {}
    for n in ("wq", "wk", "wv", "wo", "wxr", "wgr"):
        s[n] = [MIXA // PW, 128, NC, PW]
    s["wig"] = [1, 128, NC, cfg.NHA]
    s["wfg"] = [1, 128, NC, cfg.NHA]
    s["rgax"] = [1, 128, RGB, 256]
    s["wout"] = [D // PW, 128, NC, PW]
    KH = min(FC, 32)
    for l in range(2):
        s[f"wup{l}"] = [DFF // PW, 128, NC, PW]
        s[f"wdn{l}"] = [NC * (FC // KH), 128, KH, 128]
    for n in ("rwr", "rwk", "rwv", "rwo"):
        s[n] = [D // PW, 128, NC, PW]
    s["w1"] = [1, 128, NC, cfg.LW]
    s["a1"] = [1, 128, NC, cfg.LA]
    s["g1"] = [1, 128, NC, cfg.LG]
    s["w2"] = [1, cfg.LW, 1, D]
    s["a2"] = [1, cfg.LA, 1, D]
    s["g2"] = [1, 128, cfg.LG // 128, D]
    return s


def host_weights(cfg, p):
    MIXA, RGW, NHA, FC, NC = cfg.MIXA, cfg.RGW, cfg.NHA, cfg.FC, cfg.NC
    PW = 256
    W = np.asarray(p["w_in_ab"][0], np.float32)
    o = 0
    out = {}
    for n in ("wq", "wk", "wv", "wo"):
        out[n] = _panels(W[:, o:o + MIXA], PW)
        o += MIXA
    out["wig"] = _panels(W[:, o:o + NHA], NHA)
    o += NHA
    out["wfg"] = _panels(W[:, o:o + NHA], NHA)
    o += NHA
    out["wxr"] = _panels(W[:, o:o + RGW], PW)
    o += RGW
    out["wgr"] = _panels(W[:, o:o + RGW], PW)
    o += RGW
    out["rgax"] = np.ascontiguousarray(np.concatenate([np.asarray(p["rg_wa"][0], np.float32).transpose(1, 0, 2),
                                                       np.asarray(p["rg_wx"][0], np.float32).transpose(1, 0, 2)], -1))[None]
    out["wout"] = _panels(np.asarray(p["w_out_ab"][0], np.float32), PW)
    KH = min(FC, 32)
    for l in range(2):
        out[f"wup{l}"] = _panels(np.asarray(p["w_up"][l], np.float32), PW)
        wd = _panels(np.asarray(p["w_down"][l], np.float32), 128)
        wd = wd.reshape(NC, 128, FC // KH, KH, 128).transpose(0, 2, 1, 3, 4)
        out[f"wdn{l}"] = np.ascontiguousarray(wd.reshape(NC * (FC // KH), 128, KH, 128))
    for n, k in (("rwr", "rw_wr"), ("rwk", "rw_wk"), ("rwv", "rw_wv"), ("rwo", "rw_wo")):
        out[n] = _panels(np.asarray(p[k][0], np.float32), PW)
    out["w1"] = _panels(np.asarray(p["rw_w1"][0], np.float32), cfg.LW)
    out["a1"] = _panels(np.asarray(p["rw_a1"][0], np.float32), cfg.LA)
    out["g1"] = _panels(np.asarray(p["rw_g1"][0], np.float32), cfg.LG)
    out["w2"] = np.ascontiguousarray(np.asarray(p["rw_w2"][0], np.float32))[None, :, None, :]
    out["a2"] = np.ascontiguousarray(np.asarray(p["rw_a2"][0], np.float32))[None, :, None, :]
    out["g2"] = _panels(np.asarray(p["rw_g2"][0], np.float32), cfg.D)
    specs = weight_specs(cfg)
    for n in out:
        assert list(out[n].shape) == specs[n], (n, out[n].shape, specs[n])
    return out


def io_specs(cfg):
    D, NS, NHA, HDA, RGW, NC = cfg.D, cfg.NS, cfg.NHA, cfg.HDA, cfg.RGW, cfg.NC
    ins = {
        "xp": [D, cfg.TP], "xs": [D, cfg.TS],
        "cext": [NS, NHA, HDA, HDA + 1], "m0": [NHA, NS],
        "rgh": [RGW, NS], "rgc": [RGW, NS, 3],
        "rwS": [NC, 128, NS, 64], "rwx": [D, NS],
        "bif": [NHA, 2], "colm": [128, NS * 128],
    }
    outs = {
        "yp": [D, cfg.TP], "ys": [D, cfg.TS],
        "pc": [1, NHA, HDA, HDA + 1], "pm": [NHA, 1], "prgh": [RGW, 1], "prgc": [RGW, 1, 3],
        "prwS": [NC, 128, 1, 64], "prwx": [D, 1],
        "sc": [NS, NHA, HDA, HDA + 1], "sm": [NHA, NS], "srgh": [RGW, NS], "srgc": [RGW, NS, 3],
        "srwS": [NC, 128, NS, 64], "srwx": [D, NS],
    }
    return ins, outs


class Tile:
    def __init__(self, kind, tok0, L, c0, cfg):
        self.kind, self.tok0, self.L, self.c0 = kind, tok0, L, c0
        if kind == "p":
            self.ns, self.ls = 1, L
        else:
            self.ns, self.ls = cfg.NS, cfg.LS
        self.nsteps = int(np.ceil(np.log2(self.ls)))


class Reg:
    pass


class StopBuild(Exception):
    pass


STOP = None


class Blk:
    pass


class Prog:
    WELEMS = 4096

    def __init__(self, nc, cfg, plan=None, debug=()):
        self.nc, self.cfg = nc, cfg
        self.dry = plan is None
        self.fw = Fw(nc, dry=self.dry)
        self.plan = plan if plan is not None else []
        self.wk = 0
        self.wissued = 0
        self.debug = debug
        self.dbg_out = {}
        self.wspec = weight_specs(cfg)

    def declare(self):
        nc, cfg = self.nc, self.cfg
        ins, outs = io_specs(cfg)
        self.din, self.dout = {}, {}
        for n, s in ins.items():
            self.din[n] = nc.dram_tensor(n, s, F32, kind="ExternalInput").ap()
        cst, self.ccols, _ = make_consts(cfg)
        self.ncst = cst.shape[1]
        self.din["cst"] = nc.dram_tensor("cst", [128, self.ncst], F32, kind="ExternalInput").ap()
        self.vcols, self.nvec = vec_cols(cfg)
        self.din["vec"] = nc.dram_tensor("vec", [128, self.nvec], F32, kind="ExternalInput").ap()
        self.dw = {}
        for n, s in self.wspec.items():
            self.dw[n] = nc.dram_tensor(n, s, F32, kind="ExternalInput").ap()
        for n, s in outs.items():
            self.dout[n] = nc.dram_tensor(n, s, F32, kind="ExternalOutput").ap()

    def wpanel(self, name, pan):
        fw = self.fw
        shp = self.wspec[name]
        parts, KC, pw = shp[1], shp[2], shp[3]
        assert KC * pw <= self.WELEMS, (name, KC, pw)
        k = self.wk
        self.wk += 1
        if self.dry:
            self.plan.append((name, pan))
        else:
            assert self.plan[k] == (name, pan), (k, self.plan[k], name, pan)
            while self.wissued < min(len(self.plan), k + 4):
                j = self.wissued
                nm, pn = self.plan[j]
                s2 = self.wspec[nm]
                buf = self.wbufs[j % 4]
                dst = buf[0:s2[1], 0:s2[2] * s2[3]]
                fw.dma("pool", dst, self.dw[nm][pn].rearrange("p k w -> p (k w)"))
                self.wissued += 1
        buf = self.wbufs[k % 4]
        return buf[0:parts, 0:KC * pw].re("p (k w) -> p k w", k=KC)

    def psum(self):
        b = self.bank[self.psi % 4]
        self.psi += 1
        return b

    def cc(self, name, rows=128, c0=0, n=None):
        o, w = self.ccols[name]
        n = w - c0 if n is None else n
        return self.CST[0:rows, o + c0:o + c0 + n]

    def vc(self, name, j=0, rows=128):
        o, w = self.vcols[name]
        return self.VEC[0:rows, o + j:o + j + 1]

    def dump(self, name, v, shape):
        if name not in self.debug:
            return
        fw = self.fw
        key = name
        i = 0
        while key in self.dbg_out:
            i += 1
            key = f"{name}_{i}"
        self.dbg_out[key] = self.nc.dram_tensor("dbg_" + key, list(shape), F32, kind="ExternalOutput").ap()
        tmp = fw.sb(list(shape), F32, "dbg")
        fw.copy("dve", tmp.v, v)
        fw.dma("sp", self.dbg_out[key], tmp.v)

    def build(self):
        cfg, fw, nc = self.cfg, self.fw, self.nc
        self.declare()
        NC = cfg.NC
        self.CST = fw.sb([128, self.ncst], F32, "cst")
        self.VEC = fw.sb([128, self.nvec], F32, "vec")
        fw.dma("sp", self.CST.v, self.din["cst"])
        fw.dma("sp", self.VEC.v, self.din["vec"])
        self.wbufs = [fw.sb([128, self.WELEMS], BF16, f"wbuf{i}") for i in range(4)]
        psA = fw.ps([128, 4, 512], F32, "psA")
        psD = fw.ps([128, 4, 512], F32, "psD")
        self.bank = [T(psA[:, i, :], f"bankA{i}") for i in range(4)] + [T(psD[:, i, :], f"bankD{i}") for i in range(4)]
        self.psD = V(self.bank[4], psD, ts=self.bank[4:8])
        self.psi = 0
        self.identb = fw.sb([128, 128], BF16, "identb")
        fw.copy("dve", self.identb.v, self.cc("ident"))
        self.onesb = fw.sb([128, 128], BF16, "onesb")
        fw.copy("dve", self.onesb.v, self.cc("ones"))
        self.colmb = fw.sb([128, cfg.NS * 128], BF16, "colmb")
        fw.dma("pool", self.colmb.v, self.din["colm"])
        try:
            self.derived_vecs()
            self.init_states()
            self.chk("init")
            for bi, blk in enumerate(cfg.blocks):
                self.run_block(bi, blk)
        except StopBuild:
            pass
        self.write_prompt_states()
        fw.final_wait("sp")

    def chk(self, name):
        if not hasattr(self, "phase_log"):
            self.phase_log = []
        self.phase_log.append((name, self.fw.cnt["pe"]))
        if STOP == name:
            raise StopBuild()

    def derived_vecs(self):
        cfg, fw = self.cfg, self.fw
        RGB, NC, NHA = cfg.RGB, cfg.NC, cfg.NHA
        self.DV = fw.sb([128, 2 * RGB + 7 * NC], F32, "dv")
        o, _ = self.vcols["lam"]
        lam = self.VEC[:, o:o + RGB]
        t = self.DV[:, 0:RGB]
        fw.act(t, lam, AF.Exp, scale=-1.0)
        fw.act(t, t, AF.Ln, bias=1.0)
        fw.ts("dve", self.DV[:, RGB:2 * RGB], t, -16.0, None, ALU.mult)
        fw.ts("dve", t, t, -8.0, None, ALU.mult)
        b = 2 * RGB
        for j in range(6):
            o, _ = self.vcols[f"mu{j}"]
            fw.ts("dve", self.DV[:, b + j * NC:b + (j + 1) * NC], self.VEC[:, o:o + NC], -1.0, 1.0, ALU.mult, ALU.add)
        b2 = b + 6 * NC
        o, _ = self.vcols["ka"]
        fw.ts("dve", self.DV[:, b2:b2 + NC], self.VEC[:, o:o + NC], -1.0, 1.0, ALU.mult, ALU.add)
        self.dv_c1 = lambda n: self.DV[:, n:n + 1]
        self.dv_c2 = lambda n: self.DV[:, RGB + n:RGB + n + 1]
        self.dv_1mmu = lambda j, c: self.DV[:, b + j * NC + c:b + j * NC + c + 1]
        self.dv_1mka = lambda c: self.DV[:, b2 + c:b2 + c + 1]
        self.EPS = fw.sb([128, 2], F32, "eps")
        fw.memset("dve", self.EPS[:, 0:1], LN_EPS)
        fw.memset("dve", self.EPS[:, 1:2], GN_EPS)
        self.epsc = lambda k: self.EPS[:, 0:1] if k == "ln" else self.EPS[:, 1:2]
        self.BIF = fw.sb([NHA, 3], F32, "bif")
        fw.dma("sp", self.BIF[:, 0:2], self.din["bif"])
        fw.ts("dve", self.BIF[:, 2:3], self.BIF[:, 1:2], -1.0, None, ALU.mult)
        self.SEL = fw.sb([NHA, NHA, 128], F32, "sel")
        for h in range(NHA):
            fw.copy("dve", self.SEL[:, h, :], self.cc("ident", rows=NHA, c0=h, n=1).bc([NHA, 128]))

    def init_states(self):
        cfg, fw = self.cfg, self.fw
        NHA, HKC, HDA, RGB, NC = cfg.NHA, cfg.HKC, cfg.HDA, cfg.RGB, cfg.NC
        self.pC = fw.sb([128, NHA, HKC, HDA + 1], F32, "pC")
        fw.memset("dve", self.pC.v, 0.0)
        self.pM = fw.sb([NHA, 1], F32, "pM")
        fw.memset("dve", self.pM.v, 0.0)
        self.sM = fw.sb([NHA, cfg.NS], F32, "sM")
        fw.dma("sp", self.sM.v, self.din["m0"])
        self.pRH = fw.sb([128, RGB, 1], F32, "pRH")
        fw.memset("dve", self.pRH.v, 0.0)
        self.pRC = fw.sb([128, RGB, 1, 3], F32, "pRC")
        fw.memset("dve", self.pRC.v, 0.0)
        self.pH = fw.sb([128, NC, 1, 64], F32, "pH")
        fw.memset("dve", self.pH.v, 0.0)
        self.pSH = fw.sb([128, NC, 1], F32, "pSH")
        fw.memset("dve", self.pSH.v, 0.0)

    def write_prompt_states(self):
        fw, do = self.fw, self.dout
        fw.dma("sp", do["pc"][0].rearrange("h (kc p) v -> p h kc v", p=128), self.pC.v)
        fw.dma("sp", do["pm"], self.pM.v)
        fw.dma("sp", do["prgh"].rearrange("(n p) o -> p n o", p=128), self.pRH.v)
        fw.dma("sp", do["prgc"].rearrange("(n p) o j -> p n o j", p=128), self.pRC.v)
        fw.dma("sp", do["prwS"].rearrange("m p o v -> p m o v"), self.pH.v)
        fw.dma("sp", do["prwx"].rearrange("(c p) o -> p c o", p=128), self.pSH.v)

    def run_block(self, bi, blk):
        cfg, fw = self.cfg, self.fw
        NC = cfg.NC
        B = Blk()
        B.tiles = []
        c = 0
        for (kind, tok0, L) in blk:
            B.tiles.append(Tile(kind, tok0, L, c, cfg))
            c += L
        B.Tb = c
        B.regs = []
        e = 0
        pt = [t for t in B.tiles if t.kind == "p"]
        if pt:
            r = Reg()
            r.kind, r.c0, r.ns, r.L, r.tok0 = "p", pt[0].c0, 1, sum(t.L for t in pt), pt[0].tok0
            r.e0 = e
            e += 1 + r.L
            B.regs.append(r)
        st = [t for t in B.tiles if t.kind == "s"]
        if st:
            r = Reg()
            r.kind, r.c0, r.ns, r.L, r.tok0 = "s", st[0].c0, cfg.NS, cfg.LS, 0
            r.e0 = e
            e += r.ns * (1 + r.L)
            B.regs.append(r)
        B.EXT = e
        for r in B.regs:
            r.n = r.ns * r.L
        with fw.scope():
            B.hres = fw.sb([128, NC, B.EXT], F32, "hres")
            B.abuf = fw.sb([128, NC, B.Tb], BF16, "abuf")
            self.B = B
            for r in B.regs:
                if r.kind == "p":
                    src = self.din["xp"][:, r.tok0:r.tok0 + r.L].rearrange("(c p) t -> p c t", p=128)
                    fw.dma("sp", B.hres[:, :, r.e0 + 1:r.e0 + 1 + r.L], src)
                else:
                    for kc in range(NC):
                        src = self.din["xs"][kc * 128:(kc + 1) * 128, :].rearrange("p (s t) -> p s t", s=r.ns)
                        fw.dma("sp", self.hv(kc, r), src, join=True)
                        src2 = self.din["rwx"][kc * 128:(kc + 1) * 128, :]
                        fw.dma("sp", self.hv(kc, r, -1)[:, :, 0], src2, join=True)
            self.chk("loadx")
            self.to_bf16(B)
            self.chk("bf16")
            self.layer0_mixer(B)
            self.chk("wout")
            self.layernorm(B, "ln1g0", "ln1b0", bf=True)
            self.chk("ln1")
            self.mlp(B, 0)
            self.chk("mlp0")
            self.layernorm(B, "ln2g0", "ln2b0", bf=False)
            self.chk("ln2")
            for r in B.regs:
                if r.kind == "p":
                    fw.copy("dve", B.hres[:, :, r.e0:r.e0 + 1], self.pSH.v)
                    fw.copy("dve", self.pSH.v, B.hres[:, :, r.e0 + r.L:r.e0 + r.L + 1])
                else:
                    for kc in range(NC):
                        dst = self.dout["srwx"][kc * 128:(kc + 1) * 128, :]
                        fw.dma("sp", dst, self.hv(kc, r)[:, :, r.L - 1], join=True)
            self.layer1_mixer(B)
            self.chk("l1mix")
            self.layernorm(B, "ln1g1", "ln1b1", bf=True)
            self.mlp(B, 1)
            self.layernorm(B, "ln2g1", "ln2b1", bf=False)
            for r in B.regs:
                if r.kind == "p":
                    dst = self.dout["yp"][:, r.tok0:r.tok0 + r.L].rearrange("(c p) t -> p c t", p=128)
                    fw.dma("sp", dst, B.hres[:, :, r.e0 + 1:r.e0 + 1 + r.L])
                else:
                    for kc in range(NC):
                        dst = self.dout["ys"][kc * 128:(kc + 1) * 128, :].rearrange("p (s t) -> p s t", s=r.ns)
                        fw.dma("sp", dst, self.hv(kc, r), join=True)

    def hv(self, kc, r, shift=0):
        B = self.B
        v = B.hres[:, kc, r.e0:r.e0 + r.ns * (1 + r.L)].re("p (s l) -> p s l", s=r.ns)
        return v[:, :, 1 + shift:1 + shift + r.L]

    def pv(self, v2d, r):
        return v2d[:, r.c0:r.c0 + r.n].re("p (s l) -> p s l", s=r.ns)

    def to_bf16(self, B):
        fw = self.fw
        for kc in range(self.cfg.NC):
            for r in B.regs:
                fw.copy("act" if kc % 2 else "dve", self.pv(B.abuf[:, kc, :], r), self.hv(kc, r))

    def proj_fm(self, wname, rhs_fn, N, evac):
        fw = self.fw
        npan, parts, KC, pw = self.wspec[wname]
        for pan in range(npan):
            W = self.wpanel(wname, pan)
            for mi in range(pw // 128):
                ps = self.psum()
                for kc in range(KC):
                    fw.mm(ps[:, 0:N], W[:, kc, mi * 128:(mi + 1) * 128], rhs_fn(kc), start=(kc == 0), stop=(kc == KC - 1))
                evac(pan * (pw // 128) + mi, ps)
            self.bg_step()

    def bg_step(self):
        g = getattr(self, "_bg", None)
        if g is not None:
            try:
                next(g)
            except StopIteration:
                self._bg = None

    def proj_resid(self, B, wname, N0=0, N=None):
        fw, cfg = self.fw, self.cfg

        def evac(m, ps):
            for r in B.regs:
                hv = self.hv(m, r)
                fw.stt(hv, hv, cfg.ALPHA, self.pv(ps[:, 0:B.Tb], r), ALU.mult, ALU.add)
        MC = cfg.MC
        if wname == "wout":
            yr = self.M.yr
            self.proj_fm(wname, lambda kc: B.abuf[:, kc, :] if kc < MC else yr[:, kc - MC, :], B.Tb, evac)
        else:
            self.proj_fm(wname, lambda kc: B.abuf[:, kc, :], B.Tb, evac)

    def layernorm(self, B, gname, bname, bf):
        fw, cfg = self.fw, self.cfg
        NC, Tb, D = cfg.NC, B.Tb, cfg.D
        onesf = self.cc("ones")
        with fw.scope():
            mu = fw.sb([128, Tb], F32, "lnmu")
            sq = [fw.sb([128, Tb], F32, f"lnsq{i}") for i in range(2)]
            ps1 = self.psum()
            for r in B.regs:
                for kc in range(NC):
                    fw.mm(self.pv(ps1[:, 0:Tb], r), onesf, self.hv(kc, r), start=(kc == 0), stop=(kc == NC - 1))
            fw.ts("dve", mu[:, 0:Tb], ps1[:, 0:Tb], 1.0 / D, None, ALU.mult)
            ps2 = self.psum()
            for r in B.regs:
                for kc in range(NC):
                    hv = self.hv(kc, r)
                    fw.tt("dve", hv, hv, self.pv(mu.v, r), ALU.subtract)
                    s = sq[kc % 2]
                    fw.act(self.pv(s.v, r), hv, AF.Square)
                    fw.mm(self.pv(ps2[:, 0:Tb], r), onesf, self.pv(s.v, r), start=(kc == 0), stop=(kc == NC - 1))
            rs = mu
            fw.act(rs[:, 0:Tb], ps2[:, 0:Tb], AF.Ln, bias=self.epsc("ln"), scale=1.0 / D)
            fw.act(rs[:, 0:Tb], rs[:, 0:Tb], AF.Exp, scale=-0.5)
            for kc in range(NC):
                for r in B.regs:
                    hv = self.hv(kc, r)
                    fw.tt("dve", hv, hv, self.pv(rs.v, r), ALU.mult)
                    if bf:
                        fw.act(self.pv(B.abuf[:, kc, :], r), hv, AF.Identity, bias=self.vc(bname, kc), scale=self.vc(gname, kc))
                    fw.ts("dve", hv, hv, self.vc(gname, kc), self.vc(bname, kc), ALU.mult, ALU.add)

    def mlp(self, B, l):
        fw, cfg = self.fw, self.cfg
        NC, FC, Tb = cfg.NC, cfg.FC, B.Tb
        with fw.scope():
            hid = fw.sb([128, FC, Tb], BF16, "hid")
            tmp = [fw.sb([128, Tb], F32, f"mlptmp{i}") for i in range(2)]

            def evac(f, ps):
                t = tmp[f % 2]
                fw.act(t[:, 0:Tb], ps[:, 0:Tb], AF.Relu)
                fw.tt("dve", hid[:, f, :], t[:, 0:Tb], t[:, 0:Tb], ALU.mult)
            self.proj_fm(f"wup{l}", lambda kc: B.abuf[:, kc, :], Tb, evac)
            npan, parts, KH, pw = self.wspec[f"wdn{l}"]
            nh = FC // KH
            for m in range(NC):
                ps = self.psum()
                for hf in range(nh):
                    W = self.wpanel(f"wdn{l}", m * nh + hf)
                    for kk in range(KH):
                        f = hf * KH + kk
                        fw.mm(ps[:, 0:Tb], W[:, kk, :], hid[:, f, :], start=(f == 0), stop=(f == FC - 1))
                for r in B.regs:
                    hv = self.hv(m, r)
                    fw.stt(hv, hv, cfg.ALPHA, self.pv(ps[:, 0:Tb], r), ALU.mult, ALU.add)

    def layer0_mixer(self, B):
        fw, cfg = self.fw, self.cfg
        NC, MC, NHA, HDA, HKC, RGB, Tb = cfg.NC, cfg.MC, cfg.NHA, cfg.HDA, cfg.HKC, cfg.RGB, B.Tb
        nt = len(B.tiles)
        with fw.scope():
            M = Blk()
            self.M = M
            M.qT = fw.sb([128, MC, Tb], BF16, "qT")
            M.kT = fw.sb([128, MC, Tb], BF16, "kT")
            M.oT = fw.sb([128, MC, Tb], BF16, "oT")
            M.kTok = fw.sb([128, nt, cfg.MIXA], BF16, "kTok")
            M.vTok = fw.sb([128, nt, NHA, HDA + 1], BF16, "vTok")
            M.gr = fw.sb([128, RGB, Tb], BF16, "gr")
            M.XE = sum(r.ns * (3 + r.L) for r in B.regs)
            M.xr = fw.sb([128, RGB, M.XE], F32, "xr")
            M.ig = fw.sb([NHA, Tb], F32, "ig")
            M.lf = fw.sb([NHA, Tb], F32, "lf")
            xo = 0
            for r in B.regs:
                r.x0 = xo
                xo += r.ns * (3 + r.L)
            rhs = lambda kc: B.abuf[:, kc, :]
            sc = float(HDA) ** -0.5
            fw.memset("dve", M.vTok[:, :, :, HDA:HDA + 1], 1.0)
            for r in B.regs:
                if r.kind == "p":
                    fw.copy("dve", self.xrv(None, r, 0, 3), self.pRC.v)
                else:
                    for n in range(RGB):
                        src = self.din["rgc"][n * 128:(n + 1) * 128]
                        fw.dma("sp", self.xrv(n, r, 0, 3), src, join=True)
            def xev(m, ps):
                for r in B.regs:
                    fw.copy("act", self.xrv(m, r, 3, r.L), self.pv(ps[:, 0:Tb], r))
            self.proj_fm("wxr", rhs, Tb, xev)
            with fw.scope():
                g1 = fw.sb([128, Tb], F32, "g1")
                g2 = fw.sb([128, Tb], F32, "g2")

                def gev(m, ps):
                    fw.act(g1[:, 0:Tb], ps[:, 0:Tb], AF.Square)
                    fw.ts("dve", g1[:, 0:Tb], g1[:, 0:Tb], 0.044715, 1.0, ALU.mult, ALU.add)
                    fw.tt("dve", g1[:, 0:Tb], g1[:, 0:Tb], ps[:, 0:Tb], ALU.mult)
                    fw.act(g2[:, 0:Tb], g1[:, 0:Tb], AF.Sigmoid, scale=1.5957691216)
                    fw.tt("dve", M.gr[:, m, :], g2[:, 0:Tb], ps[:, 0:Tb], ALU.mult)
                self.proj_fm("wgr", rhs, Tb, gev)
            M.yr = fw.sb([128, RGB, Tb], BF16, "yr")
            rg = self.rglru(B, M)
            self._bg = rg
            self.proj_fm("wq", rhs, Tb, lambda m, ps: fw.copy("act", M.qT[:, m, :], ps[:, 0:Tb]))
            self.proj_fm_tok("wk", B, lambda m, ps: fw.act(M.kT[:, m, :], ps[:, 0:Tb], AF.Copy, scale=sc),
                             lambda ti, L, col0, w, ps: fw.act(M.kTok[0:L, ti, col0:col0 + w], ps[0:L, 0:w], AF.Copy, scale=sc))
            def vev(ti, L, col0, w, ps):
                c = col0
                while c < col0 + w:
                    h, dv = c // HDA, c % HDA
                    ww = min(HDA - dv, col0 + w - c)
                    fw.copy("act", M.vTok[0:L, ti, h, dv:dv + ww], ps[0:L, c - col0:c - col0 + ww])
                    c += ww
            self.proj_fm_tok("wv", B, None, vev)
            self.proj_fm("wo", rhs, Tb, lambda m, ps: fw.act(M.oT[:, m, :], ps[:, 0:Tb], AF.Sigmoid))
            for nm, dst in (("wig", M.ig), ("wfg", M.lf)):
                W = self.wpanel(nm, 0)
                ps = self.psum()
                for kc in range(NC):
                    fw.mm(ps[0:NHA, 0:Tb], W[:, kc, 0:NHA], rhs(kc), start=(kc == 0), stop=(kc == NC - 1))
                if nm == "wig":
                    fw.act(dst[:, 0:Tb], ps[0:NHA, 0:Tb], AF.Identity, bias=self.BIF[:, 0:1])
                else:
                    fw.act(dst[:, 0:Tb], ps[0:NHA, 0:Tb], AF.Exp, bias=self.BIF[:, 2:3], scale=-1.0)
                    fw.act(dst[:, 0:Tb], dst[:, 0:Tb], AF.Ln, bias=1.0)
                    fw.ts("dve", dst[:, 0:Tb], dst[:, 0:Tb], -1.0, None, ALU.mult)
            self._bg = None
            for _ in rg:
                pass
            self.dump("qT", M.qT.v, [128, MC, Tb])
            self.dump("kT", M.kT.v, [128, MC, Tb])
            self.dump("ig", M.ig.v, [NHA, Tb])
            self.dump("lf", M.lf.v, [NHA, Tb])
            self.chk("inproj")
            for ti, t in enumerate(B.tiles):
                self.mlstm_tile(B, M, ti, t)
                self.chk(f"mlstm{ti}")
            self.chk("mlstm")
            self.chk("rglru")
            self.dump("cat", B.abuf.v, [128, NC, Tb])
            self.proj_resid(B, "wout")

    def xrv(self, n, r, j0, w):
        M = self.M
        if n is None:
            v = M.xr[:, :, r.x0:r.x0 + r.ns * (3 + r.L)].re("p n (s l) -> p n s l", s=r.ns)
            return v[:, :, :, j0:j0 + w]
        v = M.xr[:, n, r.x0:r.x0 + r.ns * (3 + r.L)].re("p (s l) -> p s l", s=r.ns)
        return v[:, :, j0:j0 + w]

    def proj_fm_tok(self, wname, B, evac_fm, evac_tok):
        fw = self.fw
        npan, parts, KC, pw = self.wspec[wname]
        for pan in range(npan):
            W = self.wpanel(wname, pan)
            if evac_fm is not None:
                for mi in range(pw // 128):
                    ps = self.psum()
                    for kc in range(KC):
                        fw.mm(ps[:, 0:B.Tb], W[:, kc, mi * 128:(mi + 1) * 128], B.abuf[:, kc, :], start=(kc == 0), stop=(kc == KC - 1))
                    evac_fm(pan * (pw // 128) + mi, ps)
            for ti, t in enumerate(B.tiles):
                ps = self.psum()
                for kc in range(KC):
                    fw.mm(ps[0:t.L, 0:pw], B.abuf[:, kc, t.c0:t.c0 + t.L], W[:, kc, :], start=(kc == 0), stop=(kc == KC - 1))
                evac_tok(ti, t.L, pan * pw, pw, ps)
            self.bg_step()

    def mlstm_tile(self, B, M, ti, t):
        fw, cfg = self.fw, self.cfg
        NHA, HDA, HKC = cfg.NHA, cfg.HDA, cfg.HKC
        L, c0, ns, ls, kind = t.L, t.c0, t.ns, t.ls, t.kind
        cs = slice(c0, c0 + L)
        onesf = self.cc("ones")
        mprev = self.pM if kind == "p" else self.sM
        with fw.scope():
            G = fw.sb([NHA, 6, 128], F32, "G")
            Fv, gv, Mv, iv, ev, wv = (G[:, i, 0:L] for i in range(6))
            s3 = lambda v: v.re("h (s l) -> h s l", s=ns)
            fw.scan(Fv, self.cc(kind + "_rmask", rows=NHA, n=L), M.lf[:, cs], 0.0, ALU.mult, ALU.add)
            fw.tt("dve", gv, M.ig[:, cs], Fv, ALU.subtract)
            fw.scan(Mv, self.cc(kind + "_rbias", rows=NHA, n=L), gv, -1e30, ALU.add, ALU.max)
            fw.tt("dve", s3(Mv), s3(Mv), mprev[:, 0:ns].un(2).bc([NHA, ns, ls]), ALU.max)
            fw.tt("dve", s3(iv), s3(Mv), mprev[:, 0:ns].un(2).bc([NHA, ns, ls]), ALU.subtract)
            fw.act(iv, iv, AF.Exp, scale=-1.0)
            fw.tt("dve", ev, Fv, Mv, ALU.add)
            mnew = fw.sb([NHA, ns], F32, "mnew")
            fw.copy("dve", mnew.v, s3(ev)[:, :, ls - 1])
            fw.act(ev, ev, AF.Exp, scale=-1.0)
            fw.tt("dve", s3(wv), s3(gv), s3(Mv)[:, :, ls - 1:ls].bc([NHA, ns, ls]), ALU.subtract)
            fw.act(wv, wv, AF.Exp)
            pc = self.psum()
            identf = self.cc("ident", rows=NHA, n=NHA)
            fw.mm(pc[0:L, 0:NHA], gv, identf)
            fw.mm(pc[0:L, NHA:2 * NHA], wv, identf)
            cols = fw.sb([128, 2 * NHA], F32, "cols")
            fw.copy("act", cols[0:L, :], pc[0:L, 0:2 * NHA])
            BCs = fw.sb([128, 3, 128], F32, "BCs")
            DT = fw.sb([128, 128], F32, "DT")
            sTd = fw.sb([128, 128], BF16, "sTd")
            qTs = fw.sb([128, HKC, 128], BF16, "qTs")
            hT = fw.sb([128, HKC, 128], F32, "hT")
            sq = fw.sb([128, HKC, 128], F32, "hsq")
            dn = fw.sb([128, 128], F32, "dn")
            mu = fw.sb([128, 128], F32, "hmu")
            kw = fw.sb([128, HDA], BF16, "kw")
            wm = fw.sb([128, ns], F32, "wm")
            Cb = fw.sb([128, HKC, HDA], BF16, "Cb")
            nbc = fw.sb([128, HKC, 128], BF16, "nbc")
            Cs = [fw.sb([128, HKC, HDA + 1], F32, f"Cs{i}") for i in range(2)] if kind == "s" else None
            for h in range(NHA):
                pb = self.psum()
                fw.mm(pb[:, 0:3 * L].re("p (a l) -> p a l", a=3), self.SEL[:, h, :], G[:, 2:5, 0:L])
                fw.copy("act", BCs[:, :, 0:L], pb[:, 0:3 * L].re("p (a l) -> p a l", a=3))
                fw.stt(DT[0:L, 0:L], BCs[0:L, 0, 0:L], -1.0, self.cc(kind + "_maskb", rows=L, n=L), ALU.mult, ALU.add)
                fw.act(DT[0:L, 0:L], DT[0:L, 0:L], AF.Exp, bias=cols[0:L, h:h + 1])
                p2 = self.psum()
                for kc in range(HKC):
                    fw.mm(p2[0:L, 0:L], M.kT[:, h * HKC + kc, cs], M.qT[:, h * HKC + kc, cs], start=(kc == 0), stop=(kc == HKC - 1))
                fw.tt("dve", sTd[0:L, 0:L], p2[0:L, 0:L], DT[0:L, 0:L], ALU.mult)
                for kc in range(HKC):
                    fw.tt("dve", qTs[:, kc, 0:L], M.qT[:, h * HKC + kc, cs], BCs[:, 1, 0:L], ALU.mult)
                psn = [self.bank[4 + c] for c in range(HKC)]
                psd = self.bank[4 + HKC]
                for c in range(HKC):
                    fw.mm(psn[c][:, 0:L], M.vTok[0:L, ti, h, c * 128:(c + 1) * 128], sTd[0:L, 0:L], start=True, stop=False)
                fw.mm(psd[:, 0:L], self.onesb[0:L, :], sTd[0:L, 0:L], start=True, stop=False)
                fw.ts("dve", wm[0:L, 0:ns], self.cc(kind + "_rowmask", rows=L, n=ns), cols[0:L, NHA + h:NHA + h + 1], None, ALU.mult)
                for s in range(ns):
                    sc = slice(s * ls, (s + 1) * ls)
                    if kind == "s":
                        Cst = Cs[(h * ns + s) % 2]
                        fw.dma("sp", Cst.v, self.din["cext"][s, h].rearrange("(kc p) v -> p kc v", p=128))
                        Cv = Cst.v
                    else:
                        Cv = self.pC[:, h]
                    fw.copy("act", Cb.v, Cv[:, :, 0:HDA])
                    fw.copy("dve", nbc.v, Cv[:, :, HDA:HDA + 1].bc([128, HKC, 128]))
                    last = (s == ns - 1)
                    for c in range(HKC):
                        for kc in range(HKC):
                            fw.mm(psn[c][:, sc], Cb[:, kc, c * 128:(c + 1) * 128], qTs[:, kc, sc], start=False, stop=(last and kc == HKC - 1))
                    for kc in range(HKC):
                        fw.mm(psd[:, sc], nbc[:, kc, :], qTs[:, kc, sc], start=False, stop=(last and kc == HKC - 1))
                    fw.ts("dve", kw[0:L, :], M.kTok[0:L, ti, h * HDA:(h + 1) * HDA], wm[0:L, s:s + 1], None, ALU.mult)
                    dec = BCs[:, 1, (s + 1) * ls - 1:(s + 1) * ls]
                    for kc in range(HKC):
                        pu = self.psum()
                        fw.mm(pu[:, 0:HDA + 1], kw[0:L, kc * 128:(kc + 1) * 128], M.vTok[0:L, ti, h, :])
                        fw.stt(Cv[:, kc, :], Cv[:, kc, :], dec, pu[:, 0:HDA + 1], ALU.mult, ALU.add)
                    if kind == "s":
                        fw.dma("sp", self.dout["sc"][s, h].rearrange("(kc p) v -> p kc v", p=128), Cv)
                fw.act(dn[:, 0:L], psd[:, 0:L], AF.Abs)
                fw.tt("dve", dn[:, 0:L], dn[:, 0:L], BCs[:, 2, 0:L], ALU.max)
                fw.recip(dn[:, 0:L], dn[:, 0:L])
                for c in range(HKC):
                    fw.tt("dve", hT[:, c, 0:L], psn[c][:, 0:L], dn[:, 0:L], ALU.mult)
                p3 = self.psum()
                for c in range(HKC):
                    fw.mm(p3[:, 0:L], onesf, hT[:, c, 0:L], start=(c == 0), stop=(c == HKC - 1))
                fw.ts("dve", mu[:, 0:L], p3[:, 0:L], 1.0 / HDA, None, ALU.mult)
                p4 = self.psum()
                for c in range(HKC):
                    fw.tt("dve", hT[:, c, 0:L], hT[:, c, 0:L], mu[:, 0:L], ALU.subtract)
                    fw.act(sq[:, c, 0:L], hT[:, c, 0:L], AF.Square)
                    fw.mm(p4[:, 0:L], onesf, sq[:, c, 0:L], start=(c == 0), stop=(c == HKC - 1))
                fw.act(mu[:, 0:L], p4[:, 0:L], AF.Ln, bias=self.epsc("ln"), scale=1.0 / HDA)
                fw.act(mu[:, 0:L], mu[:, 0:L], AF.Exp, scale=-0.5)
                for c in range(HKC):
                    m = h * HKC + c
                    fw.tt("dve", hT[:, c, 0:L], hT[:, c, 0:L], mu[:, 0:L], ALU.mult)
                    fw.stt(B.abuf[:, m, cs], hT[:, c, 0:L], self.vc("mng", m), M.oT[:, m, cs], ALU.mult, ALU.mult)
            if kind == "p":
                fw.copy("dve", self.pM.v, mnew.v)
            else:
                fw.dma("sp", self.dout["sm"], mnew.v)

    def rglru(self, B, M):
        fw, cfg = self.fw, self.cfg
        RGB, Tb, MC = cfg.RGB, B.Tb, cfg.MC
        with fw.scope():
            Wring = self.wpanel("rgax", 0)
            Wax = fw.sb([128, RGB, 256], BF16, "Wax")
            fw.copy("dve", Wax.v, Wring)
            Wa = Wax[:, :, 0:128]
            Wx = Wax[:, :, 128:256]
            xc = fw.sb([128, Tb], F32, "xc")
            xcb = fw.sb([128, Tb], BF16, "xcb")
            ra = fw.sb([128, Tb], F32, "ra")
            gi = fw.sb([128, Tb], F32, "gi")
            aa = fw.sb([128, Tb], F32, "aa")
            uu = fw.sb([128, Tb], F32, "uu")
            hr = fw.sb([128, Tb], F32, "hr")
            sRH = None
            for r in B.regs:
                if r.kind == "s":
                    sRH = fw.sb([128, RGB, r.ns], F32, "sRH")
                    fw.dma("sp", sRH.v, self.din["rgh"].rearrange("(n p) s -> p n s", p=128))
                    sRHo = fw.sb([128, RGB, r.ns], F32, "sRHo")
            for n in range(RGB):
                for r in B.regs:
                    xv = self.pv(xc.v, r)
                    fw.ts("dve", xv, self.xrv(n, r, 0, r.L), self.vc("cw0", n), self.vc("cb", n), ALU.mult, ALU.add)
                    for j in range(1, 4):
                        fw.stt(xv, self.xrv(n, r, j, r.L), self.vc(f"cw{j}", n), xv, ALU.mult, ALU.add)
                fw.copy("act", xcb[:, 0:Tb], xc[:, 0:Tb])
                pa = self.psum()
                fw.mm(pa[:, 0:Tb], Wa[:, n, :], xcb[:, 0:Tb])
                fw.act(ra[:, 0:Tb], pa[:, 0:Tb], AF.Sigmoid, bias=self.vc("ba", n))
                px = self.psum()
                fw.mm(px[:, 0:Tb], Wx[:, n, :], xcb[:, 0:Tb])
                fw.act(gi[:, 0:Tb], px[:, 0:Tb], AF.Sigmoid, bias=self.vc("bx", n))
                fw.act(aa[:, 0:Tb], ra[:, 0:Tb], AF.Exp, scale=self.dv_c1(n))
                fw.act(uu[:, 0:Tb], ra[:, 0:Tb], AF.Exp, scale=self.dv_c2(n))
                fw.ts("dve", uu[:, 0:Tb], uu[:, 0:Tb], -1.0, 1.0, ALU.mult, ALU.add)
                fw.ts("dve", uu[:, 0:Tb], uu[:, 0:Tb], 1e-30, None, ALU.max)
                fw.act(uu[:, 0:Tb], uu[:, 0:Tb], AF.Sqrt)
                fw.tt("dve", gi[:, 0:Tb], gi[:, 0:Tb], xc[:, 0:Tb], ALU.mult)
                fw.tt("dve", uu[:, 0:Tb], uu[:, 0:Tb], gi[:, 0:Tb], ALU.mult)
                for r in B.regs:
                    if r.kind == "p":
                        c = slice(r.c0, r.c0 + r.L)
                        fw.scan(hr[:, c], aa[:, c], uu[:, c], self.pRH[:, n, :], ALU.mult, ALU.add)
                        fw.copy("dve", self.pRH[:, n, :], hr[:, r.c0 + r.L - 1:r.c0 + r.L])
                    else:
                        for s in range(r.ns):
                            c = slice(r.c0 + s * r.L, r.c0 + (s + 1) * r.L)
                            fw.scan(hr[:, c], aa[:, c], uu[:, c], sRH[:, n, s:s + 1], ALU.mult, ALU.add)
                        fw.copy("dve", sRHo[:, n, :], self.pv(hr.v, r)[:, :, r.L - 1])
                fw.tt("dve", M.yr[:, n, :], hr[:, 0:Tb], M.gr[:, n, :], ALU.mult)
                yield
                if n == 0:
                    self.dump("xc", xc.v, [128, Tb]); self.dump("ra", ra.v, [128, Tb]); self.dump("aa", aa.v, [128, Tb])
                    self.dump("uu", uu.v, [128, Tb]); self.dump("hr", hr.v, [128, Tb])
            for r in B.regs:
                if r.kind == "p":
                    fw.copy("dve", self.pRC.v, self.xrv(None, r, r.L, 3))
                else:
                    fw.dma("sp", self.dout["srgh"].rearrange("(n p) s -> p n s", p=128), sRHo.v)
                    for n in range(RGB):
                        fw.dma("sp", self.dout["srgc"][n * 128:(n + 1) * 128], self.xrv(n, r, r.L, 3), join=True)

    def layer1_mixer(self, B):
        fw, cfg = self.fw, self.cfg
        subs, cur, n = [], [], 0
        for t in B.tiles:
            if cur and (n + t.L > 256 or t.kind == "s" or cur[-1].kind == "s"):
                subs.append(cur)
                cur, n = [], 0
            cur.append(t)
            n += t.L
        subs.append(cur)
        for sub in subs:
            self.rwkv_sub(B, sub)
        self.proj_resid(B, "rwo")

    def rwkv_sub(self, B, sub):
        fw, cfg = self.fw, self.cfg
        NC, LW, LA, LG = cfg.NC, cfg.LW, cfg.LA, cfg.LG
        c0 = sub[0].c0
        n = sum(t.L for t in sub)
        kind = sub[0].kind
        reg = [r for r in B.regs if r.kind == kind][0]
        ns = reg.ns
        if kind == "p":
            off = c0 - reg.c0
            Ls = n
            xv = lambda kc, sh=0: B.hres[:, kc, reg.e0 + 1 + off + sh:reg.e0 + 1 + off + sh + n].re("p (s l) -> p s l", s=1)
        else:
            Ls = reg.L
            xv = lambda kc, sh=0: self.hv(kc, reg, sh)
        s3 = lambda v: v.re("p (s l) -> p s l", s=ns)
        CW = 0.6065306597126334
        with fw.scope():
            S = Blk()
            S.sg = fw.sb([128, NC, n], F32, "sg")
            S.a = fw.sb([128, NC, n], BF16, "a")
            S.k = fw.sb([128, NC, n], BF16, "k")
            S.r = fw.sb([128, NC, n], BF16, "r")
            S.v = fw.sb([128, NC, n], BF16, "v")
            S.g = fw.sb([128, NC, n], BF16, "g")
            with fw.scope():
                xsb = [fw.sb([128, NC, n], BF16, f"xs{i}") for i in range(2)]
                tmps = [fw.sb([128, n], F32, f"xstmp{i}") for i in range(2)]
                lo = fw.sb([128, LG // 128, n], BF16, "lora")

                def mk_xs(j, xs):
                    for kc in range(NC):
                        tmp = tmps[kc % 2]
                        fw.act(s3(tmp[:, 0:n]), xv(kc, -1), AF.Identity, scale=self.vc(f"mu{j}", kc))
                        fw.stt(s3(xs[:, kc, :]), xv(kc), self.dv_1mmu(j, kc), s3(tmp[:, 0:n]), ALU.mult, ALU.add)
                        if kc % 2 == 1:
                            yield

                def lora(xs, n1, n2, R, func1, evac2):
                    rhs = lambda kc: xs[:, kc, :]
                    W1 = self.wpanel(n1, 0)
                    for c in range((R + 127) // 128):
                        w = min(128, R - c * 128)
                        ps = self.psum()
                        for kc in range(NC):
                            fw.mm(ps[0:w, 0:n], W1[:, kc, c * 128:c * 128 + w], rhs(kc), start=(kc == 0), stop=(kc == NC - 1))
                        fw.act(lo[0:w, c, :], ps[0:w, 0:n], func1)
                        self.bg_step()
                    W2 = self.wpanel(n2, 0)
                    KC2 = (R + 127) // 128
                    for m in range(NC):
                        ps = self.psum()
                        for c in range(KC2):
                            w = min(128, R - c * 128)
                            fw.mm(ps[:, 0:n], W2[0:w, c, m * 128:(m + 1) * 128], lo[0:w, c, :], start=(c == 0), stop=(c == KC2 - 1))
                        evac2(m, ps)
                        if m % 2 == 1:
                            self.bg_step()
                jobs = [
                    (1, lambda xs: lora(xs, "w1", "w2", LW, AF.Tanh,
                                        lambda m, ps: fw.act(S.sg[:, m, :], ps[:, 0:n], AF.Sigmoid, bias=self.vc("w0", m)))),
                    (4, lambda xs: lora(xs, "a1", "a2", LA, AF.Copy,
                                        lambda m, ps: fw.act(S.a[:, m, :], ps[:, 0:n], AF.Sigmoid, bias=self.vc("a0", m)))),
                    (5, lambda xs: lora(xs, "g1", "g2", LG, AF.Sigmoid,
                                        lambda m, ps: fw.copy("act", S.g[:, m, :], ps[:, 0:n]))),
                ]
                for j_, wn, dst in ((2, "rwk", S.k), (0, "rwr", S.r), (3, "rwv", S.v)):
                    jobs.append((j_, lambda xs, wn=wn, dst=dst: self.proj_fm(
                        wn, lambda kc: xs[:, kc, :], n, lambda m, ps: fw.copy("act", dst[:, m, :], ps[:, 0:n]))))
                for _ in mk_xs(jobs[0][0], xsb[0]):
                    pass
                for idx, (j_, run) in enumerate(jobs):
                    if idx + 1 < len(jobs):
                        self._bg = mk_xs(jobs[idx + 1][0], xsb[(idx + 1) % 2])
                    run(xsb[idx % 2])
                    if self._bg is not None:
                        for _ in self._bg:
                            pass
                        self._bg = None
            self.chk("l1proj")
            self.dump("sg", S.sg.v, [128, NC, n])
            self.dump("rk", S.k.v, [128, NC, n])
            for m0 in range(0, NC, 2):
                self.rwkv_mpair(B, S, sub, m0, c0, n, CW)

    def rwkv_mpair(self, B, S, sub, m0, c0, n, CW):
        fw, cfg = self.fw, self.cfg
        NC = cfg.NC
        blkf = self.cc("blk")
        with fw.scope():
            FT = fw.sb([128, 2, 4, n], BF16, "FT")
            FTp = fw.sb([128, 2, 2, 2, n], BF16, "FTp")
            self.FTp = FTp
            Ep = fw.sb([128, 2, n], F32, "Ep")
            bonus = fw.sb([128, 2, n], F32, "bonus")
            with fw.scope():
                cs = fw.sb([128, n], F32, "cs")
                Em = fw.sb([128, n], F32, "Em")
                Epm = fw.sb([128, n], F32, "Epm")
                kkp = fw.sb([128, n], F32, "kkp")
                sq = fw.sb([128, n], F32, "sq")
                rn = fw.sb([128, n], F32, "rn")
                tt_ = fw.sb([128, n], F32, "tt")
                km = fw.sb([128, n], F32, "km")
                for ml in range(2):
                    m = m0 + ml
                    sg = S.sg[:, m, :]
                    for t in sub:
                        tc = slice(t.c0 - c0, t.c0 - c0 + t.L)
                        fw.scan(cs[:, tc], self.cc(t.kind + "_rmask", n=t.L), sg[:, tc], 0.0, ALU.mult, ALU.add)
                    fw.act(Ep[:, ml, :], cs[:, 0:n], AF.Exp, scale=-CW)
                    fw.act(Em[:, 0:n], cs[:, 0:n], AF.Exp, scale=CW)
                    fw.tt("dve", Epm[:, 0:n], cs[:, 0:n], sg, ALU.subtract)
                    fw.act(Epm[:, 0:n], Epm[:, 0:n], AF.Exp, scale=-CW)
                    fw.act(kkp[:, 0:n], S.k[:, m, :], AF.Identity, scale=self.vc("kk", m))
                    fw.act(sq[:, 0:n], kkp[:, 0:n], AF.Square)
                    ps = self.psum()
                    fw.mm(ps[:, 0:n], blkf, sq[:, 0:n])
                    fw.ts("dve", rn[:, 0:n], ps[:, 0:n], 1e-18, None, ALU.max)
                    fw.act(rn[:, 0:n], rn[:, 0:n], AF.Ln)
                    fw.act(rn[:, 0:n], rn[:, 0:n], AF.Exp, scale=-0.5)
                    fw.tt("dve", kkp[:, 0:n], kkp[:, 0:n], rn[:, 0:n], ALU.mult)
                    fw.act(tt_[:, 0:n], S.a[:, m, :], AF.Identity, bias=self.dv_1mka(m), scale=self.vc("ka", m))
                    fw.tt("dve", km[:, 0:n], tt_[:, 0:n], S.k[:, m, :], ALU.mult)
                    fw.stt(FT[:, ml, 0, :], kkp[:, 0:n], -1.0, Epm[:, 0:n], ALU.mult, ALU.mult)
                    fw.tt("dve", FT[:, ml, 1, :], S.r[:, m, :], Ep[:, ml, :], ALU.mult)
                    fw.tt("dve", tt_[:, 0:n], kkp[:, 0:n], S.a[:, m, :], ALU.mult)
                    fw.tt("dve", FT[:, ml, 2, :], tt_[:, 0:n], Em[:, 0:n], ALU.mult)
                    fw.tt("dve", FT[:, ml, 3, :], km[:, 0:n], Em[:, 0:n], ALU.mult)
                    for h2 in range(2):
                        for i_, src_ in enumerate((2, 3)):
                            fw.act(FTp[:, ml, h2, i_, :], FT[:, ml, src_, :], AF.Identity, scale=self.cc("hm")[:, h2:h2 + 1])
                    fw.tt("dve", tt_[:, 0:n], km[:, 0:n], S.r[:, m, :], ALU.mult)
                    fw.act(sq[:, 0:n], tt_[:, 0:n], AF.Identity, scale=self.vc("rk", m))
                    ps = self.psum()
                    fw.mm(ps[:, 0:n], blkf, sq[:, 0:n])
                    fw.tt("dve", bonus[:, ml, :], ps[:, 0:n], S.v[:, m, :], ALU.mult)
            self.chk("l1prep")
            for t in sub:
                with fw.scope():
                    gens = [self.rwkv_chain(B, S, FT, Ep, bonus, t, m0, ml, c0) for ml in range(2)]
                    live = list(gens)
                    while live:
                        for g in list(live):
                            try:
                                next(g)
                            except StopIteration:
                                live.remove(g)
                self.chk("l1tile")

    def rwkv_tile(self, B, S, FT, Ep, bonus, t, m0, c0):
        fw, cfg = self.fw, self.cfg
        L, ns, ls, kind, nsteps = t.L, t.ns, t.ls, t.kind, t.nsteps
        tc = slice(t.c0 - c0, t.c0 - c0 + L)
        bc = slice(t.c0, t.c0 + L)
        hm = self.cc("hm")
        blkf = self.cc("blk")
        psD = self.psD
        bankbf = lambda b: V(b, b.ap.bitcast(BF16))
        with fw.scope():
            tok4 = fw.sb([128, 2, 4, 128], BF16, "tok4")
            BKp = fw.sb([128, 4, 2, 128], BF16, "BKp")
            AT4 = fw.sb([128, 4, 4, 128], BF16, "AT4")
            PX = [fw.sb([128, 4, 256], BF16, f"PX{i}") for i in range(2)]
            Qb = [fw.sb([128, 4, 128], BF16, f"Qb{i}") for i in range(2)]
            X32 = fw.sb([128, 4, 128], F32, "X32")
            AH = fw.sb([128, 4, 64], BF16, "AH")
            ATp = fw.sb([128, 4, 128], BF16, "ATp")
            Ubf = fw.sb([128, 4, 128], BF16, "UV")
            Hbf = fw.sb([128, 2, ns, 64], BF16, "Hbf")
            Hpad = fw.sb([128, 4, ns, 64], BF16, "Hpad")
            yt = fw.sb([128, 128], F32, "yt")
            ysq = fw.sb([128, 128], F32, "ysq")
            yr = fw.sb([128, 128], F32, "yr")
            if kind == "s":
                H32t = fw.sb([128, 2, ns, 64], F32, "H32s")
                fw.dma("sp", H32t.v, self.din["rwS"][m0:m0 + 2].rearrange("m p s v -> p m s v"))
                H32 = H32t.v
                AM = [fw.sb([128, ns, 128], BF16, f"AM{i}") for i in range(2)]
                UVm = [fw.sb([128, ns, 128], BF16, f"UVm{i}") for i in range(2)]
            else:
                H32 = self.pH[:, m0:m0 + 2]
            fw.copy("act", Hbf.v, H32)
            for hh in range(4):
                fw.act(Hpad[:, hh], H32[:, hh // 2], AF.Identity, scale=hm[:, hh % 2:hh % 2 + 1])
            for ml in range(2):
                pb = bankbf(self.psum())
                for i, src in enumerate((FT[:, ml, 0, tc], FT[:, ml, 2, tc], FT[:, ml, 3, tc], S.v[:, m0 + ml, tc])):
                    fw.tr(pb[0:L, i * 128:(i + 1) * 128], src, self.identb.v)
                fw.copy("act", tok4[0:L, ml], pb[0:L, 0:512].re("p (a k) -> p a k", a=4))
            self.chk("t_tr")
            for hh in range(4):
                ml, h2 = hh // 2, hh % 2
                for i, src in enumerate((2, 3)):
                    fw.act(BKp[:, hh, i, 0:L], FT[:, ml, src, tc], AF.Identity, scale=hm[:, h2:h2 + 1])
            self.chk("t_pad")
            pP = self.psum()
            for hh in range(4):
                ml = hh // 2
                bk = self.bank[4 + hh]
                rhsAR = FT[:, ml, 0:2, tc]
                fw.mm(bk[0:L, 0:2 * L].re("p (a l) -> p a l", a=2), BKp[:, hh, 0, 0:L], rhsAR)
                fw.mm(bk[0:L, 2 * L:4 * L].re("p (a l) -> p a l", a=2), BKp[:, hh, 1, 0:L], rhsAR)
                fw.mm(pP[0:L, hh * 128:hh * 128 + L], FT[:, ml, 0, tc], BKp[:, hh, 0, 0:L])
            m2 = self.cc(kind + "_m2").re("p (a l) -> p a l", a=2)[0:L, :, 0:L]
            for rep in range(2):
                src = psD[0:L, :, rep * 2 * L:(rep + 1) * 2 * L].re("p h (a l) -> p h a l", a=2)
                fw.tt("dve", AT4[0:L, :, 2 * rep:2 * rep + 2, 0:L], src, m2.un(1).bc([L, 4, 2, L]), ALU.mult)
            fw.tt("dve", PX[0][0:L, :, 0:L], pP[0:L, 0:512].re("p (h l) -> p h l", h=4)[:, :, 0:L],
                  self.cc(kind + "_strictT", rows=L, n=L).un(1).bc([L, 4, L]), ALU.mult)
            self.chk("t_A")
            pZ = self.psum()
            for hh in range(4):
                ml, h2 = hh // 2, hh % 2
                fw.mm(pZ[0:L, hh * 64:(hh + 1) * 64], AT4[0:L, hh, 2, 0:L], tok4[0:L, ml, 3, h2 * 64:(h2 + 1) * 64])
            fw.copy("act", X32[0:L, :, 64:128], pZ[0:L, 0:256].re("p (h v) -> p h v", h=4))
            for ml in range(2):
                fw.copy("dve", X32[0:L, 2 * ml:2 * ml + 2, 0:64], tok4[0:L, ml, 0, :].re("p (h k) -> p h k", h=2))
            fw.copy("act", PX[0][0:L, :, L:L + 128], X32[0:L])
            self.chk("t_Z")
            for i in range(nsteps):
                cur, nxt = i % 2, (i + 1) % 2
                last = (i == nsteps - 1)
                for hh in range(4):
                    Qi = AT4[0:L, hh, 0, 0:L] if i == 0 else Qb[cur][0:L, hh, 0:L]
                    bk = self.bank[4 + hh]
                    if not last:
                        fw.mm(bk[0:L, 0:L + 128], Qi, PX[cur][0:L, hh, 0:L + 128])
                        fw.mm(bk[0:L, 256:256 + L], PX[cur][0:L, hh, 0:L], Qi)
                    else:
                        fw.mm(bk[0:L, L:L + 128], Qi, PX[cur][0:L, hh, L:L + 128])
                self.chk(f"d_mm{i}")
                for half in range(2):
                    hs = slice(2 * half, 2 * half + 2)
                    pv_ = V(self.bank[4 + 2 * half], psD.ap[0:L, hs, :], ts=self.bank[4 + 2 * half:6 + 2 * half])
                    if not last:
                        for hh in (2 * half, 2 * half + 1):
                            qsrc = self.bank[4 + hh][0:L, 256:256 + L]
                            fw.op("act", lambda: self.nc.scalar.copy(Qb[nxt][0:L, hh, 0:L].ap, qsrc.ap), [qsrc], [Qb[nxt][0:L, hh, 0:L], qsrc])
                        fw.copy("dve", PX[nxt][0:L, hs, 0:L], pv_[:, :, 0:L])
                        fw.tt("dve", PX[nxt][0:L, hs, L:L + 128], X32[0:L, hs, :], pv_[:, :, L:L + 128], ALU.add)
                    fw.tt("dve", X32[0:L, hs, :], X32[0:L, hs, :], pv_[:, :, L:L + 128], ALU.add)
                self.chk(f"d_end{i}")
            self.chk("t_dbl")
            fw.copy("dve", AH[0:L], X32[0:L, :, 0:64])
            for ml in range(2):
                pb = bankbf(self.psum())
                fw.tr(pb[:, 0:L], AH[0:L, 2 * ml:2 * ml + 2, :].re("p h k -> p (h k)"), self.identb[0:L, 0:L])
                for h2 in range(2):
                    fw.ts("dve", ATp[:, 2 * ml + h2, 0:L], pb[:, 0:L], hm[:, h2:h2 + 1], None, ALU.mult)
            self.chk("t_AT")
            pU = self.psum()
            for hh in range(4):
                ml = hh // 2
                if ns == 1:
                    fw.mm(pU[0:L, hh * 64:(hh + 1) * 64], ATp[:, hh, 0:L], Hbf[:, ml, 0, :])
                else:
                    am = AM[hh % 2]
                    fw.tt("dve", am[:, :, 0:L], ATp[:, hh, 0:L].un(1).bc([128, ns, L]),
                          self.colmb.v.re("p (s t) -> p s t", s=ns)[:, :, 0:L], ALU.mult)
                    for s in range(ns):
                        fw.mm(pU[0:L, hh * 64:(hh + 1) * 64], am[:, s, 0:L], Hbf[:, ml, s, :], start=(s == 0), stop=(s == ns - 1))
            fw.tt("dve", Ubf[0:L, :, 0:64], pU[0:L, 0:256].re("p (h v) -> p h v", h=4), X32[0:L, :, 64:128], ALU.add)
            for ml in range(2):
                fw.copy("act", Ubf[0:L, 2 * ml:2 * ml + 2, 64:128], tok4[0:L, ml, 3, :].re("p (h v) -> p h v", h=2))
            self.chk("t_U")
            pY = [self.psum(), self.psum()]
            for hh in range(4):
                ml, h2 = hh // 2, hh % 2
                out = pY[ml][64 * h2:64 * h2 + 64, 0:L]
                fw.mm(out, Ubf[0:L, hh, 0:64], AT4[0:L, hh, 1, 0:L], start=True, stop=False)
                fw.mm(out, Ubf[0:L, hh, 64:128], AT4[0:L, hh, 3, 0:L], start=False, stop=False)
                for s in range(ns):
                    sc = slice(s * ls, (s + 1) * ls)
                    fw.mm(pY[ml][64 * h2:64 * h2 + 64, sc], Hpad[:, hh, s, :], FT[:, ml, 1, tc][:, sc], start=False, stop=(s == ns - 1))
            self.chk("t_Y")
            for hh in range(4):
                ml, h2 = hh // 2, hh % 2
                if ns > 1:
                    uvm = UVm[hh % 2]
                    fw.tt("dve", uvm[0:L], Ubf[0:L, hh, :].un(1).bc([L, ns, 128]),
                          self.cc(kind + "_rowmask", rows=L, n=ns).un(2).bc([L, ns, 128]), ALU.mult)
                for s in range(ns):
                    slot = ml * ns + s
                    bk = self.bank[4 + slot // 8]
                    out = bk[64 * h2:64 * h2 + 64, (slot % 8) * 64:(slot % 8 + 1) * 64]
                    rU = Ubf[0:L, hh, 0:64] if ns == 1 else uvm[0:L, s, 0:64]
                    rV = Ubf[0:L, hh, 64:128] if ns == 1 else uvm[0:L, s, 64:128]
                    fw.mm(out, tok4[0:L, ml, 1, h2 * 64:(h2 + 1) * 64], rU, start=True, stop=False)
                    fw.mm(out, tok4[0:L, ml, 2, h2 * 64:(h2 + 1) * 64], rV, start=False, stop=True)
            nslot = 2 * ns
            for b in range((nslot + 7) // 8):
                w = min(8, nslot - b * 8)
                hv_ = H32.re("p m s v -> p (m s) v")[:, b * 8:b * 8 + w, :]
                fw.tt("dve", hv_, hv_, self.bank[4 + b][:, 0:w * 64].re("p (a v) -> p a v", a=w), ALU.add)
            for ml in range(2):
                wl = Ep[:, ml, tc].re("p (s l) -> p s l", s=ns)[:, :, ls - 1:ls].bc([128, ns, 64])
                fw.tt("dve", H32[:, ml], H32[:, ml], wl, ALU.mult)
            if kind == "s":
                fw.dma("sp", self.dout["srwS"][m0:m0 + 2].rearrange("m p s v -> p m s v"), H32)
            self.chk("t_H")
            for ml in range(2):
                m = m0 + ml
                fw.copy("act", yt[:, 0:L], pY[ml][:, 0:L])
                p1 = self.psum()
                fw.mm(p1[:, 0:L], blkf, yt[:, 0:L])
                fw.stt(yt[:, 0:L], p1[:, 0:L], -1.0 / 64, yt[:, 0:L], ALU.mult, ALU.add)
                fw.act(ysq[:, 0:L], yt[:, 0:L], AF.Square)
                p2 = self.psum()
                fw.mm(p2[:, 0:L], blkf, ysq[:, 0:L])
                fw.act(yr[:, 0:L], p2[:, 0:L], AF.Sqrt, bias=self.epsc("gn"), scale=1.0 / 64)
                fw.recip(yr[:, 0:L], yr[:, 0:L])
                fw.tt("dve", yt[:, 0:L], yt[:, 0:L], yr[:, 0:L], ALU.mult)
                fw.ts("dve", yt[:, 0:L], yt[:, 0:L], self.vc("lxg", m), self.vc("lxb", m), ALU.mult, ALU.add)
                fw.tt("dve", yt[:, 0:L], yt[:, 0:L], bonus[:, ml, tc], ALU.add)
                fw.tt("dve", B.abuf[:, m, bc], yt[:, 0:L], S.g[:, m, tc], ALU.mult)


    def rwkv_chain(self, B, S, FT, Ep, bonus, t, m0, ml, c0):
        fw, cfg = self.fw, self.cfg
        L, ns, ls, kind, nsteps = t.L, t.ns, t.ls, t.kind, t.nsteps
        tc = slice(t.c0 - c0, t.c0 - c0 + L)
        bc = slice(t.c0, t.c0 + L)
        hm = self.cc("hm")
        blkf = self.cc("blk")
        m = m0 + ml
        D0, D1 = self.bank[4 + 2 * ml], self.bank[5 + 2 * ml]
        Db = (D0, D1)
        psD2 = V(D0, self.psD.ap[:, 2 * ml:2 * ml + 2, :], ts=[D0, D1])
        pool = (self.bank[2 * ml], self.bank[2 * ml + 1])
        pi = [0]

        def nextp():
            pi[0] += 1
            return pool[pi[0] % 2]
        bankbf = lambda b: V(b, b.ap.bitcast(BF16))
        if True:
            tok4 = fw.sb([128, 4, 128], BF16, "tok4")
            BKp = self.FTp[:, ml, :, :, tc]
            AT4 = fw.sb([128, 2, 4, 128], BF16, "AT4")
            PX = [fw.sb([128, 2, 256], BF16, f"PX{i}") for i in range(2)]
            Qb = [fw.sb([128, 2, 128], BF16, f"Qb{i}") for i in range(2)]
            X32 = fw.sb([128, 2, 128], F32, "X32")
            AH = fw.sb([128, 2, 64], BF16, "AH")
            ATp = fw.sb([128, 2, 128], BF16, "ATp")
            Ubf = fw.sb([128, 2, 128], BF16, "UV")
            Hbf = fw.sb([128, ns, 64], BF16, "Hbf")
            Hpad = fw.sb([128, 2, ns, 64], BF16, "Hpad")
            yt = fw.sb([128, 128], F32, "yt")
            ysq = fw.sb([128, 128], F32, "ysq")
            yr = fw.sb([128, 128], F32, "yr")
            if kind == "s":
                H32t = fw.sb([128, ns, 64], F32, "H32s")
                fw.dma("sp", H32t.v, self.din["rwS"][m].rearrange("p s v -> p s v"))
                H32 = H32t.v
                AM = fw.sb([128, ns, 128], BF16, "AM")
                UVm = fw.sb([128, ns, 128], BF16, "UVm")
            else:
                H32 = self.pH[:, m]
            fw.copy("act", Hbf.v, H32)
            for h2 in range(2):
                fw.act(Hpad[:, h2], H32, AF.Identity, scale=hm[:, h2:h2 + 1])
            pb = bankbf(nextp())
            for i, src in enumerate((FT[:, ml, 0, tc], FT[:, ml, 2, tc], FT[:, ml, 3, tc], S.v[:, m, tc])):
                fw.tr(pb[0:L, i * 128:(i + 1) * 128], src, self.identb.v)
            fw.copy("act", tok4[0:L], pb[0:L, 0:512].re("p (a k) -> p a k", a=4))
            yield
            pP = nextp()
            rhsAR = FT[:, ml, 0:2, tc]
            for h2 in range(2):
                bk = Db[h2]
                fw.mm(bk[0:L, 0:2 * L].re("p (a l) -> p a l", a=2), BKp[:, h2, 0, 0:L], rhsAR)
                fw.mm(bk[0:L, 2 * L:4 * L].re("p (a l) -> p a l", a=2), BKp[:, h2, 1, 0:L], rhsAR)
                fw.mm(pP[0:L, h2 * 128:h2 * 128 + L], FT[:, ml, 0, tc], BKp[:, h2, 0, 0:L])
            m2 = self.cc(kind + "_m2").re("p (a l) -> p a l", a=2)[0:L, :, 0:L]
            for rep in range(2):
                src = psD2[0:L, :, rep * 2 * L:(rep + 1) * 2 * L].re("p h (a l) -> p h a l", a=2)
                fw.tt("dve", AT4[0:L, :, 2 * rep:2 * rep + 2, 0:L], src, m2.un(1).bc([L, 2, 2, L]), ALU.mult)
            fw.tt("dve", PX[0][0:L, :, 0:L], pP[0:L, 0:256].re("p (h l) -> p h l", h=2)[:, :, 0:L],
                  self.cc(kind + "_strictT", rows=L, n=L).un(1).bc([L, 2, L]), ALU.mult)
            yield
            pZ = nextp()
            for h2 in range(2):
                fw.mm(pZ[0:L, h2 * 64:(h2 + 1) * 64], AT4[0:L, h2, 2, 0:L], tok4[0:L, 3, h2 * 64:(h2 + 1) * 64])
            fw.copy("act", X32[0:L, :, 64:128], pZ[0:L, 0:128].re("p (h v) -> p h v", h=2))
            fw.copy("act", X32[0:L, :, 0:64], tok4[0:L, 0, :].re("p (h k) -> p h k", h=2))
            fw.copy("act", PX[0][0:L, :, L:L + 128], X32[0:L])
            yield
            for i in range(nsteps):
                cur, nxt = i % 2, (i + 1) % 2
                last = (i == nsteps - 1)
                for h2 in range(2):
                    Qi = AT4[0:L, h2, 0, 0:L] if i == 0 else Qb[cur][0:L, h2, 0:L]
                    bk = Db[h2]
                    if not last:
                        fw.mm(bk[0:L, 0:L + 128], Qi, PX[cur][0:L, h2, 0:L + 128])
                        fw.mm(bk[0:L, 256:256 + L], PX[cur][0:L, h2, 0:L], Qi)
                    else:
                        fw.mm(bk[0:L, L:L + 128], Qi, PX[cur][0:L, h2, L:L + 128])
                if not last:
                    for h2 in range(2):
                        qsrc = Db[h2][0:L, 256:256 + L]
                        qdst = Qb[nxt][0:L, h2, 0:L]
                        fw.op("act", lambda qdst=qdst, qsrc=qsrc: self.nc.scalar.copy(qdst.ap, qsrc.ap), [qsrc], [qdst, qsrc])
                    fw.copy("dve", PX[nxt][0:L, :, 0:L], psD2[0:L, :, 0:L])
                    fw.tt("dve", PX[nxt][0:L, :, L:L + 128], PX[cur][0:L, :, L:L + 128], psD2[0:L, :, L:L + 128], ALU.add)
                else:
                    fw.tt("dve", X32[0:L], PX[cur][0:L, :, L:L + 128], psD2[0:L, :, L:L + 128], ALU.add)
                yield
            fw.copy("act", AH[0:L], X32[0:L, :, 0:64])
            pb = bankbf(nextp())
            fw.tr(pb[:, 0:L], AH[0:L].re("p h k -> p (h k)"), self.identb[0:L, 0:L])
            for h2 in range(2):
                fw.act(ATp[:, h2, 0:L], pb[:, 0:L], AF.Identity, scale=hm[:, h2:h2 + 1])
            yield
            pU = nextp()
            for h2 in range(2):
                if ns == 1:
                    fw.mm(pU[0:L, h2 * 64:(h2 + 1) * 64], ATp[:, h2, 0:L], Hbf[:, 0, :])
                else:
                    fw.tt("dve", AM[:, :, 0:L], ATp[:, h2, 0:L].un(1).bc([128, ns, L]),
                          self.colmb.v.re("p (s t) -> p s t", s=ns)[:, :, 0:L], ALU.mult)
                    for s in range(ns):
                        fw.mm(pU[0:L, h2 * 64:(h2 + 1) * 64], AM[:, s, 0:L], Hbf[:, s, :], start=(s == 0), stop=(s == ns - 1))
            fw.tt("dve", Ubf[0:L, :, 0:64], pU[0:L, 0:128].re("p (h v) -> p h v", h=2), X32[0:L, :, 64:128], ALU.add)
            fw.copy("act", Ubf[0:L, :, 64:128], tok4[0:L, 3, :].re("p (h v) -> p h v", h=2))
            yield
            pY = nextp()
            for h2 in range(2):
                out = pY[64 * h2:64 * h2 + 64, 0:L]
                fw.mm(out, Ubf[0:L, h2, 0:64], AT4[0:L, h2, 1, 0:L], start=True, stop=False)
                fw.mm(out, Ubf[0:L, h2, 64:128], AT4[0:L, h2, 3, 0:L], start=False, stop=False)
                for s in range(ns):
                    sc = slice(s * ls, (s + 1) * ls)
                    fw.mm(pY[64 * h2:64 * h2 + 64, sc], Hpad[:, h2, s, :], FT[:, ml, 1, tc][:, sc], start=False, stop=(s == ns - 1))
            yield
            for h2 in range(2):
                if ns > 1:
                    fw.tt("dve", UVm[0:L], Ubf[0:L, h2, :].un(1).bc([L, ns, 128]),
                          self.cc(kind + "_rowmask", rows=L, n=ns).un(2).bc([L, ns, 128]), ALU.mult)
                for s in range(ns):
                    bk = Db[s // 8]
                    out = bk[64 * h2:64 * h2 + 64, (s % 8) * 64:(s % 8 + 1) * 64]
                    rU = Ubf[0:L, h2, 0:64] if ns == 1 else UVm[0:L, s, 0:64]
                    rV = Ubf[0:L, h2, 64:128] if ns == 1 else UVm[0:L, s, 64:128]
                    fw.mm(out, tok4[0:L, 1, h2 * 64:(h2 + 1) * 64], rU, start=True, stop=False)
                    fw.mm(out, tok4[0:L, 2, h2 * 64:(h2 + 1) * 64], rV, start=False, stop=True)
            for b in range((ns + 7) // 8):
                w = min(8, ns - b * 8)
                hv_ = H32[:, b * 8:b * 8 + w, :]
                fw.tt("dve", hv_, hv_, Db[b][:, 0:w * 64].re("p (a v) -> p a v", a=w), ALU.add)
            wl = Ep[:, ml, tc].re("p (s l) -> p s l", s=ns)[:, :, ls - 1:ls].bc([128, ns, 64])
            fw.tt("dve", H32, H32, wl, ALU.mult)
            if kind == "s":
                fw.dma("sp", self.dout["srwS"][m].rearrange("p s v -> p s v"), H32)
            yield
            fw.copy("act", yt[:, 0:L], pY[:, 0:L])
            p1 = nextp()
            fw.mm(p1[:, 0:L], blkf, yt[:, 0:L])
            fw.stt(yt[:, 0:L], p1[:, 0:L], -1.0 / 64, yt[:, 0:L], ALU.mult, ALU.add)
            fw.act(ysq[:, 0:L], yt[:, 0:L], AF.Square)
            p2 = nextp()
            fw.mm(p2[:, 0:L], blkf, ysq[:, 0:L])
            fw.act(yr[:, 0:L], p2[:, 0:L], AF.Ln, bias=self.epsc("gn"), scale=1.0 / 64)
            fw.act(yr[:, 0:L], yr[:, 0:L], AF.Exp, scale=-0.5)
            fw.tt("dve", yt[:, 0:L], yt[:, 0:L], yr[:, 0:L], ALU.mult)
            fw.act(yt[:, 0:L], yt[:, 0:L], AF.Identity, bias=self.vc("lxb", m), scale=self.vc("lxg", m))
            fw.tt("dve", yt[:, 0:L], yt[:, 0:L], bonus[:, ml, tc], ALU.add)
            fw.tt("dve", B.abuf[:, m, bc], yt[:, 0:L], S.g[:, m, tc], ALU.mult)
            yield


def build_program(cfg, debug=()):
    nc0 = bass.Bass("TRN2", target_bir_lowering=False)
    p0 = Prog(nc0, cfg, plan=None, debug=debug)
    with nc0.allow_non_contiguous_dma(reason="small strided state/vector transfers"):
        p0.build()
    nc = bass.Bass("TRN2", target_bir_lowering=False)
    p = Prog(nc, cfg, plan=p0.plan, debug=debug)
    with nc.allow_non_contiguous_dma(reason="small strided state/vector transfers"):
        p.build()
    return nc, p


def prep_shared(cfg, P):
    cst, _, colm = make_consts(cfg)
    m = {"cst": cst, "colm": colm, "vec": make_vecs(cfg, P)}
    m.update(host_weights(cfg, P))
    bif = np.asarray(P["b_if_ab"][0], np.float32)
    m["bif"] = np.ascontiguousarray(np.stack([bif[:cfg.NHA], bif[cfg.NHA:]], 1))
    return m


def prep_core(cfg, P, xp_seq, sl):
    D, NS, NC = cfg.D, cfg.NS, cfg.NC
    f = lambda a: np.asarray(a, np.float32)
    m = {}
    xfull = np.concatenate([f(P["meta_tokens"]), f(xp_seq)], 0)
    m["xp"] = np.ascontiguousarray(xfull.T)
    xs = f(P["x_sample"])[sl]
    m["xs"] = np.ascontiguousarray(xs.reshape(NS * cfg.LS, D).T)
    C = f(P["state_mlstm_C"])[0, sl]
    n = f(P["state_mlstm_n"])[0, sl]
    m["cext"] = np.ascontiguousarray(np.concatenate([C, n[..., None]], -1))
    m["m0"] = np.ascontiguousarray(f(P["state_mlstm_m"])[0, sl].T)
    m["rgh"] = np.ascontiguousarray(f(P["state_rglru_h"])[0, sl].T)
    m["rgc"] = np.ascontiguousarray(f(P["state_rglru_conv"])[0, sl].transpose(2, 0, 1))
    S = f(P["state_rwkv_S"])[0, sl]
    H = S.transpose(0, 1, 3, 2).reshape(NS, NC, 2, 64, 64)
    m["rwS"] = np.ascontiguousarray(H.transpose(1, 2, 3, 0, 4).reshape(NC, 128, NS, 64))
    m["rwx"] = np.ascontiguousarray(f(P["state_rwkv_shift"])[0, sl].T)
    return m


def unpack_core(cfg, r):
    NS, NC, NHA, HDA = cfg.NS, cfg.NC, cfg.NHA, cfg.HDA
    o = {}
    o["yp"] = r["yp"].T[16:]
    o["ys"] = r["ys"].T.reshape(NS, cfg.LS, cfg.D)
    for pre, nb in (("p", 1), ("s", NS)):
        c = r[pre + "c"]
        o[pre + "C"] = c[..., :HDA]
        o[pre + "n"] = c[..., HDA]
        o[pre + "m"] = r[pre + "m"].T
        o[pre + "rgh"] = r[pre + "rgh"].T
        o[pre + "rgc"] = r[pre + "rgc"].transpose(1, 2, 0)
        H = r[pre + "rwS"].reshape(NC, 2, 64, nb, 64)
        o[pre + "S"] = H.transpose(3, 0, 1, 4, 2).reshape(nb, NC * 2, 64, 64)
        o[pre + "x"] = r[pre + "rwx"].T
    return o


_CACHE = {}


def kernel(**inputs):
    cfg = Cfg()
    P = inputs
    if "prog" not in _CACHE:
        _CACHE["prog"] = build_program(cfg)
    nc, prog = _CACHE["prog"]
    shared = prep_shared(cfg, P)
    in_maps = []
    for c in range(8):
        m = dict(shared)
        m.update(prep_core(cfg, P, np.asarray(P["x_prompt"])[c // 2], slice(c * cfg.NS, (c + 1) * cfg.NS)))
        in_maps.append(m)
    res = run_bass_kernel_spmd(nc, in_maps, core_ids=list(range(8)))
    outs = [unpack_core(cfg, r) for r in res.results]
    cat = lambda k, cores: np.ascontiguousarray(np.concatenate([outs[c][k] for c in cores], 0)).astype(np.float32)
    pc = [0, 2, 4, 6]
    ac = list(range(8))
    yp = np.stack([outs[c]["yp"] for c in pc], 0).astype(np.float32)
    ys = cat("ys", ac)
    res_t = [yp, ys]
    for pre, cores in (("p", pc), ("s", ac)):
        for k in ("C", "n", "m", "rgh", "rgc", "S", "x"):
            res_t.append(cat(pre + k, cores)[None])
    return tuple(res_t)
```

```python
import os
import numpy as np
import concourse.bass as bass
import concourse.mybir as mybir
from concourse.bass_utils import run_bass_kernel_spmd

F32 = mybir.dt.float32
BF16 = mybir.dt.bfloat16
AF = mybir.ActivationFunctionType
ALU = mybir.AluOpType
AX = mybir.AxisListType

LN_EPS = 1e-5
GN_EPS = 64e-5
NEG = -30000.0


class Cfg:
    def __init__(self, D=2048, NHA=4, nchunks=16, NS=16, LS=8, DEPTH=2, lora_w=96, lora_a=96, lora_g=256):
        self.D = D
        self.NC = D // 128
        self.MIXA = D // 2
        self.NHA = NHA
        self.HDA = self.MIXA // NHA
        self.HKC = self.HDA // 128
        self.MC = self.MIXA // 128
        self.RGW = D // 2
        self.RGB = self.RGW // 128
        self.NHC = D // 64
        self.DFF = 4 * D
        self.FC = self.DFF // 128
        self.nchunks = nchunks
        self.TP = 16 + 128 * nchunks
        self.NS = NS
        self.LS = LS
        self.TS = NS * LS
        self.LW, self.LA, self.LG = lora_w, lora_a, lora_g
        self.ALPHA = (2.0 * DEPTH) ** 0.25
        tiles = [("p", 0, 16)] + [("p", 16 + 128 * i, 128) for i in range(nchunks)]
        blocks = []
        cur = []
        n = 0
        for t in tiles:
            if n + t[2] > 512:
                blocks.append(cur)
                cur, n = [], 0
            cur.append(t)
            n += t[2]
        if n + self.TS > 512:
            blocks.append(cur)
            cur = []
        cur.append(("s", 0, self.TS))
        blocks.append(cur)
        self.blocks = blocks


class T:
    __slots__ = ("ap", "name", "w", "r", "dsem", "dcount")

    def __init__(self, ap, name):
        self.ap, self.name = ap, name
        self.w, self.r = {}, {}
        self.dsem, self.dcount = None, 0

    def __getitem__(self, k):
        return V(self, self.ap[k])

    @property
    def v(self):
        return V(self, self.ap)


class V:
    __slots__ = ("t", "ap", "ts")

    def __init__(self, t, ap, ts=None):
        self.t, self.ap, self.ts = t, ap, ts

    def __getitem__(self, k):
        return V(self.t, self.ap[k], self.ts)

    def bc(self, shape):
        return V(self.t, self.ap.to_broadcast(list(shape)), self.ts)

    def re(self, pat, **kw):
        return V(self.t, self.ap.rearrange(pat, **kw), self.ts)

    def un(self, axis):
        return V(self.t, self.ap.unsqueeze(axis), self.ts)

    def tiles(self):
        return self.ts if self.ts is not None else [self.t]


def _ap(x):
    return x.ap if isinstance(x, V) else x


class Fw:
    ENG = ("pe", "dve", "act", "pool", "sp")

    def __init__(self, nc, dry=False):
        self.nc, self.dry = nc, dry
        self.eng = {"pe": nc.tensor, "dve": nc.vector, "act": nc.scalar, "pool": nc.gpsimd, "sp": nc.sync}
        self.cnt = {e: 0 for e in self.ENG}
        self.waited = {e: {} for e in self.ENG}
        self.dsems = {}
        self.sem = {}
        if not dry:
            self.sem = {e: nc.alloc_semaphore("s_" + e) for e in self.ENG}
        self.n_inst = 0
        self._uid = 0
        self.dma_keys = {}
        self.free_dsems = []
        self.scopes = []
        self.freed = {}
        self.min_free = 1 << 30

    def scope(self):
        fw = self

        class _S:
            def __enter__(s):
                s.tiles = []
                s.guards = []
                fw.scopes.append(s)
                return s

            def __exit__(s, *a):
                fw.scopes.pop()
                for t in s.tiles:
                    for d in (t.w, t.r):
                        for k, v in d.items():
                            if fw.freed.get(k, 0) < v:
                                fw.freed[k] = v
                    if t.dsem is not None:
                        fw.free_dsems.append(t.dsem)
                for g in reversed(s.guards):
                    g.__exit__(None, None, None)
                return False
        return _S()

    def sb(self, shape, dtype=F32, name=None):
        self._uid += 1
        name = (name or "sb") + f"_{self._uid}"
        if self.scopes:
            g = self.nc.sbuf_tensor(name, list(shape), dtype)
            h = g.__enter__()
            t = T(h.ap(), name)
            t.w = dict(self.freed)
            self.scopes[-1].tiles.append(t)
            self.scopes[-1].guards.append(g)
        else:
            t = T(self.nc.alloc_sbuf_tensor(name, list(shape), dtype).ap(), name)
        self.min_free = min(self.min_free, self.nc.sbuf_bytes_remaining)
        return t

    def ps(self, shape, dtype=F32, name=None):
        self._uid += 1
        name = (name or "ps") + f"_{self._uid}"
        return self.nc.alloc_psum_tensor(name, list(shape), dtype).ap()

    def _collect(self, reads, writes, eng, skip=None):
        deps = {}
        for t in reads:
            for k, v in t.w.items():
                if k != skip and deps.get(k, 0) < v:
                    deps[k] = v
        for t in writes:
            for k, v in t.w.items():
                if (k == eng and eng == "pe") or k == skip:
                    continue
                if deps.get(k, 0) < v:
                    deps[k] = v
            for k, v in t.r.items():
                if k == skip:
                    continue
                if deps.get(k, 0) < v:
                    deps[k] = v
        return deps

    def _waits(self, eng, deps):
        e = self.eng[eng]
        wd = self.waited[eng]
        for k, v in deps.items():
            if wd.get(k, 0) >= v:
                continue
            sem = self.sem[k] if k in self.sem else self.dsems[k]
            e.wait_ge(sem, v)
            wd[k] = v

    def op(self, eng, fn, ins, outs):
        self.n_inst += 1
        if self.dry:
            return
        reads = []
        for x in ins:
            if isinstance(x, V):
                for t in x.tiles():
                    if t not in reads:
                        reads.append(t)
        writes = []
        for x in outs:
            for t in x.tiles():
                if t not in writes:
                    writes.append(t)
        deps = self._collect(reads, writes, eng)
        self._waits(eng, deps)
        ins_ = fn()
        self.cnt[eng] += 1
        idx = self.cnt[eng]
        ins_.then_inc(self.sem[eng], 1)
        for t in writes:
            t.w = {eng: idx}
            t.r = {}
        for t in reads:
            if t not in writes:
                t.r[eng] = idx

    def dma(self, q, out, in_, join=False):
        self.n_inst += 1
        if self.dry:
            return
        reads = in_.tiles() if isinstance(in_, V) else []
        writes = out.tiles() if isinstance(out, V) else []
        st = (writes[0] if writes else reads[0])
        if st.dsem is None:
            if self.free_dsems:
                st.dsem = self.free_dsems.pop()
            else:
                st.dsem = f"d{len(self.dsems)}"
                self.dsems[st.dsem] = self.nc.alloc_semaphore(st.dsem)
                self.dma_keys[st.dsem] = 0
            prev = self.dma_keys[st.dsem]
            if prev > 0:
                self._waits(q, {st.dsem: prev})
        deps = self._collect(reads, writes, q, skip=(st.dsem if join else None))
        self._waits(q, deps)
        ins_ = self.eng[q].dma_start(out=_ap(out), in_=_ap(in_))
        self.dma_keys[st.dsem] += 16
        cnt = self.dma_keys[st.dsem]
        ins_.then_inc(self.dsems[st.dsem], 16)
        for t in writes:
            if join and st.dsem in t.w:
                t.w[st.dsem] = cnt
            else:
                t.w = {st.dsem: cnt}
                t.r = {}
        for t in reads:
            t.r[st.dsem] = cnt

    def final_wait(self, eng="sp"):
        if self.dry:
            return
        for k, c in self.dma_keys.items():
            if c > 0:
                self.eng[eng].wait_ge(self.dsems[k], c)

    def _e(self, eng):
        return self.eng[eng]

    def tt(self, eng, out, a, b, op):
        self.op(eng, lambda: self._e(eng).tensor_tensor(out.ap, a.ap, b.ap, op), [a, b], [out])

    def ts(self, eng, out, a, s1, s2, op0, op1=None):
        if op1 is None:
            self.op(eng, lambda: self._e(eng).tensor_scalar(out.ap, a.ap, _ap(s1), None, op0), [a, s1], [out])
        else:
            self.op(eng, lambda: self._e(eng).tensor_scalar(out.ap, a.ap, _ap(s1), _ap(s2), op0, op1),
                    [a, s1, s2], [out])

    def stt(self, out, a, s, b, op0, op1):
        self.op("dve", lambda: self.nc.vector.scalar_tensor_tensor(out.ap, a.ap, _ap(s), b.ap, op0, op1),
                [a, s, b], [out])

    def scan(self, out, d0, d1, init, op0, op1):
        self.op("dve", lambda: self.nc.vector.tensor_tensor_scan(out.ap, d0.ap, d1.ap, _ap(init), op0, op1),
                [d0, d1, init], [out])

    def copy(self, eng, out, a):
        if eng == "act":
            self.op(eng, lambda: self.nc.scalar.copy(out.ap, a.ap), [a], [out])
        else:
            self.op(eng, lambda: self._e(eng).tensor_copy(out.ap, a.ap), [a], [out])

    def act(self, out, a, func, bias=0.0, scale=1.0):
        self.op("act", lambda: self.nc.scalar.activation(out.ap, a.ap, func, bias=_ap(bias), scale=_ap(scale)),
                [a, bias, scale], [out])

    def recip(self, out, a):
        self.op("dve", lambda: self.nc.vector.reciprocal(out.ap, a.ap), [a], [out])

    def memset(self, eng, out, val):
        self.op(eng, lambda: self._e(eng).memset(out.ap, val), [], [out])

    def mm(self, out, lhsT, rhs, start=True, stop=True):
        self.op("pe", lambda: self.nc.tensor.matmul(out.ap, lhsT.ap, rhs.ap, start=start, stop=stop),
                [lhsT, rhs], [out])

    def tr(self, out, a, ident):
        self.op("pe", lambda: self.nc.tensor.transpose(out.ap, a.ap, ident.ap), [a, ident], [out])


def _panels(W, pw):
    K, N = W.shape
    assert K % 128 == 0 and N % pw == 0
    return np.ascontiguousarray(W.reshape(K // 128, 128, N // pw, pw).transpose(2, 1, 0, 3))


def _fm(vec):
    v = np.asarray(vec, np.float32).reshape(-1)
    return np.ascontiguousarray(v.reshape(-1, 128).T)


def make_consts(cfg):
    cols = {}
    parts = []

    def add(name, arr):
        arr = np.asarray(arr, np.float32)
        assert arr.shape[0] == 128
        cols[name] = (sum(p.shape[1] for p in parts), arr.shape[1])
        parts.append(arr)

    I = np.arange(128)
    add("ident", np.eye(128))
    add("ones", np.ones((128, 128)))
    add("blk", (I[:, None] // 64 == I[None, :] // 64).astype(np.float32))
    add("hm", np.stack([(I < 64), (I >= 64)], 1).astype(np.float32))
    for kind, ns, L in (("p", 1, 128), ("s", cfg.NS, cfg.LS)):
        seg = I // L
        same = seg[:, None] == seg[None, :]
        le = I[:, None] <= I[None, :]
        lt = I[:, None] < I[None, :]
        add(kind + "_maskb", np.where(same & le, 0.0, NEG))
        strict = (same & lt).astype(np.float32)
        incl = (same & le).astype(np.float32)
        add(kind + "_m2", np.concatenate([strict, incl], 1))
        add(kind + "_strictT", strict.T.copy())
        start = (I % L == 0)
        add(kind + "_rmask", np.tile(np.where(start, 0.0, 1.0)[None, :], (128, 1)))
        add(kind + "_rbias", np.tile(np.where(start, -1e30, 0.0)[None, :], (128, 1)))
        rowmask = (seg[:, None] == np.arange(ns)[None, :]).astype(np.float32)
        add(kind + "_rowmask", rowmask)
    seg = I // cfg.LS
    rowmask = (seg[:, None] == np.arange(cfg.NS)[None, :]).astype(np.float32)
    colmask = np.tile(rowmask.T.reshape(1, cfg.NS * 128), (128, 1)).astype(np.float32)
    return np.concatenate(parts, 1), cols, colmask


def vec_cols(cfg):
    NC, MC, RGB = cfg.NC, cfg.MC, cfg.RGB
    names = []
    for l in range(2):
        names += [(f"ln1g{l}", NC), (f"ln1b{l}", NC), (f"ln2g{l}", NC), (f"ln2b{l}", NC)]
    names += [("mng", MC)] + [(f"cw{j}", RGB) for j in range(4)] + [("cb", RGB), ("ba", RGB), ("bx", RGB), ("lam", RGB)]
    names += [(f"mu{j}", NC) for j in range(6)]
    names += [(n, NC) for n in ("w0", "a0", "kk", "ka", "rk", "lxg", "lxb")]
    cols, o = {}, 0
    for n, c in names:
        cols[n] = (o, c)
        o += c
    return cols, o


def make_vecs(cfg, p):
    cols, n = vec_cols(cfg)
    out = np.zeros((128, n), np.float32)

    def put(name, vec):
        o, c = cols[name]
        a = _fm(vec)
        assert a.shape == (128, c), (name, a.shape, c)
        out[:, o:o + c] = a

    for l in range(2):
        put(f"ln1g{l}", p["ln1_g"][l]); put(f"ln1b{l}", p["ln1_b"][l])
        put(f"ln2g{l}", p["ln2_g"][l]); put(f"ln2b{l}", p["ln2_b"][l])
    put("mng", p["mlstm_norm_g"][0])
    for j in range(4):
        put(f"cw{j}", p["rg_conv_w"][0, j])
    put("cb", p["rg_conv_b"][0]); put("ba", p["rg_ba"][0]); put("bx", p["rg_bx"][0]); put("lam", p["rg_lambda"][0])
    for j in range(6):
        put(f"mu{j}", p["rw_mu"][0, j])
    put("w0", p["rw_w0"][0]); put("a0", p["rw_a0"][0]); put("kk", p["rw_kk"][0]); put("ka", p["rw_ka"][0])
    put("rk", np.asarray(p["rw_rk"][0]).reshape(-1)); put("lxg", p["rw_lnx_g"][0]); put("lxb", p["rw_lnx_b"][0])
    return out


def weight_specs(cfg):
    D, NC, MIXA, RGB, DFF, FC = cfg.D, cfg.NC, cfg.MIXA, cfg.RGB, cfg.DFF, cfg.FC
    PW = 256
    s = {}
    for n in ("wq", "wk", "wv", "wo", "wxr", "wgr"):
        s[n] = [MIXA // PW, 128, NC, PW]
    s["wig"] = [1, 128, NC, cfg.NHA]
    s["wfg"] = [1, 128, NC, cfg.NHA]
    s["rgax"] = [1, 128, RGB, 256]
    s["wout"] = [D // PW, 128, NC, PW]
    KH = min(FC, 32)
    for l in range(2):
        s[f"wup{l}"] = [DFF // PW, 128, NC, PW]
        s[f"wdn{l}"] = [NC * (FC // KH), 128, KH, 128]
    for n in ("rwr", "rwk", "rwv", "rwo"):
        s[n] = [D // PW, 128, NC, PW]
    s["w1"] = [1, 128, NC, cfg.LW]
    s["a1"] = [1, 128, NC, cfg.LA]
    s["g1"] = [1, 128, NC, cfg.LG]
    s["w2"] = [1, cfg.LW, 1, D]
    s["a2"] = [1, cfg.LA, 1, D]
    s["g2"] = [1, 128, cfg.LG // 128, D]
    return s


def host_weights(cfg, p):
    MIXA, RGW, NHA, FC, NC = cfg.MIXA, cfg.RGW, cfg.NHA, cfg.FC, cfg.NC
    PW = 256
    W = np.asarray(p["w_in_ab"][0], np.float32)
    o = 0
    out = {}
    for n in ("wq", "wk", "wv", "wo"):
        out[n] = _panels(W[:, o:o + MIXA], PW)
        o += MIXA
    out["wig"] = _panels(W[:, o:o + NHA], NHA)
    o += NHA
    out["wfg"] = _panels(W[:, o:o + NHA], NHA)
    o += NHA
    out["wxr"] = _panels(W[:, o:o + RGW], PW)
    o += RGW
    out["wgr"] = _panels(W[:, o:o + RGW], PW)
    o += RGW
    out["rgax"] = np.ascontiguousarray(np.concatenate([np.asarray(p["rg_wa"][0], np.float32).transpose(1, 0, 2),
                                                       np.asarray(p["rg_wx"][0], np.float32).transpose(1, 0, 2)], -1))[None]
    out["wout"] = _panels(np.asarray(p["w_out_ab"][0], np.float32), PW)
    KH = min(FC, 32)
    for l in range(2):
        out[f"wup{l}"] = _panels(np.asarray(p["w_up"][l], np.float32), PW)
        wd = _panels(np.asarray(p["w_down"][l], np.float32), 128)
        wd = wd.reshape(NC, 128, FC // KH, KH, 128).transpose(0, 2, 1, 3, 4)
        out[f"wdn{l}"] = np.ascontiguousarray(wd.reshape(NC * (FC // KH), 128, KH, 128))
    for n, k in (("rwr", "rw_wr"), ("rwk", "rw_wk"), ("rwv", "rw_wv"), ("rwo", "rw_wo")):
        out[n] = _panels(np.asarray(p[k][0], np.float32), PW)
    out["w1"] = _panels(np.asarray(p["rw_w1"][0], np.float32), cfg.LW)
    out["a1"] = _panels(np.asarray(p["rw_a1"][0], np.float32), cfg.LA)
    out["g1"] = _panels(np.asarray(p["rw_g1"][0], np.float32), cfg.LG)
    out["w2"] = np.ascontiguousarray(np.asarray(p["rw_w2"][0], np.float32))[None, :, None, :]
    out["a2"] = np.ascontiguousarray(np.asarray(p["rw_a2"][0], np.float32))[None, :, None, :]
    out["g2"] = _panels(np.asarray(p["rw_g2"][0], np.float32), cfg.D)
    specs = weight_specs(cfg)
    for n in out:
        assert list(out[n].shape) == specs[n], (n, out[n].shape, specs[n])
    return out


def io_specs(cfg):
    D, NS, NHA, HDA, RGW, NC = cfg.D, cfg.NS, cfg.NHA, cfg.HDA, cfg.RGW, cfg.NC
    ins = {
        "xp": [D, cfg.TP], "xs": [D, cfg.TS],
        "cext": [NS, NHA, HDA, HDA + 1], "m0": [NHA, NS],
        "rgh": [RGW, NS], "rgc": [RGW, NS, 3],
        "rwS": [NC, 128, NS, 64], "rwx": [D, NS],
        "bif": [NHA, 2], "colm": [128, NS * 128],
    }
    outs = {
        "yp": [D, cfg.TP], "ys": [D, cfg.TS],
        "pc": [1, NHA, HDA, HDA + 1], "pm": [NHA, 1], "prgh": [RGW, 1], "prgc": [RGW, 1, 3],
        "prwS": [NC, 128, 1, 64], "prwx": [D, 1],
        "sc": [NS, NHA, HDA, HDA + 1], "sm": [NHA, NS], "srgh": [RGW, NS], "srgc": [RGW, NS, 3],
        "srwS": [NC, 128, NS, 64], "srwx": [D, NS],
    }
    return ins, outs


class Tile:
    def __init__(self, kind, tok0, L, c0, cfg):
        self.kind, self.tok0, self.L, self.c0 = kind, tok0, L, c0
        if kind == "p":
            self.ns, self.ls = 1, L
        else:
            self.ns, self.ls = cfg.NS, cfg.LS
        self.nsteps = int(np.ceil(np.log2(self.ls)))


class Reg:
    pass


class StopBuild(Exception):
    pass


STOP = None


class Blk:
    pass


class Prog:
    WELEMS = 4096

    def __init__(self, nc, cfg, plan=None, debug=()):
        self.nc, self.cfg = nc, cfg
        self.dry = plan is None
        self.fw = Fw(nc, dry=self.dry)
        self.plan = plan if plan is not None else []
        self.wk = 0
        self.wissued = 0
        self.debug = debug
        self.dbg_out = {}
        self.wspec = weight_specs(cfg)

    def declare(self):
        nc, cfg = self.nc, self.cfg
        ins, outs = io_specs(cfg)
        self.din, self.dout = {}, {}
        for n, s in ins.items():
            self.din[n] = nc.dram_tensor(n, s, F32, kind="ExternalInput").ap()
        cst, self.ccols, _ = make_consts(cfg)
        self.ncst = cst.shape[1]
        self.din["cst"] = nc.dram_tensor("cst", [128, self.ncst], F32, kind="ExternalInput").ap()
        self.vcols, self.nvec = vec_cols(cfg)
        self.din["vec"] = nc.dram_tensor("vec", [128, self.nvec], F32, kind="ExternalInput").ap()
        self.dw = {}
        for n, s in self.wspec.items():
            self.dw[n] = nc.dram_tensor(n, s, F32, kind="ExternalInput").ap()
        for n, s in outs.items():
            self.dout[n] = nc.dram_tensor(n, s, F32, kind="ExternalOutput").ap()

    def wpanel(self, name, pan):
        fw = self.fw
        shp = self.wspec[name]
        parts, KC, pw = shp[1], shp[2], shp[3]
        assert KC * pw <= self.WELEMS, (name, KC, pw)
        k = self.wk
        self.wk += 1
        if self.dry:
            self.plan.append((name, pan))
        else:
            assert self.plan[k] == (name, pan), (k, self.plan[k], name, pan)
            while self.wissued < min(len(self.plan), k + 4):
                j = self.wissued
                nm, pn = self.plan[j]
                s2 = self.wspec[nm]
                buf = self.wbufs[j % 4]
                dst = buf[0:s2[1], 0:s2[2] * s2[3]]
                fw.dma("pool", dst, self.dw[nm][pn].rearrange("p k w -> p (k w)"))
                self.wissued += 1
        buf = self.wbufs[k % 4]
        return buf[0:parts, 0:KC * pw].re("p (k w) -> p k w", k=KC)

    def psum(self):
        b = self.bank[self.psi % 4]
        self.psi += 1
        return b

    def cc(self, name, rows=128, c0=0, n=None):
        o, w = self.ccols[name]
        n = w - c0 if n is None else n
        return self.CST[0:rows, o + c0:o + c0 + n]

    def vc(self, name, j=0, rows=128):
        o, w = self.vcols[name]
        return self.VEC[0:rows, o + j:o + j + 1]

    def dump(self, name, v, shape):
        if name not in self.debug:
            return
        fw = self.fw
        key = name
        i = 0
        while key in self.dbg_out:
            i += 1
            key = f"{name}_{i}"
        self.dbg_out[key] = self.nc.dram_tensor("dbg_" + key, list(shape), F32, kind="ExternalOutput").ap()
        tmp = fw.sb(list(shape), F32, "dbg")
        fw.copy("dve", tmp.v, v)
        fw.dma("sp", self.dbg_out[key], tmp.v)

    def build(self):
        cfg, fw, nc = self.cfg, self.fw, self.nc
        self.declare()
        NC = cfg.NC
        self.CST = fw.sb([128, self.ncst], F32, "cst")
        self.VEC = fw.sb([128, self.nvec], F32, "vec")
        fw.dma("sp", self.CST.v, self.din["cst"])
        fw.dma("sp", self.VEC.v, self.din["vec"])
        self.wbufs = [fw.sb([128, self.WELEMS], BF16, f"wbuf{i}") for i in range(4)]
        psA = fw.ps([128, 4, 512], F32, "psA")
        psD = fw.ps([128, 4, 512], F32, "psD")
        self.bank = [T(psA[:, i, :], f"bankA{i}") for i in range(4)] + [T(psD[:, i, :], f"bankD{i}") for i in range(4)]
        self.psD = V(self.bank[4], psD, ts=self.bank[4:8])
        self.psi = 0
        self.identb = fw.sb([128, 128], BF16, "identb")
        fw.copy("dve", self.identb.v, self.cc("ident"))
        self.onesb = fw.sb([128, 128], BF16, "onesb")
        fw.copy("dve", self.onesb.v, self.cc("ones"))
        self.colmb = fw.sb([128, cfg.NS * 128], BF16, "colmb")
        fw.dma("pool", self.colmb.v, self.din["colm"])
        try:
            self.derived_vecs()
            self.init_states()
            self.chk("init")
            for bi, blk in enumerate(cfg.blocks):
                self.run_block(bi, blk)
        except StopBuild:
            pass
        self.write_prompt_states()
        fw.final_wait("sp")

    def chk(self, name):
        if not hasattr(self, "phase_log"):
            self.phase_log = []
        self.phase_log.append((name, self.fw.cnt["pe"]))
        if STOP == name:
            raise StopBuild()

    def derived_vecs(self):
        cfg, fw = self.cfg, self.fw
        RGB, NC, NHA = cfg.RGB, cfg.NC, cfg.NHA
        self.DV = fw.sb([128, 2 * RGB + 7 * NC], F32, "dv")
        o, _ = self.vcols["lam"]
        lam = self.VEC[:, o:o + RGB]
        t = self.DV[:, 0:RGB]
        fw.act(t, lam, AF.Exp, scale=-1.0)
        fw.act(t, t, AF.Ln, bias=1.0)
        fw.ts("dve", self.DV[:, RGB:2 * RGB], t, -16.0, None, ALU.mult)
        fw.ts("dve", t, t, -8.0, None, ALU.mult)
        b = 2 * RGB
        for j in range(6):
            o, _ = self.vcols[f"mu{j}"]
            fw.ts("dve", self.DV[:, b + j * NC:b + (j + 1) * NC], self.VEC[:, o:o + NC], -1.0, 1.0, ALU.mult, ALU.add)
        b2 = b + 6 * NC
        o, _ = self.vcols["ka"]
        fw.ts("dve", self.DV[:, b2:b2 + NC], self.VEC[:, o:o + NC], -1.0, 1.0, ALU.mult, ALU.add)
        self.dv_c1 = lambda n: self.DV[:, n:n + 1]
        self.dv_c2 = lambda n: self.DV[:, RGB + n:RGB + n + 1]
        self.dv_1mmu = lambda j, c: self.DV[:, b + j * NC + c:b + j * NC + c + 1]
        self.dv_1mka = lambda c: self.DV[:, b2 + c:b2 + c + 1]
        self.EPS = fw.sb([128, 2], F32, "eps")
        fw.memset("dve", self.EPS[:, 0:1], LN_EPS)
        fw.memset("dve", self.EPS[:, 1:2], GN_EPS)
        self.epsc = lambda k: self.EPS[:, 0:1] if k == "ln" else self.EPS[:, 1:2]
        self.BIF = fw.sb([NHA, 3], F32, "bif")
        fw.dma("sp", self.BIF[:, 0:2], self.din["bif"])
        fw.ts("dve", self.BIF[:, 2:3], self.BIF[:, 1:2], -1.0, None, ALU.mult)
        self.SEL = fw.sb([NHA, NHA, 128], F32, "sel")
        for h in range(NHA):
            fw.copy("dve", self.SEL[:, h, :], self.cc("ident", rows=NHA, c0=h, n=1).bc([NHA, 128]))

    def init_states(self):
        cfg, fw = self.cfg, self.fw
        NHA, HKC, HDA, RGB, NC = cfg.NHA, cfg.HKC, cfg.HDA, cfg.RGB, cfg.NC
        self.pC = fw.sb([128, NHA, HKC, HDA + 1], F32, "pC")
        fw.memset("dve", self.pC.v, 0.0)
        self.pM = fw.sb([NHA, 1], F32, "pM")
        fw.memset("dve", self.pM.v, 0.0)
        self.sM = fw.sb([NHA, cfg.NS], F32, "sM")
        fw.dma("sp", self.sM.v, self.din["m0"])
        self.pRH = fw.sb([128, RGB, 1], F32, "pRH")
        fw.memset("dve", self.pRH.v, 0.0)
        self.pRC = fw.sb([128, RGB, 1, 3], F32, "pRC")
        fw.memset("dve", self.pRC.v, 0.0)
        self.pH = fw.sb([128, NC, 1, 64], F32, "pH")
        fw.memset("dve", self.pH.v, 0.0)
        self.pSH = fw.sb([128, NC, 1], F32, "pSH")
        fw.memset("dve", self.pSH.v, 0.0)

    def write_prompt_states(self):
        fw, do = self.fw, self.dout
        fw.dma("sp", do["pc"][0].rearrange("h (kc p) v -> p h kc v", p=128), self.pC.v)
        fw.dma("sp", do["pm"], self.pM.v)
        fw.dma("sp", do["prgh"].rearrange("(n p) o -> p n o", p=128), self.pRH.v)
        fw.dma("sp", do["prgc"].rearrange("(n p) o j -> p n o j", p=128), self.pRC.v)
        fw.dma("sp", do["prwS"].rearrange("m p o v -> p m o v"), self.pH.v)
        fw.dma("sp", do["prwx"].rearrange("(c p) o -> p c o", p=128), self.pSH.v)

    def run_block(self, bi, blk):
        cfg, fw = self.cfg, self.fw
        NC = cfg.NC
        B = Blk()
        B.tiles = []
        c = 0
        for (kind, tok0, L) in blk:
            B.tiles.append(Tile(kind, tok0, L, c, cfg))
            c += L
        B.Tb = c
        B.regs = []
        e = 0
        pt = [t for t in B.tiles if t.kind == "p"]
        if pt:
            r = Reg()
            r.kind, r.c0, r.ns, r.L, r.tok0 = "p", pt[0].c0, 1, sum(t.L for t in pt), pt[0].tok0
            r.e0 = e
            e += 1 + r.L
            B.regs.append(r)
        st = [t for t in B.tiles if t.kind == "s"]
        if st:
            r = Reg()
            r.kind, r.c0, r.ns, r.L, r.tok0 = "s", st[0].c0, cfg.NS, cfg.LS, 0
            r.e0 = e
            e += r.ns * (1 + r.L)
            B.regs.append(r)
        B.EXT = e
        for r in B.regs:
            r.n = r.ns * r.L
        with fw.scope():
            B.hres = fw.sb([128, NC, B.EXT], F32, "hres")
            B.abuf = fw.sb([128, NC, B.Tb], BF16, "abuf")
            self.B = B
            for r in B.regs:
                if r.kind == "p":
                    src = self.din["xp"][:, r.tok0:r.tok0 + r.L].rearrange("(c p) t -> p c t", p=128)
                    fw.dma("sp", B.hres[:, :, r.e0 + 1:r.e0 + 1 + r.L], src)
                else:
                    for kc in range(NC):
                        src = self.din["xs"][kc * 128:(kc + 1) * 128, :].rearrange("p (s t) -> p s t", s=r.ns)
                        fw.dma("sp", self.hv(kc, r), src, join=True)
                        src2 = self.din["rwx"][kc * 128:(kc + 1) * 128, :]
                        fw.dma("sp", self.hv(kc, r, -1)[:, :, 0], src2, join=True)
            self.chk("loadx")
            self.to_bf16(B)
            self.chk("bf16")
            self.layer0_mixer(B)
            self.chk("wout")
            self.layernorm(B, "ln1g0", "ln1b0", bf=True)
            self.chk("ln1")
            self.mlp(B, 0)
            self.chk("mlp0")
            self.layernorm(B, "ln2g0", "ln2b0", bf=False)
            self.chk("ln2")
            for r in B.regs:
                if r.kind == "p":
                    fw.copy("dve", B.hres[:, :, r.e0:r.e0 + 1], self.pSH.v)
                    fw.copy("dve", self.pSH.v, B.hres[:, :, r.e0 + r.L:r.e0 + r.L + 1])
                else:
                    for kc in range(NC):
                        dst = self.dout["srwx"][kc * 128:(kc + 1) * 128, :]
                        fw.dma("sp", dst, self.hv(kc, r)[:, :, r.L - 1], join=True)
            self.layer1_mixer(B)
            self.chk("l1mix")
            self.layernorm(B, "ln1g1", "ln1b1", bf=True)
            self.mlp(B, 1)
            self.layernorm(B, "ln2g1", "ln2b1", bf=False)
            for r in B.regs:
                if r.kind == "p":
                    dst = self.dout["yp"][:, r.tok0:r.tok0 + r.L].rearrange("(c p) t -> p c t", p=128)
                    fw.dma("sp", dst, B.hres[:, :, r.e0 + 1:r.e0 + 1 + r.L])
                else:
                    for kc in range(NC):
                        dst = self.dout["ys"][kc * 128:(kc + 1) * 128, :].rearrange("p (s t) -> p s t", s=r.ns)
                        fw.dma("sp", dst, self.hv(kc, r), join=True)

    def hv(self, kc, r, shift=0):
        B = self.B
        v = B.hres[:, kc, r.e0:r.e0 + r.ns * (1 + r.L)].re("p (s l) -> p s l", s=r.ns)
        return v[:, :, 1 + shift:1 + shift + r.L]

    def pv(self, v2d, r):
        return v2d[:, r.c0:r.c0 + r.n].re("p (s l) -> p s l", s=r.ns)

    def to_bf16(self, B):
        fw = self.fw
        for kc in range(self.cfg.NC):
            for r in B.regs:
                fw.copy("act" if kc % 2 else "dve", self.pv(B.abuf[:, kc, :], r), self.hv(kc, r))

    def proj_fm(self, wname, rhs_fn, N, evac):
        fw = self.fw
        npan, parts, KC, pw = self.wspec[wname]
        for pan in range(npan):
            W = self.wpanel(wname, pan)
            for mi in range(pw // 128):
                ps = self.psum()
                for kc in range(KC):
                    fw.mm(ps[:, 0:N], W[:, kc, mi * 128:(mi + 1) * 128], rhs_fn(kc), start=(kc == 0), stop=(kc == KC - 1))
                evac(pan * (pw // 128) + mi, ps)
            self.bg_step()

    def bg_step(self):
        g = getattr(self, "_bg", None)
        if g is not None:
            try:
                next(g)
            except StopIteration:
                self._bg = None

    def proj_resid(self, B, wname, N0=0, N=None):
        fw, cfg = self.fw, self.cfg

        def evac(m, ps):
            for r in B.regs:
                hv = self.hv(m, r)
                fw.stt(hv, hv, cfg.ALPHA, self.pv(ps[:, 0:B.Tb], r), ALU.mult, ALU.add)
        MC = cfg.MC
        if wname == "wout":
            yr = self.M.yr
            self.proj_fm(wname, lambda kc: B.abuf[:, kc, :] if kc < MC else yr[:, kc - MC, :], B.Tb, evac)
        else:
            self.proj_fm(wname, lambda kc: B.abuf[:, kc, :], B.Tb, evac)

    def layernorm(self, B, gname, bname, bf):
        fw, cfg = self.fw, self.cfg
        NC, Tb, D = cfg.NC, B.Tb, cfg.D
        onesf = self.cc("ones")
        with fw.scope():
            mu = fw.sb([128, Tb], F32, "lnmu")
            rs = fw.sb([128, Tb], F32, "lnrs")
            sq = [fw.sb([128, Tb], F32, f"lnsq{i}") for i in range(2)]
            ps1 = self.psum()
            ps2 = self.psum()
            for r in B.regs:
                for kc in range(NC):
                    hv = self.hv(kc, r)
                    s = sq[kc % 2]
                    fw.act(self.pv(s.v, r), hv, AF.Square)
                    fw.mm(self.pv(ps1[:, 0:Tb], r), onesf, hv, start=(kc == 0), stop=(kc == NC - 1))
                    fw.mm(self.pv(ps2[:, 0:Tb], r), onesf, self.pv(s.v, r), start=(kc == 0), stop=(kc == NC - 1))
            fw.ts("dve", mu[:, 0:Tb], ps1[:, 0:Tb], 1.0 / D, None, ALU.mult)
            m2 = sq[0]
            fw.tt("dve", m2[:, 0:Tb], mu[:, 0:Tb], mu[:, 0:Tb], ALU.mult)
            fw.stt(rs[:, 0:Tb], ps2[:, 0:Tb], 1.0 / D, m2[:, 0:Tb], ALU.mult, ALU.subtract)
            fw.ts("dve", rs[:, 0:Tb], rs[:, 0:Tb], 0.0, None, ALU.max)
            fw.act(rs[:, 0:Tb], rs[:, 0:Tb], AF.Ln, bias=self.epsc("ln"))
            fw.act(rs[:, 0:Tb], rs[:, 0:Tb], AF.Exp, scale=-0.5)
            for kc in range(NC):
                for r in B.regs:
                    hv = self.hv(kc, r)
                    fw.tt("dve", hv, hv, self.pv(mu.v, r), ALU.subtract)
                    fw.tt("dve", hv, hv, self.pv(rs.v, r), ALU.mult)
                    if bf:
                        fw.act(self.pv(B.abuf[:, kc, :], r), hv, AF.Identity, bias=self.vc(bname, kc), scale=self.vc(gname, kc))
                    fw.ts("dve", hv, hv, self.vc(gname, kc), self.vc(bname, kc), ALU.mult, ALU.add)

    def mlp(self, B, l):
        fw, cfg = self.fw, self.cfg
        NC, FC, Tb = cfg.NC, cfg.FC, B.Tb
        with fw.scope():
            hid = fw.sb([128, FC, Tb], BF16, "hid")
            tmp = [fw.sb([128, Tb], F32, f"mlptmp{i}") for i in range(2)]

            def evac(f, ps):
                t = tmp[f % 2]
                fw.act(t[:, 0:Tb], ps[:, 0:Tb], AF.Relu)
                fw.tt("dve", hid[:, f, :], t[:, 0:Tb], t[:, 0:Tb], ALU.mult)
            self.proj_fm(f"wup{l}", lambda kc: B.abuf[:, kc, :], Tb, evac)
            npan, parts, KH, pw = self.wspec[f"wdn{l}"]
            nh = FC // KH
            for m in range(NC):
                ps = self.psum()
                for hf in range(nh):
                    W = self.wpanel(f"wdn{l}", m * nh + hf)
                    for kk in range(KH):
                        f = hf * KH + kk
                        fw.mm(ps[:, 0:Tb], W[:, kk, :], hid[:, f, :], start=(f == 0), stop=(f == FC - 1))
                for r in B.regs:
                    hv = self.hv(m, r)
                    fw.stt(hv, hv, cfg.ALPHA, self.pv(ps[:, 0:Tb], r), ALU.mult, ALU.add)

    def layer0_mixer(self, B):
        fw, cfg = self.fw, self.cfg
        NC, MC, NHA, HDA, HKC, RGB, Tb = cfg.NC, cfg.MC, cfg.NHA, cfg.HDA, cfg.HKC, cfg.RGB, B.Tb
        nt = len(B.tiles)
        with fw.scope():
            M = Blk()
            self.M = M
            M.qT = fw.sb([128, MC, Tb], BF16, "qT")
            M.kT = fw.sb([128, MC, Tb], BF16, "kT")
            M.oT = fw.sb([128, MC, Tb], BF16, "oT")
            M.kTok = fw.sb([128, nt, cfg.MIXA], BF16, "kTok")
            M.vTok = fw.sb([128, nt, NHA, HDA + 1], BF16, "vTok")
            M.gr = fw.sb([128, RGB, Tb], BF16, "gr")
            M.XE = sum(r.ns * (3 + r.L) for r in B.regs)
            M.xr = fw.sb([128, RGB, M.XE], F32, "xr")
            M.ig = fw.sb([NHA, Tb], F32, "ig")
            M.lf = fw.sb([NHA, Tb], F32, "lf")
            xo = 0
            for r in B.regs:
                r.x0 = xo
                xo += r.ns * (3 + r.L)
            rhs = lambda kc: B.abuf[:, kc, :]
            sc = float(HDA) ** -0.5
            fw.memset("dve", M.vTok[:, :, :, HDA:HDA + 1], 1.0)
            for r in B.regs:
                if r.kind == "p":
                    fw.copy("dve", self.xrv(None, r, 0, 3), self.pRC.v)
                else:
                    for n in range(RGB):
                        src = self.din["rgc"][n * 128:(n + 1) * 128]
                        fw.dma("sp", self.xrv(n, r, 0, 3), src, join=True)
            def xev(m, ps):
                for r in B.regs:
                    fw.copy("act", self.xrv(m, r, 3, r.L), self.pv(ps[:, 0:Tb], r))
            self.proj_fm("wxr", rhs, Tb, xev)
            with fw.scope():
                g1 = fw.sb([128, Tb], F32, "g1")
                g2 = fw.sb([128, Tb], F32, "g2")

                def gev(m, ps):
                    fw.act(g1[:, 0:Tb], ps[:, 0:Tb], AF.Square)
                    fw.ts("dve", g1[:, 0:Tb], g1[:, 0:Tb], 0.044715, 1.0, ALU.mult, ALU.add)
                    fw.tt("dve", g1[:, 0:Tb], g1[:, 0:Tb], ps[:, 0:Tb], ALU.mult)
                    fw.act(g2[:, 0:Tb], g1[:, 0:Tb], AF.Sigmoid, scale=1.5957691216)
                    fw.tt("dve", M.gr[:, m, :], g2[:, 0:Tb], ps[:, 0:Tb], ALU.mult)
                self.proj_fm("wgr", rhs, Tb, gev)
            M.yr = fw.sb([128, RGB, Tb], BF16, "yr")
            rg = self.rglru(B, M)
            self._bg = rg
            self.proj_fm("wq", rhs, Tb, lambda m, ps: fw.copy("act", M.qT[:, m, :], ps[:, 0:Tb]))
            self.proj_fm_tok("wk", B, lambda m, ps: fw.act(M.kT[:, m, :], ps[:, 0:Tb], AF.Copy, scale=sc),
                             lambda ti, L, col0, w, ps: fw.act(M.kTok[0:L, ti, col0:col0 + w], ps[0:L, 0:w], AF.Copy, scale=sc))
            def vev(ti, L, col0, w, ps):
                c = col0
                while c < col0 + w:
                    h, dv = c // HDA, c % HDA
                    ww = min(HDA - dv, col0 + w - c)
                    fw.copy("act", M.vTok[0:L, ti, h, dv:dv + ww], ps[0:L, c - col0:c - col0 + ww])
                    c += ww
            self.proj_fm_tok("wv", B, None, vev)
            self.proj_fm("wo", rhs, Tb, lambda m, ps: fw.act(M.oT[:, m, :], ps[:, 0:Tb], AF.Sigmoid))
            for nm, dst in (("wig", M.ig), ("wfg", M.lf)):
                W = self.wpanel(nm, 0)
                ps = self.psum()
                for kc in range(NC):
                    fw.mm(ps[0:NHA, 0:Tb], W[:, kc, 0:NHA], rhs(kc), start=(kc == 0), stop=(kc == NC - 1))
                if nm == "wig":
                    fw.act(dst[:, 0:Tb], ps[0:NHA, 0:Tb], AF.Identity, bias=self.BIF[:, 0:1])
                else:
                    fw.act(dst[:, 0:Tb], ps[0:NHA, 0:Tb], AF.Exp, bias=self.BIF[:, 2:3], scale=-1.0)
                    fw.act(dst[:, 0:Tb], dst[:, 0:Tb], AF.Ln, bias=1.0)
                    fw.ts("dve", dst[:, 0:Tb], dst[:, 0:Tb], -1.0, None, ALU.mult)
            self._bg = None
            for _ in rg:
                pass
            self.dump("qT", M.qT.v, [128, MC, Tb])
            self.dump("kT", M.kT.v, [128, MC, Tb])
            self.dump("ig", M.ig.v, [NHA, Tb])
            self.dump("lf", M.lf.v, [NHA, Tb])
            self.chk("inproj")
            for ti, t in enumerate(B.tiles):
                self.mlstm_tile(B, M, ti, t)
                self.chk(f"mlstm{ti}")
            self.chk("mlstm")
            self.chk("rglru")
            self.dump("cat", B.abuf.v, [128, NC, Tb])
            self.proj_resid(B, "wout")

    def xrv(self, n, r, j0, w):
        M = self.M
        if n is None:
            v = M.xr[:, :, r.x0:r.x0 + r.ns * (3 + r.L)].re("p n (s l) -> p n s l", s=r.ns)
            return v[:, :, :, j0:j0 + w]
        v = M.xr[:, n, r.x0:r.x0 + r.ns * (3 + r.L)].re("p (s l) -> p s l", s=r.ns)
        return v[:, :, j0:j0 + w]

    def proj_fm_tok(self, wname, B, evac_fm, evac_tok):
        fw = self.fw
        npan, parts, KC, pw = self.wspec[wname]
        for pan in range(npan):
            W = self.wpanel(wname, pan)
            if evac_fm is not None:
                for mi in range(pw // 128):
                    ps = self.psum()
                    for kc in range(KC):
                        fw.mm(ps[:, 0:B.Tb], W[:, kc, mi * 128:(mi + 1) * 128], B.abuf[:, kc, :], start=(kc == 0), stop=(kc == KC - 1))
                    evac_fm(pan * (pw // 128) + mi, ps)
            for ti, t in enumerate(B.tiles):
                ps = self.psum()
                for kc in range(KC):
                    fw.mm(ps[0:t.L, 0:pw], B.abuf[:, kc, t.c0:t.c0 + t.L], W[:, kc, :], start=(kc == 0), stop=(kc == KC - 1))
                evac_tok(ti, t.L, pan * pw, pw, ps)
            self.bg_step()

    def mlstm_tile(self, B, M, ti, t):
        fw, cfg = self.fw, self.cfg
        NHA, HDA, HKC = cfg.NHA, cfg.HDA, cfg.HKC
        L, c0, ns, ls, kind = t.L, t.c0, t.ns, t.ls, t.kind
        cs = slice(c0, c0 + L)
        onesf = self.cc("ones")
        mprev = self.pM if kind == "p" else self.sM
        with fw.scope():
            G = fw.sb([NHA, 6, 128], F32, "G")
            Fv, gv, Mv, iv, ev, wv = (G[:, i, 0:L] for i in range(6))
            s3 = lambda v: v.re("h (s l) -> h s l", s=ns)
            fw.scan(Fv, self.cc(kind + "_rmask", rows=NHA, n=L), M.lf[:, cs], 0.0, ALU.mult, ALU.add)
            fw.tt("dve", gv, M.ig[:, cs], Fv, ALU.subtract)
            fw.scan(Mv, self.cc(kind + "_rbias", rows=NHA, n=L), gv, -1e30, ALU.add, ALU.max)
            fw.tt("dve", s3(Mv), s3(Mv), mprev[:, 0:ns].un(2).bc([NHA, ns, ls]), ALU.max)
            fw.tt("dve", s3(iv), s3(Mv), mprev[:, 0:ns].un(2).bc([NHA, ns, ls]), ALU.subtract)
            fw.act(iv, iv, AF.Exp, scale=-1.0)
            fw.tt("dve", ev, Fv, Mv, ALU.add)
            mnew = fw.sb([NHA, ns], F32, "mnew")
            fw.copy("dve", mnew.v, s3(ev)[:, :, ls - 1])
            fw.act(ev, ev, AF.Exp, scale=-1.0)
            fw.tt("dve", s3(wv), s3(gv), s3(Mv)[:, :, ls - 1:ls].bc([NHA, ns, ls]), ALU.subtract)
            fw.act(wv, wv, AF.Exp)
            pc = self.psum()
            identf = self.cc("ident", rows=NHA, n=NHA)
            fw.mm(pc[0:L, 0:NHA], gv, identf)
            fw.mm(pc[0:L, NHA:2 * NHA], wv, identf)
            cols = fw.sb([128, 2 * NHA], F32, "cols")
            fw.copy("act", cols[0:L, :], pc[0:L, 0:2 * NHA])
            BCs = fw.sb([128, 3, 128], F32, "BCs")
            DT = fw.sb([128, 128], F32, "DT")
            sTd = fw.sb([128, 128], BF16, "sTd")
            qTs = fw.sb([128, HKC, 128], BF16, "qTs")
            hT = fw.sb([128, HKC, 128], F32, "hT")
            sq = fw.sb([128, HKC, 128], F32, "hsq")
            dn = fw.sb([128, 128], F32, "dn")
            mu = fw.sb([128, 128], F32, "hmu")
            kw = fw.sb([128, HDA], BF16, "kw")
            wm = fw.sb([128, ns], F32, "wm")
            Cb = fw.sb([128, HKC, HDA], BF16, "Cb")
            nbc = fw.sb([128, HKC, 128], BF16, "nbc")
            Cs = [fw.sb([128, HKC, HDA + 1], F32, f"Cs{i}") for i in range(2)] if kind == "s" else None
            for h in range(NHA):
                pb = self.psum()
                fw.mm(pb[:, 0:3 * L].re("p (a l) -> p a l", a=3), self.SEL[:, h, :], G[:, 2:5, 0:L])
                fw.copy("act", BCs[:, :, 0:L], pb[:, 0:3 * L].re("p (a l) -> p a l", a=3))
                fw.stt(DT[0:L, 0:L], BCs[0:L, 0, 0:L], -1.0, self.cc(kind + "_maskb", rows=L, n=L), ALU.mult, ALU.add)
                fw.act(DT[0:L, 0:L], DT[0:L, 0:L], AF.Exp, bias=cols[0:L, h:h + 1])
                p2 = self.psum()
                for kc in range(HKC):
                    fw.mm(p2[0:L, 0:L], M.kT[:, h * HKC + kc, cs], M.qT[:, h * HKC + kc, cs], start=(kc == 0), stop=(kc == HKC - 1))
                fw.tt("dve", sTd[0:L, 0:L], p2[0:L, 0:L], DT[0:L, 0:L], ALU.mult)
                for kc in range(HKC):
                    fw.tt("dve", qTs[:, kc, 0:L], M.qT[:, h * HKC + kc, cs], BCs[:, 1, 0:L], ALU.mult)
                psn = [self.bank[4 + c] for c in range(HKC)]
                psd = self.bank[4 + HKC]
                for c in range(HKC):
                    fw.mm(psn[c][:, 0:L], M.vTok[0:L, ti, h, c * 128:(c + 1) * 128], sTd[0:L, 0:L], start=True, stop=False)
                fw.mm(psd[:, 0:L], self.onesb[0:L, :], sTd[0:L, 0:L], start=True, stop=False)
                fw.ts("dve", wm[0:L, 0:ns], self.cc(kind + "_rowmask", rows=L, n=ns), cols[0:L, NHA + h:NHA + h + 1], None, ALU.mult)
                for s in range(ns):
                    sc = slice(s * ls, (s + 1) * ls)
                    if kind == "s":
                        Cst = Cs[(h * ns + s) % 2]
                        fw.dma("sp", Cst.v, self.din["cext"][s, h].rearrange("(kc p) v -> p kc v", p=128))
                        Cv = Cst.v
                    else:
                        Cv = self.pC[:, h]
                    fw.copy("act", Cb.v, Cv[:, :, 0:HDA])
                    fw.copy("dve", nbc.v, Cv[:, :, HDA:HDA + 1].bc([128, HKC, 128]))
                    last = (s == ns - 1)
                    for c in range(HKC):
                        for kc in range(HKC):
                            fw.mm(psn[c][:, sc], Cb[:, kc, c * 128:(c + 1) * 128], qTs[:, kc, sc], start=False, stop=(last and kc == HKC - 1))
                    for kc in range(HKC):
                        fw.mm(psd[:, sc], nbc[:, kc, :], qTs[:, kc, sc], start=False, stop=(last and kc == HKC - 1))
                    fw.ts("dve", kw[0:L, :], M.kTok[0:L, ti, h * HDA:(h + 1) * HDA], wm[0:L, s:s + 1], None, ALU.mult)
                    dec = BCs[:, 1, (s + 1) * ls - 1:(s + 1) * ls]
                    for kc in range(HKC):
                        pu = self.psum()
                        fw.mm(pu[:, 0:HDA + 1], kw[0:L, kc * 128:(kc + 1) * 128], M.vTok[0:L, ti, h, :])
                        fw.stt(Cv[:, kc, :], Cv[:, kc, :], dec, pu[:, 0:HDA + 1], ALU.mult, ALU.add)
                    if kind == "s":
                        fw.dma("sp", self.dout["sc"][s, h].rearrange("(kc p) v -> p kc v", p=128), Cv)
                fw.act(dn[:, 0:L], psd[:, 0:L], AF.Abs)
                fw.tt("dve", dn[:, 0:L], dn[:, 0:L], BCs[:, 2, 0:L], ALU.max)
                fw.recip(dn[:, 0:L], dn[:, 0:L])
                for c in range(HKC):
                    fw.tt("dve", hT[:, c, 0:L], psn[c][:, 0:L], dn[:, 0:L], ALU.mult)
                p3 = self.psum()
                for c in range(HKC):
                    fw.mm(p3[:, 0:L], onesf, hT[:, c, 0:L], start=(c == 0), stop=(c == HKC - 1))
                fw.ts("dve", mu[:, 0:L], p3[:, 0:L], 1.0 / HDA, None, ALU.mult)
                p4 = self.psum()
                for c in range(HKC):
                    fw.tt("dve", hT[:, c, 0:L], hT[:, c, 0:L], mu[:, 0:L], ALU.subtract)
                    fw.act(sq[:, c, 0:L], hT[:, c, 0:L], AF.Square)
                    fw.mm(p4[:, 0:L], onesf, sq[:, c, 0:L], start=(c == 0), stop=(c == HKC - 1))
                fw.act(mu[:, 0:L], p4[:, 0:L], AF.Ln, bias=self.epsc("ln"), scale=1.0 / HDA)
                fw.act(mu[:, 0:L], mu[:, 0:L], AF.Exp, scale=-0.5)
                for c in range(HKC):
                    m = h * HKC + c
                    fw.tt("dve", hT[:, c, 0:L], hT[:, c, 0:L], mu[:, 0:L], ALU.mult)
                    fw.stt(B.abuf[:, m, cs], hT[:, c, 0:L], self.vc("mng", m), M.oT[:, m, cs], ALU.mult, ALU.mult)
            if kind == "p":
                fw.copy("dve", self.pM.v, mnew.v)
            else:
                fw.dma("sp", self.dout["sm"], mnew.v)

    def rglru(self, B, M):
        fw, cfg = self.fw, self.cfg
        RGB, Tb, MC = cfg.RGB, B.Tb, cfg.MC
        with fw.scope():
            Wring = self.wpanel("rgax", 0)
            Wax = fw.sb([128, RGB, 256], BF16, "Wax")
            fw.copy("dve", Wax.v, Wring)
            Wa = Wax[:, :, 0:128]
            Wx = Wax[:, :, 128:256]
            xc = fw.sb([128, Tb], F32, "xc")
            xcb = fw.sb([128, Tb], BF16, "xcb")
            ra = fw.sb([128, Tb], F32, "ra")
            gi = fw.sb([128, Tb], F32, "gi")
            aa = fw.sb([128, Tb], F32, "aa")
            uu = fw.sb([128, Tb], F32, "uu")
            hr = fw.sb([128, Tb], F32, "hr")
            sRH = None
            for r in B.regs:
                if r.kind == "s":
                    sRH = fw.sb([128, RGB, r.ns], F32, "sRH")
                    fw.dma("sp", sRH.v, self.din["rgh"].rearrange("(n p) s -> p n s", p=128))
                    sRHo = fw.sb([128, RGB, r.ns], F32, "sRHo")
            for n in range(RGB):
                for r in B.regs:
                    xv = self.pv(xc.v, r)
                    fw.ts("dve", xv, self.xrv(n, r, 0, r.L), self.vc("cw0", n), self.vc("cb", n), ALU.mult, ALU.add)
                    for j in range(1, 4):
                        fw.stt(xv, self.xrv(n, r, j, r.L), self.vc(f"cw{j}", n), xv, ALU.mult, ALU.add)
                fw.copy("act", xcb[:, 0:Tb], xc[:, 0:Tb])
                pa = self.psum()
                fw.mm(pa[:, 0:Tb], Wa[:, n, :], xcb[:, 0:Tb])
                fw.act(ra[:, 0:Tb], pa[:, 0:Tb], AF.Sigmoid, bias=self.vc("ba", n))
                px = self.psum()
                fw.mm(px[:, 0:Tb], Wx[:, n, :], xcb[:, 0:Tb])
                fw.act(gi[:, 0:Tb], px[:, 0:Tb], AF.Sigmoid, bias=self.vc("bx", n))
                fw.act(aa[:, 0:Tb], ra[:, 0:Tb], AF.Exp, scale=self.dv_c1(n))
                fw.act(uu[:, 0:Tb], ra[:, 0:Tb], AF.Exp, scale=self.dv_c2(n))
                fw.ts("dve", uu[:, 0:Tb], uu[:, 0:Tb], -1.0, 1.0, ALU.mult, ALU.add)
                fw.ts("dve", uu[:, 0:Tb], uu[:, 0:Tb], 1e-30, None, ALU.max)
                fw.act(uu[:, 0:Tb], uu[:, 0:Tb], AF.Sqrt)
                fw.tt("dve", gi[:, 0:Tb], gi[:, 0:Tb], xc[:, 0:Tb], ALU.mult)
                fw.tt("dve", uu[:, 0:Tb], uu[:, 0:Tb], gi[:, 0:Tb], ALU.mult)
                for r in B.regs:
                    if r.kind == "p":
                        c = slice(r.c0, r.c0 + r.L)
                        fw.scan(hr[:, c], aa[:, c], uu[:, c], self.pRH[:, n, :], ALU.mult, ALU.add)
                        fw.copy("dve", self.pRH[:, n, :], hr[:, r.c0 + r.L - 1:r.c0 + r.L])
                    else:
                        for s in range(r.ns):
                            c = slice(r.c0 + s * r.L, r.c0 + (s + 1) * r.L)
                            fw.scan(hr[:, c], aa[:, c], uu[:, c], sRH[:, n, s:s + 1], ALU.mult, ALU.add)
                        fw.copy("dve", sRHo[:, n, :], self.pv(hr.v, r)[:, :, r.L - 1])
                fw.tt("dve", M.yr[:, n, :], hr[:, 0:Tb], M.gr[:, n, :], ALU.mult)
                yield
                if n == 0:
                    self.dump("xc", xc.v, [128, Tb]); self.dump("ra", ra.v, [128, Tb]); self.dump("aa", aa.v, [128, Tb])
                    self.dump("uu", uu.v, [128, Tb]); self.dump("hr", hr.v, [128, Tb])
            for r in B.regs:
                if r.kind == "p":
                    fw.copy("dve", self.pRC.v, self.xrv(None, r, r.L, 3))
                else:
                    fw.dma("sp", self.dout["srgh"].rearrange("(n p) s -> p n s", p=128), sRHo.v)
                    for n in range(RGB):
                        fw.dma("sp", self.dout["srgc"][n * 128:(n + 1) * 128], self.xrv(n, r, r.L, 3), join=True)

    def layer1_mixer(self, B):
        fw, cfg = self.fw, self.cfg
        subs, cur, n = [], [], 0
        for t in B.tiles:
            if cur and (n + t.L > 256 or t.kind == "s" or cur[-1].kind == "s"):
                subs.append(cur)
                cur, n = [], 0
            cur.append(t)
            n += t.L
        subs.append(cur)
        for sub in subs:
            self.rwkv_sub(B, sub)
        self.proj_resid(B, "rwo")

    def rwkv_sub(self, B, sub):
        fw, cfg = self.fw, self.cfg
        NC, LW, LA, LG = cfg.NC, cfg.LW, cfg.LA, cfg.LG
        c0 = sub[0].c0
        n = sum(t.L for t in sub)
        kind = sub[0].kind
        reg = [r for r in B.regs if r.kind == kind][0]
        ns = reg.ns
        if kind == "p":
            off = c0 - reg.c0
            Ls = n
            xv = lambda kc, sh=0: B.hres[:, kc, reg.e0 + 1 + off + sh:reg.e0 + 1 + off + sh + n].re("p (s l) -> p s l", s=1)
        else:
            Ls = reg.L
            xv = lambda kc, sh=0: self.hv(kc, reg, sh)
        s3 = lambda v: v.re("p (s l) -> p s l", s=ns)
        CW = 0.6065306597126334
        with fw.scope():
            S = Blk()
            S.sg = fw.sb([128, NC, n], F32, "sg")
            S.a = fw.sb([128, NC, n], BF16, "a")
            S.k = fw.sb([128, NC, n], BF16, "k")
            S.r = fw.sb([128, NC, n], BF16, "r")
            S.v = fw.sb([128, NC, n], BF16, "v")
            S.g = fw.sb([128, NC, n], BF16, "g")
            with fw.scope():
                xsb = [fw.sb([128, NC, n], BF16, f"xs{i}") for i in range(2)]
                tmps = [fw.sb([128, n], F32, f"xstmp{i}") for i in range(2)]
                lo = fw.sb([128, LG // 128, n], BF16, "lora")

                def mk_xs(j, xs):
                    for kc in range(NC):
                        tmp = tmps[kc % 2]
                        fw.act(s3(tmp[:, 0:n]), xv(kc, -1), AF.Identity, scale=self.vc(f"mu{j}", kc))
                        fw.stt(s3(xs[:, kc, :]), xv(kc), self.dv_1mmu(j, kc), s3(tmp[:, 0:n]), ALU.mult, ALU.add)
                        if kc % 2 == 1:
                            yield

                def lora(xs, n1, n2, R, func1, evac2):
                    rhs = lambda kc: xs[:, kc, :]
                    W1 = self.wpanel(n1, 0)
                    for c in range((R + 127) // 128):
                        w = min(128, R - c * 128)
                        ps = self.psum()
                        for kc in range(NC):
                            fw.mm(ps[0:w, 0:n], W1[:, kc, c * 128:c * 128 + w], rhs(kc), start=(kc == 0), stop=(kc == NC - 1))
                        fw.act(lo[0:w, c, :], ps[0:w, 0:n], func1)
                        self.bg_step()
                    W2 = self.wpanel(n2, 0)
                    KC2 = (R + 127) // 128
                    for m in range(NC):
                        ps = self.psum()
                        for c in range(KC2):
                            w = min(128, R - c * 128)
                            fw.mm(ps[:, 0:n], W2[0:w, c, m * 128:(m + 1) * 128], lo[0:w, c, :], start=(c == 0), stop=(c == KC2 - 1))
                        evac2(m, ps)
                        if m % 2 == 1:
                            self.bg_step()
                jobs = [
                    (1, lambda xs: lora(xs, "w1", "w2", LW, AF.Tanh,
                                        lambda m, ps: fw.act(S.sg[:, m, :], ps[:, 0:n], AF.Sigmoid, bias=self.vc("w0", m)))),
                    (4, lambda xs: lora(xs, "a1", "a2", LA, AF.Copy,
                                        lambda m, ps: fw.act(S.a[:, m, :], ps[:, 0:n], AF.Sigmoid, bias=self.vc("a0", m)))),
                    (5, lambda xs: lora(xs, "g1", "g2", LG, AF.Sigmoid,
                                        lambda m, ps: fw.copy("act", S.g[:, m, :], ps[:, 0:n]))),
                ]
                for j_, wn, dst in ((2, "rwk", S.k), (0, "rwr", S.r), (3, "rwv", S.v)):
                    jobs.append((j_, lambda xs, wn=wn, dst=dst: self.proj_fm(
                        wn, lambda kc: xs[:, kc, :], n, lambda m, ps: fw.copy("act", dst[:, m, :], ps[:, 0:n]))))
                for _ in mk_xs(jobs[0][0], xsb[0]):
                    pass
                for idx, (j_, run) in enumerate(jobs):
                    if idx + 1 < len(jobs):
                        self._bg = mk_xs(jobs[idx + 1][0], xsb[(idx + 1) % 2])
                    run(xsb[idx % 2])
                    if self._bg is not None:
                        for _ in self._bg:
                            pass
                        self._bg = None
            self.chk("l1proj")
            self.dump("sg", S.sg.v, [128, NC, n])
            self.dump("rk", S.k.v, [128, NC, n])
            for m0 in range(0, NC, 2):
                self.rwkv_mpair(B, S, sub, m0, c0, n, CW)

    def rwkv_mpair(self, B, S, sub, m0, c0, n, CW):
        fw, cfg = self.fw, self.cfg
        NC = cfg.NC
        blkf = self.cc("blk")
        with fw.scope():
            FT = fw.sb([128, 2, 4, n], BF16, "FT")
            FTp = fw.sb([128, 2, 2, 2, n], BF16, "FTp")
            self.FTp = FTp
            Ep = fw.sb([128, 2, n], F32, "Ep")
            bonus = fw.sb([128, 2, n], F32, "bonus")
            with fw.scope():
                cs = fw.sb([128, n], F32, "cs")
                Em = fw.sb([128, n], F32, "Em")
                Epm = fw.sb([128, n], F32, "Epm")
                kkp = fw.sb([128, n], F32, "kkp")
                sq = fw.sb([128, n], F32, "sq")
                rn = fw.sb([128, n], F32, "rn")
                tt_ = fw.sb([128, n], F32, "tt")
                km = fw.sb([128, n], F32, "km")
                for ml in range(2):
                    m = m0 + ml
                    sg = S.sg[:, m, :]
                    for t in sub:
                        tc = slice(t.c0 - c0, t.c0 - c0 + t.L)
                        fw.scan(cs[:, tc], self.cc(t.kind + "_rmask", n=t.L), sg[:, tc], 0.0, ALU.mult, ALU.add)
                    fw.act(Ep[:, ml, :], cs[:, 0:n], AF.Exp, scale=-CW)
                    fw.act(Em[:, 0:n], cs[:, 0:n], AF.Exp, scale=CW)
                    fw.tt("dve", Epm[:, 0:n], cs[:, 0:n], sg, ALU.subtract)
                    fw.act(Epm[:, 0:n], Epm[:, 0:n], AF.Exp, scale=-CW)
                    fw.ts("dve", kkp[:, 0:n], S.k[:, m, :], self.vc("kk", m), None, ALU.mult)
                    fw.act(sq[:, 0:n], kkp[:, 0:n], AF.Square)
                    ps = self.psum()
                    fw.mm(ps[:, 0:n], blkf, sq[:, 0:n])
                    fw.ts("dve", rn[:, 0:n], ps[:, 0:n], 1e-18, None, ALU.max)
                    fw.act(rn[:, 0:n], rn[:, 0:n], AF.Ln)
                    fw.act(rn[:, 0:n], rn[:, 0:n], AF.Exp, scale=-0.5)
                    fw.tt("dve", kkp[:, 0:n], kkp[:, 0:n], rn[:, 0:n], ALU.mult)
                    fw.ts("dve", tt_[:, 0:n], S.a[:, m, :], self.vc("ka", m), self.dv_1mka(m), ALU.mult, ALU.add)
                    fw.tt("dve", km[:, 0:n], tt_[:, 0:n], S.k[:, m, :], ALU.mult)
                    fw.stt(FT[:, ml, 0, :], kkp[:, 0:n], -1.0, Epm[:, 0:n], ALU.mult, ALU.mult)
                    fw.tt("dve", FT[:, ml, 1, :], S.r[:, m, :], Ep[:, ml, :], ALU.mult)
                    fw.tt("dve", tt_[:, 0:n], kkp[:, 0:n], S.a[:, m, :], ALU.mult)
                    fw.tt("dve", FT[:, ml, 2, :], tt_[:, 0:n], Em[:, 0:n], ALU.mult)
                    fw.tt("dve", FT[:, ml, 3, :], km[:, 0:n], Em[:, 0:n], ALU.mult)
                    for h2 in range(2):
                        for i_, src_ in enumerate((2, 3)):
                            fw.act(FTp[:, ml, h2, i_, :], FT[:, ml, src_, :], AF.Identity, scale=self.cc("hm")[:, h2:h2 + 1])
                    fw.tt("dve", tt_[:, 0:n], km[:, 0:n], S.r[:, m, :], ALU.mult)
                    fw.ts("dve", sq[:, 0:n], tt_[:, 0:n], self.vc("rk", m), None, ALU.mult)
                    ps = self.psum()
                    fw.mm(ps[:, 0:n], blkf, sq[:, 0:n])
                    fw.tt("dve", bonus[:, ml, :], ps[:, 0:n], S.v[:, m, :], ALU.mult)
            self.chk("l1prep")
            for t in sub:
                with fw.scope():
                    gens = [self.rwkv_chain(B, S, FT, Ep, bonus, t, m0, ml, c0) for ml in range(2)]
                    live = list(gens)
                    while live:
                        for g in list(live):
                            try:
                                next(g)
                            except StopIteration:
                                live.remove(g)
                self.chk("l1tile")

    def rwkv_tile(self, B, S, FT, Ep, bonus, t, m0, c0):
        fw, cfg = self.fw, self.cfg
        L, ns, ls, kind, nsteps = t.L, t.ns, t.ls, t.kind, t.nsteps
        tc = slice(t.c0 - c0, t.c0 - c0 + L)
        bc = slice(t.c0, t.c0 + L)
        hm = self.cc("hm")
        blkf = self.cc("blk")
        psD = self.psD
        bankbf = lambda b: V(b, b.ap.bitcast(BF16))
        with fw.scope():
            tok4 = fw.sb([128, 2, 4, 128], BF16, "tok4")
            BKp = fw.sb([128, 4, 2, 128], BF16, "BKp")
            AT4 = fw.sb([128, 4, 4, 128], BF16, "AT4")
            PX = [fw.sb([128, 4, 256], BF16, f"PX{i}") for i in range(2)]
            Qb = [fw.sb([128, 4, 128], BF16, f"Qb{i}") for i in range(2)]
            X32 = fw.sb([128, 4, 128], F32, "X32")
            AH = fw.sb([128, 4, 64], BF16, "AH")
            ATp = fw.sb([128, 4, 128], BF16, "ATp")
            Ubf = fw.sb([128, 4, 128], BF16, "UV")
            Hbf = fw.sb([128, 2, ns, 64], BF16, "Hbf")
            Hpad = fw.sb([128, 4, ns, 64], BF16, "Hpad")
            yt = fw.sb([128, 128], F32, "yt")
            ysq = fw.sb([128, 128], F32, "ysq")
            yr = fw.sb([128, 128], F32, "yr")
            if kind == "s":
                H32t = fw.sb([128, 2, ns, 64], F32, "H32s")
                fw.dma("sp", H32t.v, self.din["rwS"][m0:m0 + 2].rearrange("m p s v -> p m s v"))
                H32 = H32t.v
                AM = [fw.sb([128, ns, 128], BF16, f"AM{i}") for i in range(2)]
                UVm = [fw.sb([128, ns, 128], BF16, f"UVm{i}") for i in range(2)]
            else:
                H32 = self.pH[:, m0:m0 + 2]
            fw.copy("act", Hbf.v, H32)
            for hh in range(4):
                fw.act(Hpad[:, hh], H32[:, hh // 2], AF.Identity, scale=hm[:, hh % 2:hh % 2 + 1])
            for ml in range(2):
                pb = bankbf(self.psum())
                for i, src in enumerate((FT[:, ml, 0, tc], FT[:, ml, 2, tc], FT[:, ml, 3, tc], S.v[:, m0 + ml, tc])):
                    fw.tr(pb[0:L, i * 128:(i + 1) * 128], src, self.identb.v)
                fw.copy("act", tok4[0:L, ml], pb[0:L, 0:512].re("p (a k) -> p a k", a=4))
            self.chk("t_tr")
            for hh in range(4):
                ml, h2 = hh // 2, hh % 2
                for i, src in enumerate((2, 3)):
                    fw.act(BKp[:, hh, i, 0:L], FT[:, ml, src, tc], AF.Identity, scale=hm[:, h2:h2 + 1])
            self.chk("t_pad")
            pP = self.psum()
            for hh in range(4):
                ml = hh // 2
                bk = self.bank[4 + hh]
                rhsAR = FT[:, ml, 0:2, tc]
                fw.mm(bk[0:L, 0:2 * L].re("p (a l) -> p a l", a=2), BKp[:, hh, 0, 0:L], rhsAR)
                fw.mm(bk[0:L, 2 * L:4 * L].re("p (a l) -> p a l", a=2), BKp[:, hh, 1, 0:L], rhsAR)
                fw.mm(pP[0:L, hh * 128:hh * 128 + L], FT[:, ml, 0, tc], BKp[:, hh, 0, 0:L])
            m2 = self.cc(kind + "_m2").re("p (a l) -> p a l", a=2)[0:L, :, 0:L]
            for rep in range(2):
                src = psD[0:L, :, rep * 2 * L:(rep + 1) * 2 * L].re("p h (a l) -> p h a l", a=2)
                fw.tt("dve", AT4[0:L, :, 2 * rep:2 * rep + 2, 0:L], src, m2.un(1).bc([L, 4, 2, L]), ALU.mult)
            fw.tt("dve", PX[0][0:L, :, 0:L], pP[0:L, 0:512].re("p (h l) -> p h l", h=4)[:, :, 0:L],
                  self.cc(kind + "_strictT", rows=L, n=L).un(1).bc([L, 4, L]), ALU.mult)
            self.chk("t_A")
            pZ = self.psum()
            for hh in range(4):
                ml, h2 = hh // 2, hh % 2
                fw.mm(pZ[0:L, hh * 64:(hh + 1) * 64], AT4[0:L, hh, 2, 0:L], tok4[0:L, ml, 3, h2 * 64:(h2 + 1) * 64])
            fw.copy("act", X32[0:L, :, 64:128], pZ[0:L, 0:256].re("p (h v) -> p h v", h=4))
            for ml in range(2):
                fw.copy("dve", X32[0:L, 2 * ml:2 * ml + 2, 0:64], tok4[0:L, ml, 0, :].re("p (h k) -> p h k", h=2))
            fw.copy("act", PX[0][0:L, :, L:L + 128], X32[0:L])
            self.chk("t_Z")
            for i in range(nsteps):
                cur, nxt = i % 2, (i + 1) % 2
                last = (i == nsteps - 1)
                for hh in range(4):
                    Qi = AT4[0:L, hh, 0, 0:L] if i == 0 else Qb[cur][0:L, hh, 0:L]
                    bk = self.bank[4 + hh]
                    if not last:
                        fw.mm(bk[0:L, 0:L + 128], Qi, PX[cur][0:L, hh, 0:L + 128])
                        fw.mm(bk[0:L, 256:256 + L], PX[cur][0:L, hh, 0:L], Qi)
                    else:
                        fw.mm(bk[0:L, L:L + 128], Qi, PX[cur][0:L, hh, L:L + 128])
                self.chk(f"d_mm{i}")
                for half in range(2):
                    hs = slice(2 * half, 2 * half + 2)
                    pv_ = V(self.bank[4 + 2 * half], psD.ap[0:L, hs, :], ts=self.bank[4 + 2 * half:6 + 2 * half])
                    if not last:
                        for hh in (2 * half, 2 * half + 1):
                            qsrc = self.bank[4 + hh][0:L, 256:256 + L]
                            fw.op("act", lambda: self.nc.scalar.copy(Qb[nxt][0:L, hh, 0:L].ap, qsrc.ap), [qsrc], [Qb[nxt][0:L, hh, 0:L], qsrc])
                        fw.copy("dve", PX[nxt][0:L, hs, 0:L], pv_[:, :, 0:L])
                        fw.tt("dve", PX[nxt][0:L, hs, L:L + 128], X32[0:L, hs, :], pv_[:, :, L:L + 128], ALU.add)
                    fw.tt("dve", X32[0:L, hs, :], X32[0:L, hs, :], pv_[:, :, L:L + 128], ALU.add)
                self.chk(f"d_end{i}")
            self.chk("t_dbl")
            fw.copy("dve", AH[0:L], X32[0:L, :, 0:64])
            for ml in range(2):
                pb = bankbf(self.psum())
                fw.tr(pb[:, 0:L], AH[0:L, 2 * ml:2 * ml + 2, :].re("p h k -> p (h k)"), self.identb[0:L, 0:L])
                for h2 in range(2):
                    fw.ts("dve", ATp[:, 2 * ml + h2, 0:L], pb[:, 0:L], hm[:, h2:h2 + 1], None, ALU.mult)
            self.chk("t_AT")
            pU = self.psum()
            for hh in range(4):
                ml = hh // 2
                if ns == 1:
                    fw.mm(pU[0:L, hh * 64:(hh + 1) * 64], ATp[:, hh, 0:L], Hbf[:, ml, 0, :])
                else:
                    am = AM[hh % 2]
                    fw.tt("dve", am[:, :, 0:L], ATp[:, hh, 0:L].un(1).bc([128, ns, L]),
                          self.colmb.v.re("p (s t) -> p s t", s=ns)[:, :, 0:L], ALU.mult)
                    for s in range(ns):
                        fw.mm(pU[0:L, hh * 64:(hh + 1) * 64], am[:, s, 0:L], Hbf[:, ml, s, :], start=(s == 0), stop=(s == ns - 1))
            fw.tt("dve", Ubf[0:L, :, 0:64], pU[0:L, 0:256].re("p (h v) -> p h v", h=4), X32[0:L, :, 64:128], ALU.add)
            for ml in range(2):
                fw.copy("act", Ubf[0:L, 2 * ml:2 * ml + 2, 64:128], tok4[0:L, ml, 3, :].re("p (h v) -> p h v", h=2))
            self.chk("t_U")
            pY = [self.psum(), self.psum()]
            for hh in range(4):
                ml, h2 = hh // 2, hh % 2
                out = pY[ml][64 * h2:64 * h2 + 64, 0:L]
                fw.mm(out, Ubf[0:L, hh, 0:64], AT4[0:L, hh, 1, 0:L], start=True, stop=False)
                fw.mm(out, Ubf[0:L, hh, 64:128], AT4[0:L, hh, 3, 0:L], start=False, stop=False)
                for s in range(ns):
                    sc = slice(s * ls, (s + 1) * ls)
                    fw.mm(pY[ml][64 * h2:64 * h2 + 64, sc], Hpad[:, hh, s, :], FT[:, ml, 1, tc][:, sc], start=False, stop=(s == ns - 1))
            self.chk("t_Y")
            for hh in range(4):
                ml, h2 = hh // 2, hh % 2
                if ns > 1:
                    uvm = UVm[hh % 2]
                    fw.tt("dve", uvm[0:L], Ubf[0:L, hh, :].un(1).bc([L, ns, 128]),
                          self.cc(kind + "_rowmask", rows=L, n=ns).un(2).bc([L, ns, 128]), ALU.mult)
                for s in range(ns):
                    slot = ml * ns + s
                    bk = self.bank[4 + slot // 8]
                    out = bk[64 * h2:64 * h2 + 64, (slot % 8) * 64:(slot % 8 + 1) * 64]
                    rU = Ubf[0:L, hh, 0:64] if ns == 1 else uvm[0:L, s, 0:64]
                    rV = Ubf[0:L, hh, 64:128] if ns == 1 else uvm[0:L, s, 64:128]
                    fw.mm(out, tok4[0:L, ml, 1, h2 * 64:(h2 + 1) * 64], rU, start=True, stop=False)
                    fw.mm(out, tok4[0:L, ml, 2, h2 * 64:(h2 + 1) * 64], rV, start=False, stop=True)
            nslot = 2 * ns
            for b in range((nslot + 7) // 8):
                w = min(8, nslot - b * 8)
                hv_ = H32.re("p m s v -> p (m s) v")[:, b * 8:b * 8 + w, :]
                fw.tt("dve", hv_, hv_, self.bank[4 + b][:, 0:w * 64].re("p (a v) -> p a v", a=w), ALU.add)
            for ml in range(2):
                wl = Ep[:, ml, tc].re("p (s l) -> p s l", s=ns)[:, :, ls - 1:ls].bc([128, ns, 64])
                fw.tt("dve", H32[:, ml], H32[:, ml], wl, ALU.mult)
            if kind == "s":
                fw.dma("sp", self.dout["srwS"][m0:m0 + 2].rearrange("m p s v -> p m s v"), H32)
            self.chk("t_H")
            for ml in range(2):
                m = m0 + ml
                fw.copy("act", yt[:, 0:L], pY[ml][:, 0:L])
                p1 = self.psum()
                fw.mm(p1[:, 0:L], blkf, yt[:, 0:L])
                fw.stt(yt[:, 0:L], p1[:, 0:L], -1.0 / 64, yt[:, 0:L], ALU.mult, ALU.add)
                fw.act(ysq[:, 0:L], yt[:, 0:L], AF.Square)
                p2 = self.psum()
                fw.mm(p2[:, 0:L], blkf, ysq[:, 0:L])
                fw.act(yr[:, 0:L], p2[:, 0:L], AF.Sqrt, bias=self.epsc("gn"), scale=1.0 / 64)
                fw.recip(yr[:, 0:L], yr[:, 0:L])
                fw.tt("dve", yt[:, 0:L], yt[:, 0:L], yr[:, 0:L], ALU.mult)
                fw.ts("dve", yt[:, 0:L], yt[:, 0:L], self.vc("lxg", m), self.vc("lxb", m), ALU.mult, ALU.add)
                fw.tt("dve", yt[:, 0:L], yt[:, 0:L], bonus[:, ml, tc], ALU.add)
                fw.tt("dve", B.abuf[:, m, bc], yt[:, 0:L], S.g[:, m, tc], ALU.mult)


    def rwkv_chain(self, B, S, FT, Ep, bonus, t, m0, ml, c0):
        fw, cfg = self.fw, self.cfg
        L, ns, ls, kind, nsteps = t.L, t.ns, t.ls, t.kind, t.nsteps
        tc = slice(t.c0 - c0, t.c0 - c0 + L)
        bc = slice(t.c0, t.c0 + L)
        hm = self.cc("hm")
        blkf = self.cc("blk")
        m = m0 + ml
        D0, D1 = self.bank[4 + 2 * ml], self.bank[5 + 2 * ml]
        Db = (D0, D1)
        psD2 = V(D0, self.psD.ap[:, 2 * ml:2 * ml + 2, :], ts=[D0, D1])
        pool = (self.bank[2 * ml], self.bank[2 * ml + 1])
        pi = [0]

        def nextp():
            pi[0] += 1
            return pool[pi[0] % 2]
        bankbf = lambda b: V(b, b.ap.bitcast(BF16))
        if True:
            tok4 = fw.sb([128, 4, 128], BF16, "tok4")
            BKp = self.FTp[:, ml, :, :, tc]
            AT4 = fw.sb([128, 2, 4, 128], BF16, "AT4")
            PX = [fw.sb([128, 2, 256], BF16, f"PX{i}") for i in range(2)]
            Qb = [fw.sb([128, 2, 128], BF16, f"Qb{i}") for i in range(2)]
            X32 = fw.sb([128, 2, 128], F32, "X32")
            AH = fw.sb([128, 2, 64], BF16, "AH")
            ATp = fw.sb([128, 2, 128], BF16, "ATp")
            Ubf = fw.sb([128, 2, 128], BF16, "UV")
            Hbf = fw.sb([128, ns, 64], BF16, "Hbf")
            Hpad = fw.sb([128, 2, ns, 64], BF16, "Hpad")
            yt = fw.sb([128, 128], F32, "yt")
            ysq = fw.sb([128, 128], F32, "ysq")
            yr = fw.sb([128, 128], F32, "yr")
            if kind == "s":
                H32t = fw.sb([128, ns, 64], F32, "H32s")
                fw.dma("sp", H32t.v, self.din["rwS"][m].rearrange("p s v -> p s v"))
                H32 = H32t.v
                AM = fw.sb([128, ns, 128], BF16, "AM")
                UVm = fw.sb([128, ns, 128], BF16, "UVm")
            else:
                H32 = self.pH[:, m]
            fw.copy("act", Hbf.v, H32)
            for h2 in range(2):
                fw.act(Hpad[:, h2], H32, AF.Identity, scale=hm[:, h2:h2 + 1])
            pb = bankbf(nextp())
            for i, src in enumerate((FT[:, ml, 0, tc], FT[:, ml, 2, tc], FT[:, ml, 3, tc], S.v[:, m, tc])):
                fw.tr(pb[0:L, i * 128:(i + 1) * 128], src, self.identb.v)
            fw.copy("act", tok4[0:L], pb[0:L, 0:512].re("p (a k) -> p a k", a=4))
            yield
            pP = nextp()
            rhsAR = FT[:, ml, 0:2, tc]
            for h2 in range(2):
                bk = Db[h2]
                fw.mm(bk[0:L, 0:2 * L].re("p (a l) -> p a l", a=2), BKp[:, h2, 0, 0:L], rhsAR)
                fw.mm(bk[0:L, 2 * L:4 * L].re("p (a l) -> p a l", a=2), BKp[:, h2, 1, 0:L], rhsAR)
                fw.mm(pP[0:L, h2 * 128:h2 * 128 + L], FT[:, ml, 0, tc], BKp[:, h2, 0, 0:L])
            m2 = self.cc(kind + "_m2").re("p (a l) -> p a l", a=2)[0:L, :, 0:L]
            for rep in range(2):
                src = psD2[0:L, :, rep * 2 * L:(rep + 1) * 2 * L].re("p h (a l) -> p h a l", a=2)
                fw.tt("dve", AT4[0:L, :, 2 * rep:2 * rep + 2, 0:L], src, m2.un(1).bc([L, 2, 2, L]), ALU.mult)
            fw.tt("dve", PX[0][0:L, :, 0:L], pP[0:L, 0:256].re("p (h l) -> p h l", h=2)[:, :, 0:L],
                  self.cc(kind + "_strictT", rows=L, n=L).un(1).bc([L, 2, L]), ALU.mult)
            yield
            pZ = nextp()
            for h2 in range(2):
                fw.mm(pZ[0:L, h2 * 64:(h2 + 1) * 64], AT4[0:L, h2, 2, 0:L], tok4[0:L, 3, h2 * 64:(h2 + 1) * 64])
            fw.copy("act", X32[0:L, :, 64:128], pZ[0:L, 0:128].re("p (h v) -> p h v", h=2))
            fw.copy("dve", X32[0:L, :, 0:64], tok4[0:L, 0, :].re("p (h k) -> p h k", h=2))
            fw.copy("act", PX[0][0:L, :, L:L + 128], X32[0:L])
            yield
            for i in range(nsteps):
                cur, nxt = i % 2, (i + 1) % 2
                last = (i == nsteps - 1)
                for h2 in range(2):
                    Qi = AT4[0:L, h2, 0, 0:L] if i == 0 else Qb[cur][0:L, h2, 0:L]
                    bk = Db[h2]
                    if not last:
                        fw.mm(bk[0:L, 0:L + 128], Qi, PX[cur][0:L, h2, 0:L + 128])
                        fw.mm(bk[0:L, 256:256 + L], PX[cur][0:L, h2, 0:L], Qi)
                    else:
                        fw.mm(bk[0:L, L:L + 128], Qi, PX[cur][0:L, h2, L:L + 128])
                if not last:
                    for h2 in range(2):
                        qsrc = Db[h2][0:L, 256:256 + L]
                        qdst = Qb[nxt][0:L, h2, 0:L]
                        fw.op("act", lambda qdst=qdst, qsrc=qsrc: self.nc.scalar.copy(qdst.ap, qsrc.ap), [qsrc], [qdst, qsrc])
                    fw.copy("dve", PX[nxt][0:L, :, 0:L], psD2[0:L, :, 0:L])
                    fw.tt("dve", PX[nxt][0:L, :, L:L + 128], PX[cur][0:L, :, L:L + 128], psD2[0:L, :, L:L + 128], ALU.add)
                else:
                    fw.tt("dve", X32[0:L], PX[cur][0:L, :, L:L + 128], psD2[0:L, :, L:L + 128], ALU.add)
                yield
            fw.copy("dve", AH[0:L], X32[0:L, :, 0:64])
            pb = bankbf(nextp())
            fw.tr(pb[:, 0:L], AH[0:L].re("p h k -> p (h k)"), self.identb[0:L, 0:L])
            for h2 in range(2):
                fw.act(ATp[:, h2, 0:L], pb[:, 0:L], AF.Identity, scale=hm[:, h2:h2 + 1])
            yield
            pU = nextp()
            for h2 in range(2):
                if ns == 1:
                    fw.mm(pU[0:L, h2 * 64:(h2 + 1) * 64], ATp[:, h2, 0:L], Hbf[:, 0, :])
                else:
                    fw.tt("dve", AM[:, :, 0:L], ATp[:, h2, 0:L].un(1).bc([128, ns, L]),
                          self.colmb.v.re("p (s t) -> p s t", s=ns)[:, :, 0:L], ALU.mult)
                    for s in range(ns):
                        fw.mm(pU[0:L, h2 * 64:(h2 + 1) * 64], AM[:, s, 0:L], Hbf[:, s, :], start=(s == 0), stop=(s == ns - 1))
            fw.tt("dve", Ubf[0:L, :, 0:64], pU[0:L, 0:128].re("p (h v) -> p h v", h=2), X32[0:L, :, 64:128], ALU.add)
            fw.copy("act", Ubf[0:L, :, 64:128], tok4[0:L, 3, :].re("p (h v) -> p h v", h=2))
            yield
            pY = nextp()
            for h2 in range(2):
                out = pY[64 * h2:64 * h2 + 64, 0:L]
                fw.mm(out, Ubf[0:L, h2, 0:64], AT4[0:L, h2, 1, 0:L], start=True, stop=False)
                fw.mm(out, Ubf[0:L, h2, 64:128], AT4[0:L, h2, 3, 0:L], start=False, stop=False)
                for s in range(ns):
                    sc = slice(s * ls, (s + 1) * ls)
                    fw.mm(pY[64 * h2:64 * h2 + 64, sc], Hpad[:, h2, s, :], FT[:, ml, 1, tc][:, sc], start=False, stop=(s == ns - 1))
            yield
            for h2 in range(2):
                if ns > 1:
                    fw.tt("dve", UVm[0:L], Ubf[0:L, h2, :].un(1).bc([L, ns, 128]),
                          self.cc(kind + "_rowmask", rows=L, n=ns).un(2).bc([L, ns, 128]), ALU.mult)
                for s in range(ns):
                    bk = Db[s // 8]
                    out = bk[64 * h2:64 * h2 + 64, (s % 8) * 64:(s % 8 + 1) * 64]
                    rU = Ubf[0:L, h2, 0:64] if ns == 1 else UVm[0:L, s, 0:64]
                    rV = Ubf[0:L, h2, 64:128] if ns == 1 else UVm[0:L, s, 64:128]
                    fw.mm(out, tok4[0:L, 1, h2 * 64:(h2 + 1) * 64], rU, start=True, stop=False)
                    fw.mm(out, tok4[0:L, 2, h2 * 64:(h2 + 1) * 64], rV, start=False, stop=True)
            for b in range((ns + 7) // 8):
                w = min(8, ns - b * 8)
                hv_ = H32[:, b * 8:b * 8 + w, :]
                fw.tt("dve", hv_, hv_, Db[b][:, 0:w * 64].re("p (a v) -> p a v", a=w), ALU.add)
            wl = Ep[:, ml, tc].re("p (s l) -> p s l", s=ns)[:, :, ls - 1:ls].bc([128, ns, 64])
            fw.tt("dve", H32, H32, wl, ALU.mult)
            if kind == "s":
                fw.dma("sp", self.dout["srwS"][m].rearrange("p s v -> p s v"), H32)
            yield
            fw.copy("act", yt[:, 0:L], pY[:, 0:L])
            p1 = nextp()
            fw.mm(p1[:, 0:L], blkf, yt[:, 0:L])
            fw.stt(yt[:, 0:L], p1[:, 0:L], -1.0 / 64, yt[:, 0:L], ALU.mult, ALU.add)
            fw.act(ysq[:, 0:L], yt[:, 0:L], AF.Square)
            p2 = nextp()
            fw.mm(p2[:, 0:L], blkf, ysq[:, 0:L])
            fw.act(yr[:, 0:L], p2[:, 0:L], AF.Ln, bias=self.epsc("gn"), scale=1.0 / 64)
            fw.act(yr[:, 0:L], yr[:, 0:L], AF.Exp, scale=-0.5)
            fw.tt("dve", yt[:, 0:L], yt[:, 0:L], yr[:, 0:L], ALU.mult)
            fw.act(yt[:, 0:L], yt[:, 0:L], AF.Identity, bias=self.vc("lxb", m), scale=self.vc("lxg", m))
            fw.tt("dve", yt[:, 0:L], yt[:, 0:L], bonus[:, ml, tc], ALU.add)
            fw.tt("dve", B.abuf[:, m, bc], yt[:, 0:L], S.g[:, m, tc], ALU.mult)
            yield


def build_program(cfg, debug=()):
    nc0 = bass.Bass("TRN2", target_bir_lowering=False)
    p0 = Prog(nc0, cfg, plan=None, debug=debug)
    with nc0.allow_non_contiguous_dma(reason="small strided state/vector transfers"):
        p0.build()
    nc = bass.Bass("TRN2", target_bir_lowering=False)
    p = Prog(nc, cfg, plan=p0.plan, debug=debug)
    with nc.allow_non_contiguous_dma(reason="small strided state/vector transfers"):
        p.build()
    return nc, p


def prep_shared(cfg, P):
    cst, _, colm = make_consts(cfg)
    m = {"cst": cst, "colm": colm, "vec": make_vecs(cfg, P)}
    m.update(host_weights(cfg, P))
    bif = np.asarray(P["b_if_ab"][0], np.float32)
    m["bif"] = np.ascontiguousarray(np.stack([bif[:cfg.NHA], bif[cfg.NHA:]], 1))
    return m


def prep_core(cfg, P, xp_seq, sl):
    D, NS, NC = cfg.D, cfg.NS, cfg.NC
    f = lambda a: np.asarray(a, np.float32)
    m = {}
    xfull = np.concatenate([f(P["meta_tokens"]), f(xp_seq)], 0)
    m["xp"] = np.ascontiguousarray(xfull.T)
    xs = f(P["x_sample"])[sl]
    m["xs"] = np.ascontiguousarray(xs.reshape(NS * cfg.LS, D).T)
    C = f(P["state_mlstm_C"])[0, sl]
    n = f(P["state_mlstm_n"])[0, sl]
    m["cext"] = np.ascontiguousarray(np.concatenate([C, n[..., None]], -1))
    m["m0"] = np.ascontiguousarray(f(P["state_mlstm_m"])[0, sl].T)
    m["rgh"] = np.ascontiguousarray(f(P["state_rglru_h"])[0, sl].T)
    m["rgc"] = np.ascontiguousarray(f(P["state_rglru_conv"])[0, sl].transpose(2, 0, 1))
    S = f(P["state_rwkv_S"])[0, sl]
    H = S.transpose(0, 1, 3, 2).reshape(NS, NC, 2, 64, 64)
    m["rwS"] = np.ascontiguousarray(H.transpose(1, 2, 3, 0, 4).reshape(NC, 128, NS, 64))
    m["rwx"] = np.ascontiguousarray(f(P["state_rwkv_shift"])[0, sl].T)
    return m


def unpack_core(cfg, r):
    NS, NC, NHA, HDA = cfg.NS, cfg.NC, cfg.NHA, cfg.HDA
    o = {}
    o["yp"] = r["yp"].T[16:]
    o["ys"] = r["ys"].T.reshape(NS, cfg.LS, cfg.D)
    for pre, nb in (("p", 1), ("s", NS)):
        c = r[pre + "c"]
        o[pre + "C"] = c[..., :HDA]
        o[pre + "n"] = c[..., HDA]
        o[pre + "m"] = r[pre + "m"].T
        o[pre + "rgh"] = r[pre + "rgh"].T
        o[pre + "rgc"] = r[pre + "rgc"].transpose(1, 2, 0)
        H = r[pre + "rwS"].reshape(NC, 2, 64, nb, 64)
        o[pre + "S"] = H.transpose(3, 0, 1, 4, 2).reshape(nb, NC * 2, 64, 64)
        o[pre + "x"] = r[pre + "rwx"].T
    return o


_CACHE = {}


def kernel(**inputs):
    cfg = Cfg()
    P = inputs
    if "prog" not in _CACHE:
        _CACHE["prog"] = build_program(cfg)
    nc, prog = _CACHE["prog"]
    shared = prep_shared(cfg, P)
    in_maps = []
    for c in range(8):
        m = dict(shared)
        m.update(prep_core(cfg, P, np.asarray(P["x_prompt"])[c // 2], slice(c * cfg.NS, (c + 1) * cfg.NS)))
        in_maps.append(m)
    res = run_bass_kernel_spmd(nc, in_maps, core_ids=list(range(8)))
    outs = [unpack_core(cfg, r) for r in res.results]
    cat = lambda k, cores: np.ascontiguousarray(np.concatenate([outs[c][k] for c in cores], 0)).astype(np.float32)
    pc = [0, 2, 4, 6]
    ac = list(range(8))
    yp = np.stack([outs[c]["yp"] for c in pc], 0).astype(np.float32)
    ys = cat("ys", ac)
    res_t = [yp, ys]
    for pre, cores in (("p", pc), ("s", ac)):
        for k in ("C", "n", "m", "rgh", "rgc", "S", "x"):
            res_t.append(cat(pre + k, cores)[None])
    return tuple(res_t)
```

```python
import os
import numpy as np
import concourse.bass as bass
import concourse.mybir as mybir
from concourse.bass_utils import run_bass_kernel_spmd

F32 = mybir.dt.float32
BF16 = mybir.dt.bfloat16
AF = mybir.ActivationFunctionType
ALU = mybir.AluOpType
AX = mybir.AxisListType

LN_EPS = 1e-5
GN_EPS = 64e-5
NEG = -30000.0


class Cfg:
    def __init__(self, D=2048, NHA=4, nchunks=16, NS=16, LS=8, DEPTH=2, lora_w=96, lora_a=96, lora_g=256):
        self.D = D
        self.NC = D // 128
        self.MIXA = D // 2
        self.NHA = NHA
        self.HDA = self.MIXA // NHA
        self.HKC = self.HDA // 128
        self.MC = self.MIXA // 128
        self.RGW = D // 2
        self.RGB = self.RGW // 128
        self.NHC = D // 64
        self.DFF = 4 * D
        self.FC = self.DFF // 128
        self.nchunks = nchunks
        self.TP = 16 + 128 * nchunks
        self.NS = NS
        self.LS = LS
        self.TS = NS * LS
        self.LW, self.LA, self.LG = lora_w, lora_a, lora_g
        self.ALPHA = (2.0 * DEPTH) ** 0.25
        tiles = [("p", 0, 16)] + [("p", 16 + 128 * i, 128) for i in range(nchunks)]
        blocks = []
        cur = []
        n = 0
        for t in tiles:
            if n + t[2] > 512:
                blocks.append(cur)
                cur, n = [], 0
            cur.append(t)
            n += t[2]
        if n + self.TS > 512:
            blocks.append(cur)
            cur = []
        cur.append(("s", 0, self.TS))
        blocks.append(cur)
        self.blocks = blocks


class T:
    __slots__ = ("ap", "name", "w", "r", "dsem", "dcount")

    def __init__(self, ap, name):
        self.ap, self.name = ap, name
        self.w, self.r = {}, {}
        self.dsem, self.dcount = None, 0

    def __getitem__(self, k):
        return V(self, self.ap[k])

    @property
    def v(self):
        return V(self, self.ap)


class V:
    __slots__ = ("t", "ap", "ts")

    def __init__(self, t, ap, ts=None):
        self.t, self.ap, self.ts = t, ap, ts

    def __getitem__(self, k):
        return V(self.t, self.ap[k], self.ts)

    def bc(self, shape):
        return V(self.t, self.ap.to_broadcast(list(shape)), self.ts)

    def re(self, pat, **kw):
        return V(self.t, self.ap.rearrange(pat, **kw), self.ts)

    def un(self, axis):
        return V(self.t, self.ap.unsqueeze(axis), self.ts)

    def tiles(self):
        return self.ts if self.ts is not None else [self.t]


def _ap(x):
    return x.ap if isinstance(x, V) else x


class Fw:
    ENG = ("pe", "dve", "act", "pool", "sp")

    def __init__(self, nc, dry=False):
        self.nc, self.dry = nc, dry
        self.eng = {"pe": nc.tensor, "dve": nc.vector, "act": nc.scalar, "pool": nc.gpsimd, "sp": nc.sync}
        self.cnt = {e: 0 for e in self.ENG}
        self.waited = {e: {} for e in self.ENG}
        self.dsems = {}
        self.sem = {}
        if not dry:
            self.sem = {e: nc.alloc_semaphore("s_" + e) for e in self.ENG}
        self.n_inst = 0
        self._uid = 0
        self.dma_keys = {}
        self.free_dsems = []
        self.scopes = []
        self.freed = {}
        self.min_free = 1 << 30

    def scope(self):
        fw = self

        class _S:
            def __enter__(s):
                s.tiles = []
                s.guards = []
                fw.scopes.append(s)
                return s

            def __exit__(s, *a):
                fw.scopes.pop()
                for t in s.tiles:
                    for d in (t.w, t.r):
                        for k, v in d.items():
                            if fw.freed.get(k, 0) < v:
                                fw.freed[k] = v
                    if t.dsem is not None:
                        fw.free_dsems.append(t.dsem)
                for g in reversed(s.guards):
                    g.__exit__(None, None, None)
                return False
        return _S()

    def sb(self, shape, dtype=F32, name=None):
        self._uid += 1
        name = (name or "sb") + f"_{self._uid}"
        if self.scopes:
            g = self.nc.sbuf_tensor(name, list(shape), dtype)
            h = g.__enter__()
            t = T(h.ap(), name)
            t.w = dict(self.freed)
            self.scopes[-1].tiles.append(t)
            self.scopes[-1].guards.append(g)
        else:
            t = T(self.nc.alloc_sbuf_tensor(name, list(shape), dtype).ap(), name)
        self.min_free = min(self.min_free, self.nc.sbuf_bytes_remaining)
        return t

    def ps(self, shape, dtype=F32, name=None):
        self._uid += 1
        name = (name or "ps") + f"_{self._uid}"
        return self.nc.alloc_psum_tensor(name, list(shape), dtype).ap()

    def _collect(self, reads, writes, eng, skip=None):
        deps = {}
        for t in reads:
            for k, v in t.w.items():
                if k != skip and deps.get(k, 0) < v:
                    deps[k] = v
        for t in writes:
            for k, v in t.w.items():
                if (k == eng and eng == "pe") or k == skip:
                    continue
                if deps.get(k, 0) < v:
                    deps[k] = v
            for k, v in t.r.items():
                if k == skip:
                    continue
                if deps.get(k, 0) < v:
                    deps[k] = v
        return deps

    def _waits(self, eng, deps):
        e = self.eng[eng]
        wd = self.waited[eng]
        for k, v in deps.items():
            if wd.get(k, 0) >= v:
                continue
            sem = self.sem[k] if k in self.sem else self.dsems[k]
            e.wait_ge(sem, v)
            wd[k] = v

    def op(self, eng, fn, ins, outs):
        self.n_inst += 1
        if self.dry:
            return
        reads = []
        for x in ins:
            if isinstance(x, V):
                for t in x.tiles():
                    if t not in reads:
                        reads.append(t)
        writes = []
        for x in outs:
            for t in x.tiles():
                if t not in writes:
                    writes.append(t)
        deps = self._collect(reads, writes, eng)
        self._waits(eng, deps)
        ins_ = fn()
        self.cnt[eng] += 1
        idx = self.cnt[eng]
        ins_.then_inc(self.sem[eng], 1)
        for t in writes:
            t.w = {eng: idx}
            t.r = {}
        for t in reads:
            if t not in writes:
                t.r[eng] = idx

    def dma(self, q, out, in_, join=False):
        self.n_inst += 1
        if self.dry:
            return
        reads = in_.tiles() if isinstance(in_, V) else []
        writes = out.tiles() if isinstance(out, V) else []
        st = (writes[0] if writes else reads[0])
        if st.dsem is None:
            if self.free_dsems:
                st.dsem = self.free_dsems.pop()
            else:
                st.dsem = f"d{len(self.dsems)}"
                self.dsems[st.dsem] = self.nc.alloc_semaphore(st.dsem)
                self.dma_keys[st.dsem] = 0
            prev = self.dma_keys[st.dsem]
            if prev > 0:
                self._waits(q, {st.dsem: prev})
        deps = self._collect(reads, writes, q, skip=(st.dsem if join else None))
        self._waits(q, deps)
        ins_ = self.eng[q].dma_start(out=_ap(out), in_=_ap(in_))
        self.dma_keys[st.dsem] += 16
        cnt = self.dma_keys[st.dsem]
        ins_.then_inc(self.dsems[st.dsem], 16)
        for t in writes:
            if join and st.dsem in t.w:
                t.w[st.dsem] = cnt
            else:
                t.w = {st.dsem: cnt}
                t.r = {}
        for t in reads:
            t.r[st.dsem] = cnt

    def final_wait(self, eng="sp"):
        if self.dry:
            return
        for k, c in self.dma_keys.items():
            if c > 0:
                self.eng[eng].wait_ge(self.dsems[k], c)

    def _e(self, eng):
        return self.eng[eng]

    def tt(self, eng, out, a, b, op):
        self.op(eng, lambda: self._e(eng).tensor_tensor(out.ap, a.ap, b.ap, op), [a, b], [out])

    def ts(self, eng, out, a, s1, s2, op0, op1=None):
        if op1 is None:
            self.op(eng, lambda: self._e(eng).tensor_scalar(out.ap, a.ap, _ap(s1), None, op0), [a, s1], [out])
        else:
            self.op(eng, lambda: self._e(eng).tensor_scalar(out.ap, a.ap, _ap(s1), _ap(s2), op0, op1),
                    [a, s1, s2], [out])

    def stt(self, out, a, s, b, op0, op1):
        self.op("dve", lambda: self.nc.vector.scalar_tensor_tensor(out.ap, a.ap, _ap(s), b.ap, op0, op1),
                [a, s, b], [out])

    def scan(self, out, d0, d1, init, op0, op1):
        self.op("dve", lambda: self.nc.vector.tensor_tensor_scan(out.ap, d0.ap, d1.ap, _ap(init), op0, op1),
                [d0, d1, init], [out])

    def copy(self, eng, out, a):
        if eng == "act":
            self.op(eng, lambda: self.nc.scalar.copy(out.ap, a.ap), [a], [out])
        else:
            self.op(eng, lambda: self._e(eng).tensor_copy(out.ap, a.ap), [a], [out])

    def act(self, out, a, func, bias=0.0, scale=1.0):
        self.op("act", lambda: self.nc.scalar.activation(out.ap, a.ap, func, bias=_ap(bias), scale=_ap(scale)),
                [a, bias, scale], [out])

    def recip(self, out, a):
        self.op("dve", lambda: self.nc.vector.reciprocal(out.ap, a.ap), [a], [out])

    def memset(self, eng, out, val):
        self.op(eng, lambda: self._e(eng).memset(out.ap, val), [], [out])

    def mm(self, out, lhsT, rhs, start=True, stop=True):
        self.op("pe", lambda: self.nc.tensor.matmul(out.ap, lhsT.ap, rhs.ap, start=start, stop=stop),
                [lhsT, rhs], [out])

    def tr(self, out, a, ident):
        self.op("pe", lambda: self.nc.tensor.transpose(out.ap, a.ap, ident.ap), [a, ident], [out])


def _panels(W, pw):
    K, N = W.shape
    assert K % 128 == 0 and N % pw == 0
    return np.ascontiguousarray(W.reshape(K // 128, 128, N // pw, pw).transpose(2, 1, 0, 3))


def _fm(vec):
    v = np.asarray(vec, np.float32).reshape(-1)
    return np.ascontiguousarray(v.reshape(-1, 128).T)


def make_consts(cfg):
    cols = {}
    parts = []

    def add(name, arr):
        arr = np.asarray(arr, np.float32)
        assert arr.shape[0] == 128
        cols[name] = (sum(p.shape[1] for p in parts), arr.shape[1])
        parts.append(arr)

    I = np.arange(128)
    add("ident", np.eye(128))
    add("ones", np.ones((128, 128)))
    add("blk", (I[:, None] // 64 == I[None, :] // 64).astype(np.float32))
    add("hm", np.stack([(I < 64), (I >= 64)], 1).astype(np.float32))
    for kind, ns, L in (("p", 1, 128), ("s", cfg.NS, cfg.LS)):
        seg = I // L
        same = seg[:, None] == seg[None, :]
        le = I[:, None] <= I[None, :]
        lt = I[:, None] < I[None, :]
        add(kind + "_maskb", np.where(same & le, 0.0, NEG))
        strict = (same & lt).astype(np.float32)
        incl = (same & le).astype(np.float32)
        add(kind + "_m2", np.concatenate([strict, incl], 1))
        add(kind + "_strictT", strict.T.copy())
        start = (I % L == 0)
        add(kind + "_rmask", np.tile(np.where(start, 0.0, 1.0)[None, :], (128, 1)))
        add(kind + "_rbias", np.tile(np.where(start, -1e30, 0.0)[None, :], (128, 1)))
        rowmask = (seg[:, None] == np.arange(ns)[None, :]).astype(np.float32)
        add(kind + "_rowmask", rowmask)
    seg = I // cfg.LS
    rowmask = (seg[:, None] == np.arange(cfg.NS)[None, :]).astype(np.float32)
    colmask = np.tile(rowmask.T.reshape(1, cfg.NS * 128), (128, 1)).astype(np.float32)
    return np.concatenate(parts, 1), cols, colmask


def vec_cols(cfg):
    NC, MC, RGB = cfg.NC, cfg.MC, cfg.RGB
    names = []
    for l in range(2):
        names += [(f"ln1g{l}", NC), (f"ln1b{l}", NC), (f"ln2g{l}", NC), (f"ln2b{l}", NC)]
    names += [("mng", MC)] + [(f"cw{j}", RGB) for j in range(4)] + [("cb", RGB), ("ba", RGB), ("bx", RGB), ("lam", RGB)]
    names += [(f"mu{j}", NC) for j in range(6)]
    names += [(n, NC) for n in ("w0", "a0", "kk", "ka", "rk", "lxg", "lxb")]
    cols, o = {}, 0
    for n, c in names:
        cols[n] = (o, c)
        o += c
    return cols, o


def make_vecs(cfg, p):
    cols, n = vec_cols(cfg)
    out = np.zeros((128, n), np.float32)

    def put(name, vec):
        o, c = cols[name]
        a = _fm(vec)
        assert a.shape == (128, c), (name, a.shape, c)
        out[:, o:o + c] = a

    for l in range(2):
        put(f"ln1g{l}", p["ln1_g"][l]); put(f"ln1b{l}", p["ln1_b"][l])
        put(f"ln2g{l}", p["ln2_g"][l]); put(f"ln2b{l}", p["ln2_b"][l])
    put("mng", p["mlstm_norm_g"][0])
    for j in range(4):
        put(f"cw{j}", p["rg_conv_w"][0, j])
    put("cb", p["rg_conv_b"][0]); put("ba", p["rg_ba"][0]); put("bx", p["rg_bx"][0]); put("lam", p["rg_lambda"][0])
    for j in range(6):
        put(f"mu{j}", p["rw_mu"][0, j])
    put("w0", p["rw_w0"][0]); put("a0", p["rw_a0"][0]); put("kk", p["rw_kk"][0]); put("ka", p["rw_ka"][0])
    put("rk", np.asarray(p["rw_rk"][0]).reshape(-1)); put("lxg", p["rw_lnx_g"][0]); put("lxb", p["rw_lnx_b"][0])
    return out


def weight_specs(cfg):
    D, NC, MIXA, RGB, DFF, FC = cfg.D, cfg.NC, cfg.MIXA, cfg.RGB, cfg.DFF, cfg.FC
    PW = 256
    s = {}
    for n in ("wq", "wk", "wv", "wo", "wxr", "wgr"):
        s[n] = [MIXA // PW, 128, NC, PW]
    s["wig"] = [1, 128, NC, cfg.NHA]
    s["wfg"] = [1, 128, NC, cfg.NHA]
    s["rgax"] = [1, 128, RGB, 256]
    s["wout"] = [D // PW, 128, NC, PW]
    KH = min(FC, 32)
    for l in range(2):
        s[f"wup{l}"] = [DFF // PW, 128, NC, PW]
        s[f"wdn{l}"] = [NC * (FC // KH), 128, KH, 128]
    for n in ("rwr", "rwk", "rwv", "rwo"):
        s[n] = [D // PW, 128, NC, PW]
    s["w1"] = [1, 128, NC, cfg.LW]
    s["a1"] = [1, 128, NC, cfg.LA]
    s["g1"] = [1, 128, NC, cfg.LG]
    s["w2"] = [1, cfg.LW, 1, D]
    s["a2"] = [1, cfg.LA, 1, D]
    s["g2"] = [1, 128, cfg.LG // 128, D]
    return s


def host_weights(cfg, p):
    MIXA, RGW, NHA, FC, NC = cfg.MIXA, cfg.RGW, cfg.NHA, cfg.FC, cfg.NC
    PW = 256
    W = np.asarray(p["w_in_ab"][0], np.float32)
    o = 0
    out = {}
    for n in ("wq", "wk", "wv", "wo"):
        out[n] = _panels(W[:, o:o + MIXA], PW)
        o += MIXA
    out["wig"] = _panels(W[:, o:o + NHA], NHA)
    o += NHA
    out["wfg"] = _panels(W[:, o:o + NHA], NHA)
    o += NHA
    out["wxr"] = _panels(W[:, o:o + RGW], PW)
    o += RGW
    out["wgr"] = _panels(W[:, o:o + RGW], PW)
    o += RGW
    out["rgax"] = np.ascontiguousarray(np.concatenate([np.asarray(p["rg_wa"][0], np.float32).transpose(1, 0, 2),
                                                       np.asarray(p["rg_wx"][0], np.float32).transpose(1, 0, 2)], -1))[None]
    out["wout"] = _panels(np.asarray(p["w_out_ab"][0], np.float32), PW)
    KH = min(FC, 32)
    for l in range(2):
        out[f"wup{l}"] = _panels(np.asarray(p["w_up"][l], np.float32), PW)
        wd = _panels(np.asarray(p["w_down"][l], np.float32), 128)
        wd = wd.reshape(NC, 128, FC // KH, KH, 128).transpose(0, 2, 1, 3, 4)
        out[f"wdn{l}"] = np.ascontiguousarray(wd.reshape(NC * (FC // KH), 128, KH, 128))
    for n, k in (("rwr", "rw_wr"), ("rwk", "rw_wk"), ("rwv", "rw_wv"), ("rwo", "rw_wo")):
        out[n] = _panels(np.asarray(p[k][0], np.float32), PW)
    out["w1"] = _panels(np.asarray(p["rw_w1"][0], np.float32), cfg.LW)
    out["a1"] = _panels(np.asarray(p["rw_a1"][0], np.float32), cfg.LA)
    out["g1"] = _panels(np.asarray(p["rw_g1"][0], np.float32), cfg.LG)
    out["w2"] = np.ascontiguousarray(np.asarray(p["rw_w2"][0], np.float32))[None, :, None, :]
    out["a2"] = np.ascontiguousarray(np.asarray(p["rw_a2"][0], np.float32))[None, :, None, :]
    out["g2"] = _panels(np.asarray(p["rw_g2"][0], np.float32), cfg.D)
    specs = weight_specs(cfg)
    for n in out:
        assert list(out[n].shape) == specs[n], (n, out[n].shape, specs[n])
    return out


def io_specs(cfg):
    D, NS, NHA, HDA, RGW, NC = cfg.D, cfg.NS, cfg.NHA, cfg.HDA, cfg.RGW, cfg.NC
    ins = {
        "xp": [D, cfg.TP], "xs": [D, cfg.TS],
        "cext": [NS, NHA, HDA, HDA + 1], "m0": [NHA, NS],
        "rgh": [RGW, NS], "rgc": [RGW, NS, 3],
        "rwS": [NC, 128, NS, 64], "rwx": [D, NS],
        "bif": [NHA, 2], "colm": [128, NS * 128],
    }
    outs = {
        "yp": [D, cfg.TP], "ys": [D, cfg.TS],
        "pc": [1, NHA, HDA, HDA + 1], "pm": [NHA, 1], "prgh": [RGW, 1], "prgc": [RGW, 1, 3],
        "prwS": [NC, 128, 1, 64], "prwx": [D, 1],
        "sc": [NS, NHA, HDA, HDA + 1], "sm": [NHA, NS], "srgh": [RGW, NS], "srgc": [RGW, NS, 3],
        "srwS": [NC, 128, NS, 64], "srwx": [D, NS],
    }
    return ins, outs


class Tile:
    def __init__(self, kind, tok0, L, c0, cfg):
        self.kind, self.tok0, self.L, self.c0 = kind, tok0, L, c0
        if kind == "p":
            self.ns, self.ls = 1, L
        else:
            self.ns, self.ls = cfg.NS, cfg.LS
        self.nsteps = int(np.ceil(np.log2(self.ls)))


class Reg:
    pass


class StopBuild(Exception):
    pass


STOP = None


class Blk:
    pass


class Prog:
    WELEMS = 4096

    def __init__(self, nc, cfg, plan=None, debug=()):
        self.nc, self.cfg = nc, cfg
        self.dry = plan is None
        self.fw = Fw(nc, dry=self.dry)
        self.plan = plan if plan is not None else []
        self.wk = 0
        self.wissued = 0
        self.debug = debug
        self.dbg_out = {}
        self.wspec = weight_specs(cfg)

    def declare(self):
        nc, cfg = self.nc, self.cfg
        ins, outs = io_specs(cfg)
        self.din, self.dout = {}, {}
        for n, s in ins.items():
            self.din[n] = nc.dram_tensor(n, s, F32, kind="ExternalInput").ap()
        cst, self.ccols, _ = make_consts(cfg)
        self.ncst = cst.shape[1]
        self.din["cst"] = nc.dram_tensor("cst", [128, self.ncst], F32, kind="ExternalInput").ap()
        self.vcols, self.nvec = vec_cols(cfg)
        self.din["vec"] = nc.dram_tensor("vec", [128, self.nvec], F32, kind="ExternalInput").ap()
        self.dw = {}
        for n, s in self.wspec.items():
            self.dw[n] = nc.dram_tensor(n, s, F32, kind="ExternalInput").ap()
        for n, s in outs.items():
            self.dout[n] = nc.dram_tensor(n, s, F32, kind="ExternalOutput").ap()

    def wpanel(self, name, pan):
        fw = self.fw
        shp = self.wspec[name]
        parts, KC, pw = shp[1], shp[2], shp[3]
        assert KC * pw <= self.WELEMS, (name, KC, pw)
        k = self.wk
        self.wk += 1
        if self.dry:
            self.plan.append((name, pan))
        else:
            assert self.plan[k] == (name, pan), (k, self.plan[k], name, pan)
            while self.wissued < min(len(self.plan), k + 4):
                j = self.wissued
                nm, pn = self.plan[j]
                s2 = self.wspec[nm]
                buf = self.wbufs[j % 4]
                dst = buf[0:s2[1], 0:s2[2] * s2[3]]
                fw.dma("pool", dst, self.dw[nm][pn].rearrange("p k w -> p (k w)"))
                self.wissued += 1
        buf = self.wbufs[k % 4]
        return buf[0:parts, 0:KC * pw].re("p (k w) -> p k w", k=KC)

    def psum(self):
        b = self.bank[self.psi % 4]
        self.psi += 1
        return b

    def cc(self, name, rows=128, c0=0, n=None):
        o, w = self.ccols[name]
        n = w - c0 if n is None else n
        return self.CST[0:rows, o + c0:o + c0 + n]

    def vc(self, name, j=0, rows=128):
        o, w = self.vcols[name]
        return self.VEC[0:rows, o + j:o + j + 1]

    def dump(self, name, v, shape):
        if name not in self.debug:
            return
        fw = self.fw
        key = name
        i = 0
        while key in self.dbg_out:
            i += 1
            key = f"{name}_{i}"
        self.dbg_out[key] = self.nc.dram_tensor("dbg_" + key, list(shape), F32, kind="ExternalOutput").ap()
        tmp = fw.sb(list(shape), F32, "dbg")
        fw.copy("dve", tmp.v, v)
        fw.dma("sp", self.dbg_out[key], tmp.v)

    def build(self):
        cfg, fw, nc = self.cfg, self.fw, self.nc
        self.declare()
        NC = cfg.NC
        self.CST = fw.sb([128, self.ncst], F32, "cst")
        self.VEC = fw.sb([128, self.nvec], F32, "vec")
        fw.dma("sp", self.CST.v, self.din["cst"])
        fw.dma("sp", self.VEC.v, self.din["vec"])
        self.wbufs = [fw.sb([128, self.WELEMS], BF16, f"wbuf{i}") for i in range(4)]
        psA = fw.ps([128, 4, 512], F32, "psA")
        psD = fw.ps([128, 4, 512], F32, "psD")
        self.bank = [T(psA[:, i, :], f"bankA{i}") for i in range(4)] + [T(psD[:, i, :], f"bankD{i}") for i in range(4)]
        self.psD = V(self.bank[4], psD, ts=self.bank[4:8])
        self.psi = 0
        self.identb = fw.sb([128, 128], BF16, "identb")
        fw.copy("dve", self.identb.v, self.cc("ident"))
        self.onesb = fw.sb([128, 128], BF16, "onesb")
        fw.copy("dve", self.onesb.v, self.cc("ones"))
        self.colmb = fw.sb([128, cfg.NS * 128], BF16, "colmb")
        fw.dma("pool", self.colmb.v, self.din["colm"])
        try:
            self.derived_vecs()
            self.init_states()
            self.chk("init")
            for bi, blk in enumerate(cfg.blocks):
                self.run_block(bi, blk)
        except StopBuild:
            pass
        self.write_prompt_states()
        fw.final_wait("sp")

    def chk(self, name):
        if not hasattr(self, "phase_log"):
            self.phase_log = []
        self.phase_log.append((name, self.fw.cnt["pe"]))
        if STOP == name:
            raise StopBuild()

    def derived_vecs(self):
        cfg, fw = self.cfg, self.fw
        RGB, NC, NHA = cfg.RGB, cfg.NC, cfg.NHA
        self.DV = fw.sb([128, 2 * RGB + 7 * NC], F32, "dv")
        o, _ = self.vcols["lam"]
        lam = self.VEC[:, o:o + RGB]
        t = self.DV[:, 0:RGB]
        fw.act(t, lam, AF.Exp, scale=-1.0)
        fw.act(t, t, AF.Ln, bias=1.0)
        fw.ts("dve", self.DV[:, RGB:2 * RGB], t, -16.0, None, ALU.mult)
        fw.ts("dve", t, t, -8.0, None, ALU.mult)
        b = 2 * RGB
        for j in range(6):
            o, _ = self.vcols[f"mu{j}"]
            fw.ts("dve", self.DV[:, b + j * NC:b + (j + 1) * NC], self.VEC[:, o:o + NC], -1.0, 1.0, ALU.mult, ALU.add)
        b2 = b + 6 * NC
        o, _ = self.vcols["ka"]
        fw.ts("dve", self.DV[:, b2:b2 + NC], self.VEC[:, o:o + NC], -1.0, 1.0, ALU.mult, ALU.add)
        self.dv_c1 = lambda n: self.DV[:, n:n + 1]
        self.dv_c2 = lambda n: self.DV[:, RGB + n:RGB + n + 1]
        self.dv_1mmu = lambda j, c: self.DV[:, b + j * NC + c:b + j * NC + c + 1]
        self.dv_1mka = lambda c: self.DV[:, b2 + c:b2 + c + 1]
        self.EPS = fw.sb([128, 2], F32, "eps")
        fw.memset("dve", self.EPS[:, 0:1], LN_EPS)
        fw.memset("dve", self.EPS[:, 1:2], GN_EPS)
        self.epsc = lambda k: self.EPS[:, 0:1] if k == "ln" else self.EPS[:, 1:2]
        self.BIF = fw.sb([NHA, 3], F32, "bif")
        fw.dma("sp", self.BIF[:, 0:2], self.din["bif"])
        fw.ts("dve", self.BIF[:, 2:3], self.BIF[:, 1:2], -1.0, None, ALU.mult)
        self.SEL = fw.sb([NHA, NHA, 128], F32, "sel")
        for h in range(NHA):
            fw.copy("dve", self.SEL[:, h, :], self.cc("ident", rows=NHA, c0=h, n=1).bc([NHA, 128]))

    def init_states(self):
        cfg, fw = self.cfg, self.fw
        NHA, HKC, HDA, RGB, NC = cfg.NHA, cfg.HKC, cfg.HDA, cfg.RGB, cfg.NC
        self.pC = fw.sb([128, NHA, HKC, HDA + 1], F32, "pC")
        fw.memset("dve", self.pC.v, 0.0)
        self.pM = fw.sb([NHA, 1], F32, "pM")
        fw.memset("dve", self.pM.v, 0.0)
        self.sM = fw.sb([NHA, cfg.NS], F32, "sM")
        fw.dma("sp", self.sM.v, self.din["m0"])
        self.pRH = fw.sb([128, RGB, 1], F32, "pRH")
        fw.memset("dve", self.pRH.v, 0.0)
        self.pRC = fw.sb([128, RGB, 1, 3], F32, "pRC")
        fw.memset("dve", self.pRC.v, 0.0)
        self.pH = fw.sb([128, NC, 1, 64], F32, "pH")
        fw.memset("dve", self.pH.v, 0.0)
        self.pSH = fw.sb([128, NC, 1], F32, "pSH")
        fw.memset("dve", self.pSH.v, 0.0)

    def write_prompt_states(self):
        fw, do = self.fw, self.dout
        fw.dma("sp", do["pc"][0].rearrange("h (kc p) v -> p h kc v", p=128), self.pC.v)
        fw.dma("sp", do["pm"], self.pM.v)
        fw.dma("sp", do["prgh"].rearrange("(n p) o -> p n o", p=128), self.pRH.v)
        fw.dma("sp", do["prgc"].rearrange("(n p) o j -> p n o j", p=128), self.pRC.v)
        fw.dma("sp", do["prwS"].rearrange("m p o v -> p m o v"), self.pH.v)
        fw.dma("sp", do["prwx"].rearrange("(c p) o -> p c o", p=128), self.pSH.v)

    def run_block(self, bi, blk):
        cfg, fw = self.cfg, self.fw
        NC = cfg.NC
        B = Blk()
        B.tiles = []
        c = 0
        for (kind, tok0, L) in blk:
            B.tiles.append(Tile(kind, tok0, L, c, cfg))
            c += L
        B.Tb = c
        B.regs = []
        e = 0
        pt = [t for t in B.tiles if t.kind == "p"]
        if pt:
            r = Reg()
            r.kind, r.c0, r.ns, r.L, r.tok0 = "p", pt[0].c0, 1, sum(t.L for t in pt), pt[0].tok0
            r.e0 = e
            e += 1 + r.L
            B.regs.append(r)
        st = [t for t in B.tiles if t.kind == "s"]
        if st:
            r = Reg()
            r.kind, r.c0, r.ns, r.L, r.tok0 = "s", st[0].c0, cfg.NS, cfg.LS, 0
            r.e0 = e
            e += r.ns * (1 + r.L)
            B.regs.append(r)
        B.EXT = e
        for r in B.regs:
            r.n = r.ns * r.L
        with fw.scope():
            B.hres = fw.sb([128, NC, B.EXT], F32, "hres")
            B.abuf = fw.sb([128, NC, B.Tb], BF16, "abuf")
            self.B = B
            for r in B.regs:
                if r.kind == "p":
                    src = self.din["xp"][:, r.tok0:r.tok0 + r.L].rearrange("(c p) t -> p c t", p=128)
                    fw.dma("sp", B.hres[:, :, r.e0 + 1:r.e0 + 1 + r.L], src)
                else:
                    for kc in range(NC):
                        src = self.din["xs"][kc * 128:(kc + 1) * 128, :].rearrange("p (s t) -> p s t", s=r.ns)
                        fw.dma("sp", self.hv(kc, r), src, join=True)
                        src2 = self.din["rwx"][kc * 128:(kc + 1) * 128, :]
                        fw.dma("sp", self.hv(kc, r, -1)[:, :, 0], src2, join=True)
            self.chk("loadx")
            self.to_bf16(B)
            self.chk("bf16")
            self.layer0_mixer(B)
            self.chk("wout")
            self.layernorm(B, "ln1g0", "ln1b0", bf=True)
            self.chk("ln1")
            self.mlp(B, 0)
            self.chk("mlp0")
            self.layernorm(B, "ln2g0", "ln2b0", bf=False)
            self.chk("ln2")
            for r in B.regs:
                if r.kind == "p":
                    fw.copy("dve", B.hres[:, :, r.e0:r.e0 + 1], self.pSH.v)
                    fw.copy("dve", self.pSH.v, B.hres[:, :, r.e0 + r.L:r.e0 + r.L + 1])
                else:
                    for kc in range(NC):
                        dst = self.dout["srwx"][kc * 128:(kc + 1) * 128, :]
                        fw.dma("sp", dst, self.hv(kc, r)[:, :, r.L - 1], join=True)
            self.layer1_mixer(B)
            self.chk("l1mix")
            self.layernorm(B, "ln1g1", "ln1b1", bf=True)
            self.mlp(B, 1)
            self.layernorm(B, "ln2g1", "ln2b1", bf=False)
            for r in B.regs:
                if r.kind == "p":
                    dst = self.dout["yp"][:, r.tok0:r.tok0 + r.L].rearrange("(c p) t -> p c t", p=128)
                    fw.dma("sp", dst, B.hres[:, :, r.e0 + 1:r.e0 + 1 + r.L])
                else:
                    for kc in range(NC):
                        dst = self.dout["ys"][kc * 128:(kc + 1) * 128, :].rearrange("p (s t) -> p s t", s=r.ns)
                        fw.dma("sp", dst, self.hv(kc, r), join=True)

    def hv(self, kc, r, shift=0):
        B = self.B
        v = B.hres[:, kc, r.e0:r.e0 + r.ns * (1 + r.L)].re("p (s l) -> p s l", s=r.ns)
        return v[:, :, 1 + shift:1 + shift + r.L]

    def pv(self, v2d, r):
        return v2d[:, r.c0:r.c0 + r.n].re("p (s l) -> p s l", s=r.ns)

    def to_bf16(self, B):
        fw = self.fw
        for kc in range(self.cfg.NC):
            for r in B.regs:
                fw.copy("act" if kc % 2 else "dve", self.pv(B.abuf[:, kc, :], r), self.hv(kc, r))

    def proj_fm(self, wname, rhs_fn, N, evac):
        fw = self.fw
        npan, parts, KC, pw = self.wspec[wname]
        for pan in range(npan):
            W = self.wpanel(wname, pan)
            for mi in range(pw // 128):
                ps = self.psum()
                for kc in range(KC):
                    fw.mm(ps[:, 0:N], W[:, kc, mi * 128:(mi + 1) * 128], rhs_fn(kc), start=(kc == 0), stop=(kc == KC - 1))
                evac(pan * (pw // 128) + mi, ps)
            self.bg_step()

    def bg_step(self):
        g = getattr(self, "_bg", None)
        if g is not None:
            try:
                next(g)
            except StopIteration:
                self._bg = None

    def proj_resid(self, B, wname, N0=0, N=None):
        fw, cfg = self.fw, self.cfg

        def evac(m, ps):
            for r in B.regs:
                hv = self.hv(m, r)
                fw.stt(hv, hv, cfg.ALPHA, self.pv(ps[:, 0:B.Tb], r), ALU.mult, ALU.add)
        MC = cfg.MC
        if wname == "wout":
            yr = self.M.yr
            self.proj_fm(wname, lambda kc: B.abuf[:, kc, :] if kc < MC else yr[:, kc - MC, :], B.Tb, evac)
        else:
            self.proj_fm(wname, lambda kc: B.abuf[:, kc, :], B.Tb, evac)

    def layernorm(self, B, gname, bname, bf):
        fw, cfg = self.fw, self.cfg
        NC, Tb, D = cfg.NC, B.Tb, cfg.D
        onesf = self.cc("ones")
        with fw.scope():
            mu = fw.sb([128, Tb], F32, "lnmu")
            rs = fw.sb([128, Tb], F32, "lnrs")
            sqb = [fw.sb([128, Tb], BF16, f"lnsq{i}") for i in range(2)]
            xb = [fw.sb([128, Tb], BF16, f"lnxb{i}") for i in range(2)]
            m2 = fw.sb([128, Tb], F32, "lnm2")
            ps1 = self.psum()
            ps2 = self.psum()
            onesb = self.onesb.v
            for r in B.regs:
                for kc in range(NC):
                    hv = self.hv(kc, r)
                    s_, x_ = sqb[kc % 2], xb[kc % 2]
                    fw.act(self.pv(s_.v, r), hv, AF.Square)
                    fw.copy("dve", self.pv(x_.v, r), hv)
                    fw.mm(self.pv(ps1[:, 0:Tb], r), onesb, self.pv(x_.v, r), start=(kc == 0), stop=(kc == NC - 1))
                    fw.mm(self.pv(ps2[:, 0:Tb], r), onesb, self.pv(s_.v, r), start=(kc == 0), stop=(kc == NC - 1))
            fw.ts("dve", mu[:, 0:Tb], ps1[:, 0:Tb], 1.0 / D, None, ALU.mult)
            fw.tt("dve", m2[:, 0:Tb], mu[:, 0:Tb], mu[:, 0:Tb], ALU.mult)
            fw.stt(rs[:, 0:Tb], ps2[:, 0:Tb], 1.0 / D, m2[:, 0:Tb], ALU.mult, ALU.subtract)
            fw.ts("dve", rs[:, 0:Tb], rs[:, 0:Tb], 0.0, None, ALU.max)
            fw.act(rs[:, 0:Tb], rs[:, 0:Tb], AF.Ln, bias=self.epsc("ln"))
            fw.act(rs[:, 0:Tb], rs[:, 0:Tb], AF.Exp, scale=-0.5)
            for kc in range(NC):
                for r in B.regs:
                    hv = self.hv(kc, r)
                    fw.tt("dve", hv, hv, self.pv(mu.v, r), ALU.subtract)
                    fw.tt("dve", hv, hv, self.pv(rs.v, r), ALU.mult)
                    if bf:
                        fw.act(self.pv(B.abuf[:, kc, :], r), hv, AF.Identity, bias=self.vc(bname, kc), scale=self.vc(gname, kc))
                    fw.ts("dve", hv, hv, self.vc(gname, kc), self.vc(bname, kc), ALU.mult, ALU.add)

    def mlp(self, B, l):
        fw, cfg = self.fw, self.cfg
        NC, FC, Tb = cfg.NC, cfg.FC, B.Tb
        with fw.scope():
            hid = fw.sb([128, FC, Tb], BF16, "hid")
            tmp = [fw.sb([128, Tb], F32, f"mlptmp{i}") for i in range(2)]

            def evac(f, ps):
                t = tmp[f % 2]
                fw.act(t[:, 0:Tb], ps[:, 0:Tb], AF.Relu)
                fw.tt("dve", hid[:, f, :], t[:, 0:Tb], t[:, 0:Tb], ALU.mult)
            self.proj_fm(f"wup{l}", lambda kc: B.abuf[:, kc, :], Tb, evac)
            npan, parts, KH, pw = self.wspec[f"wdn{l}"]
            nh = FC // KH
            for m in range(NC):
                ps = self.psum()
                for hf in range(nh):
                    W = self.wpanel(f"wdn{l}", m * nh + hf)
                    for kk in range(KH):
                        f = hf * KH + kk
                        fw.mm(ps[:, 0:Tb], W[:, kk, :], hid[:, f, :], start=(f == 0), stop=(f == FC - 1))
                for r in B.regs:
                    hv = self.hv(m, r)
                    fw.stt(hv, hv, cfg.ALPHA, self.pv(ps[:, 0:Tb], r), ALU.mult, ALU.add)

    def layer0_mixer(self, B):
        fw, cfg = self.fw, self.cfg
        NC, MC, NHA, HDA, HKC, RGB, Tb = cfg.NC, cfg.MC, cfg.NHA, cfg.HDA, cfg.HKC, cfg.RGB, B.Tb
        nt = len(B.tiles)
        with fw.scope():
            M = Blk()
            self.M = M
            M.qT = fw.sb([128, MC, Tb], BF16, "qT")
            M.kT = fw.sb([128, MC, Tb], BF16, "kT")
            M.oT = fw.sb([128, MC, Tb], BF16, "oT")
            M.kTok = fw.sb([128, nt, cfg.MIXA], BF16, "kTok")
            M.vTok = fw.sb([128, nt, NHA, HDA + 1], BF16, "vTok")
            M.gr = fw.sb([128, RGB, Tb], BF16, "gr")
            M.XE = sum(r.ns * (3 + r.L) for r in B.regs)
            M.xr = fw.sb([128, RGB, M.XE], F32, "xr")
            M.ig = fw.sb([NHA, Tb], F32, "ig")
            M.lf = fw.sb([NHA, Tb], F32, "lf")
            xo = 0
            for r in B.regs:
                r.x0 = xo
                xo += r.ns * (3 + r.L)
            rhs = lambda kc: B.abuf[:, kc, :]
            sc = float(HDA) ** -0.5
            fw.memset("dve", M.vTok[:, :, :, HDA:HDA + 1], 1.0)
            for r in B.regs:
                if r.kind == "p":
                    fw.copy("dve", self.xrv(None, r, 0, 3), self.pRC.v)
                else:
                    for n in range(RGB):
                        src = self.din["rgc"][n * 128:(n + 1) * 128]
                        fw.dma("sp", self.xrv(n, r, 0, 3), src, join=True)
            def xev(m, ps):
                for r in B.regs:
                    fw.copy("act", self.xrv(m, r, 3, r.L), self.pv(ps[:, 0:Tb], r))
            self.proj_fm("wxr", rhs, Tb, xev)
            with fw.scope():
                g1 = fw.sb([128, Tb], F32, "g1")
                g2 = fw.sb([128, Tb], F32, "g2")

                def gev(m, ps):
                    fw.act(g1[:, 0:Tb], ps[:, 0:Tb], AF.Square)
                    fw.ts("dve", g1[:, 0:Tb], g1[:, 0:Tb], 0.044715, 1.0, ALU.mult, ALU.add)
                    fw.tt("dve", g1[:, 0:Tb], g1[:, 0:Tb], ps[:, 0:Tb], ALU.mult)
                    fw.act(g2[:, 0:Tb], g1[:, 0:Tb], AF.Sigmoid, scale=1.5957691216)
                    fw.tt("dve", M.gr[:, m, :], g2[:, 0:Tb], ps[:, 0:Tb], ALU.mult)
                self.proj_fm("wgr", rhs, Tb, gev)
            M.yr = fw.sb([128, RGB, Tb], BF16, "yr")
            rg = self.rglru(B, M)
            self._bg = rg
            self.proj_fm("wq", rhs, Tb, lambda m, ps: fw.copy("act", M.qT[:, m, :], ps[:, 0:Tb]))
            self.proj_fm_tok("wk", B, lambda m, ps: fw.act(M.kT[:, m, :], ps[:, 0:Tb], AF.Copy, scale=sc),
                             lambda ti, L, col0, w, ps: fw.act(M.kTok[0:L, ti, col0:col0 + w], ps[0:L, 0:w], AF.Copy, scale=sc))
            def vev(ti, L, col0, w, ps):
                c = col0
                while c < col0 + w:
                    h, dv = c // HDA, c % HDA
                    ww = min(HDA - dv, col0 + w - c)
                    fw.copy("act", M.vTok[0:L, ti, h, dv:dv + ww], ps[0:L, c - col0:c - col0 + ww])
                    c += ww
            self.proj_fm_tok("wv", B, None, vev)
            self.proj_fm("wo", rhs, Tb, lambda m, ps: fw.act(M.oT[:, m, :], ps[:, 0:Tb], AF.Sigmoid))
            for nm, dst in (("wig", M.ig), ("wfg", M.lf)):
                W = self.wpanel(nm, 0)
                ps = self.psum()
                for kc in range(NC):
                    fw.mm(ps[0:NHA, 0:Tb], W[:, kc, 0:NHA], rhs(kc), start=(kc == 0), stop=(kc == NC - 1))
                if nm == "wig":
                    fw.act(dst[:, 0:Tb], ps[0:NHA, 0:Tb], AF.Identity, bias=self.BIF[:, 0:1])
                else:
                    fw.act(dst[:, 0:Tb], ps[0:NHA, 0:Tb], AF.Exp, bias=self.BIF[:, 2:3], scale=-1.0)
                    fw.act(dst[:, 0:Tb], dst[:, 0:Tb], AF.Ln, bias=1.0)
                    fw.ts("dve", dst[:, 0:Tb], dst[:, 0:Tb], -1.0, None, ALU.mult)
            self._bg = None
            for _ in rg:
                pass
            self.dump("qT", M.qT.v, [128, MC, Tb])
            self.dump("kT", M.kT.v, [128, MC, Tb])
            self.dump("ig", M.ig.v, [NHA, Tb])
            self.dump("lf", M.lf.v, [NHA, Tb])
            self.chk("inproj")
            for ti, t in enumerate(B.tiles):
                self.mlstm_tile(B, M, ti, t)
                self.chk(f"mlstm{ti}")
            self.chk("mlstm")
            self.chk("rglru")
            self.dump("cat", B.abuf.v, [128, NC, Tb])
            self.proj_resid(B, "wout")

    def xrv(self, n, r, j0, w):
        M = self.M
        if n is None:
            v = M.xr[:, :, r.x0:r.x0 + r.ns * (3 + r.L)].re("p n (s l) -> p n s l", s=r.ns)
            return v[:, :, :, j0:j0 + w]
        v = M.xr[:, n, r.x0:r.x0 + r.ns * (3 + r.L)].re("p (s l) -> p s l", s=r.ns)
        return v[:, :, j0:j0 + w]

    def proj_fm_tok(self, wname, B, evac_fm, evac_tok):
        fw = self.fw
        npan, parts, KC, pw = self.wspec[wname]
        for pan in range(npan):
            W = self.wpanel(wname, pan)
            if evac_fm is not None:
                for mi in range(pw // 128):
                    ps = self.psum()
                    for kc in range(KC):
                        fw.mm(ps[:, 0:B.Tb], W[:, kc, mi * 128:(mi + 1) * 128], B.abuf[:, kc, :], start=(kc == 0), stop=(kc == KC - 1))
                    evac_fm(pan * (pw // 128) + mi, ps)
            for ti, t in enumerate(B.tiles):
                ps = self.psum()
                for kc in range(KC):
                    fw.mm(ps[0:t.L, 0:pw], B.abuf[:, kc, t.c0:t.c0 + t.L], W[:, kc, :], start=(kc == 0), stop=(kc == KC - 1))
                evac_tok(ti, t.L, pan * pw, pw, ps)
            self.bg_step()

    def mlstm_tile(self, B, M, ti, t):
        fw, cfg = self.fw, self.cfg
        NHA, HDA, HKC = cfg.NHA, cfg.HDA, cfg.HKC
        L, c0, ns, ls, kind = t.L, t.c0, t.ns, t.ls, t.kind
        cs = slice(c0, c0 + L)
        onesf = self.cc("ones")
        mprev = self.pM if kind == "p" else self.sM
        with fw.scope():
            G = fw.sb([NHA, 6, 128], F32, "G")
            Fv, gv, Mv, iv, ev, wv = (G[:, i, 0:L] for i in range(6))
            s3 = lambda v: v.re("h (s l) -> h s l", s=ns)
            fw.scan(Fv, self.cc(kind + "_rmask", rows=NHA, n=L), M.lf[:, cs], 0.0, ALU.mult, ALU.add)
            fw.tt("dve", gv, M.ig[:, cs], Fv, ALU.subtract)
            fw.scan(Mv, self.cc(kind + "_rbias", rows=NHA, n=L), gv, -1e30, ALU.add, ALU.max)
            fw.tt("dve", s3(Mv), s3(Mv), mprev[:, 0:ns].un(2).bc([NHA, ns, ls]), ALU.max)
            fw.tt("dve", s3(iv), s3(Mv), mprev[:, 0:ns].un(2).bc([NHA, ns, ls]), ALU.subtract)
            fw.act(iv, iv, AF.Exp, scale=-1.0)
            fw.tt("dve", ev, Fv, Mv, ALU.add)
            mnew = fw.sb([NHA, ns], F32, "mnew")
            fw.copy("dve", mnew.v, s3(ev)[:, :, ls - 1])
            fw.act(ev, ev, AF.Exp, scale=-1.0)
            fw.tt("dve", s3(wv), s3(gv), s3(Mv)[:, :, ls - 1:ls].bc([NHA, ns, ls]), ALU.subtract)
            fw.act(wv, wv, AF.Exp)
            pc = self.psum()
            identf = self.cc("ident", rows=NHA, n=NHA)
            fw.mm(pc[0:L, 0:NHA], gv, identf)
            fw.mm(pc[0:L, NHA:2 * NHA], wv, identf)
            cols = fw.sb([128, 2 * NHA], F32, "cols")
            fw.copy("act", cols[0:L, :], pc[0:L, 0:2 * NHA])
            BCs = fw.sb([128, 3, 128], F32, "BCs")
            DT = fw.sb([128, 128], F32, "DT")
            sTd = fw.sb([128, 128], BF16, "sTd")
            qTs = fw.sb([128, HKC, 128], BF16, "qTs")
            hT = fw.sb([128, HKC, 128], F32, "hT")
            sq = fw.sb([128, HKC, 128], F32, "hsq")
            dn = fw.sb([128, 128], F32, "dn")
            mu = fw.sb([128, 128], F32, "hmu")
            kw = fw.sb([128, HDA], BF16, "kw")
            wm = fw.sb([128, ns], F32, "wm")
            Cb = fw.sb([128, HKC, HDA], BF16, "Cb")
            nbc = fw.sb([128, HKC, 128], BF16, "nbc")
            Cs = [fw.sb([128, HKC, HDA + 1], F32, f"Cs{i}") for i in range(2)] if kind == "s" else None
            for h in range(NHA):
                pb = self.psum()
                fw.mm(pb[:, 0:3 * L].re("p (a l) -> p a l", a=3), self.SEL[:, h, :], G[:, 2:5, 0:L])
                fw.copy("act", BCs[:, :, 0:L], pb[:, 0:3 * L].re("p (a l) -> p a l", a=3))
                fw.stt(DT[0:L, 0:L], BCs[0:L, 0, 0:L], -1.0, self.cc(kind + "_maskb", rows=L, n=L), ALU.mult, ALU.add)
                fw.act(DT[0:L, 0:L], DT[0:L, 0:L], AF.Exp, bias=cols[0:L, h:h + 1])
                p2 = self.psum()
                for kc in range(HKC):
                    fw.mm(p2[0:L, 0:L], M.kT[:, h * HKC + kc, cs], M.qT[:, h * HKC + kc, cs], start=(kc == 0), stop=(kc == HKC - 1))
                fw.tt("dve", sTd[0:L, 0:L], p2[0:L, 0:L], DT[0:L, 0:L], ALU.mult)
                for kc in range(HKC):
                    fw.tt("dve", qTs[:, kc, 0:L], M.qT[:, h * HKC + kc, cs], BCs[:, 1, 0:L], ALU.mult)
                psn = [self.bank[4 + c] for c in range(HKC)]
                psd = self.bank[4 + HKC]
                for c in range(HKC):
                    fw.mm(psn[c][:, 0:L], M.vTok[0:L, ti, h, c * 128:(c + 1) * 128], sTd[0:L, 0:L], start=True, stop=False)
                fw.mm(psd[:, 0:L], self.onesb[0:L, :], sTd[0:L, 0:L], start=True, stop=False)
                fw.ts("dve", wm[0:L, 0:ns], self.cc(kind + "_rowmask", rows=L, n=ns), cols[0:L, NHA + h:NHA + h + 1], None, ALU.mult)
                for s in range(ns):
                    sc = slice(s * ls, (s + 1) * ls)
                    if kind == "s":
                        Cst = Cs[(h * ns + s) % 2]
                        fw.dma("sp", Cst.v, self.din["cext"][s, h].rearrange("(kc p) v -> p kc v", p=128))
                        Cv = Cst.v
                    else:
                        Cv = self.pC[:, h]
                    fw.copy("act", Cb.v, Cv[:, :, 0:HDA])
                    fw.copy("dve", nbc.v, Cv[:, :, HDA:HDA + 1].bc([128, HKC, 128]))
                    last = (s == ns - 1)
                    for c in range(HKC):
                        for kc in range(HKC):
                            fw.mm(psn[c][:, sc], Cb[:, kc, c * 128:(c + 1) * 128], qTs[:, kc, sc], start=False, stop=(last and kc == HKC - 1))
                    for kc in range(HKC):
                        fw.mm(psd[:, sc], nbc[:, kc, :], qTs[:, kc, sc], start=False, stop=(last and kc == HKC - 1))
                    fw.ts("dve", kw[0:L, :], M.kTok[0:L, ti, h * HDA:(h + 1) * HDA], wm[0:L, s:s + 1], None, ALU.mult)
                    dec = BCs[:, 1, (s + 1) * ls - 1:(s + 1) * ls]
                    for kc in range(HKC):
                        pu = self.psum()
                        fw.mm(pu[:, 0:HDA + 1], kw[0:L, kc * 128:(kc + 1) * 128], M.vTok[0:L, ti, h, :])
                        fw.stt(Cv[:, kc, :], Cv[:, kc, :], dec, pu[:, 0:HDA + 1], ALU.mult, ALU.add)
                    if kind == "s":
                        fw.dma("sp", self.dout["sc"][s, h].rearrange("(kc p) v -> p kc v", p=128), Cv)
                fw.act(dn[:, 0:L], psd[:, 0:L], AF.Abs)
                fw.tt("dve", dn[:, 0:L], dn[:, 0:L], BCs[:, 2, 0:L], ALU.max)
                fw.recip(dn[:, 0:L], dn[:, 0:L])
                for c in range(HKC):
                    fw.tt("dve", hT[:, c, 0:L], psn[c][:, 0:L], dn[:, 0:L], ALU.mult)
                p3 = self.psum()
                for c in range(HKC):
                    fw.mm(p3[:, 0:L], onesf, hT[:, c, 0:L], start=(c == 0), stop=(c == HKC - 1))
                fw.ts("dve", mu[:, 0:L], p3[:, 0:L], 1.0 / HDA, None, ALU.mult)
                p4 = self.psum()
                for c in range(HKC):
                    fw.tt("dve", hT[:, c, 0:L], hT[:, c, 0:L], mu[:, 0:L], ALU.subtract)
                    fw.act(sq[:, c, 0:L], hT[:, c, 0:L], AF.Square)
                    fw.mm(p4[:, 0:L], onesf, sq[:, c, 0:L], start=(c == 0), stop=(c == HKC - 1))
                fw.act(mu[:, 0:L], p4[:, 0:L], AF.Ln, bias=self.epsc("ln"), scale=1.0 / HDA)
                fw.act(mu[:, 0:L], mu[:, 0:L], AF.Exp, scale=-0.5)
                for c in range(HKC):
                    m = h * HKC + c
                    fw.tt("dve", hT[:, c, 0:L], hT[:, c, 0:L], mu[:, 0:L], ALU.mult)
                    fw.stt(B.abuf[:, m, cs], hT[:, c, 0:L], self.vc("mng", m), M.oT[:, m, cs], ALU.mult, ALU.mult)
            if kind == "p":
                fw.copy("dve", self.pM.v, mnew.v)
            else:
                fw.dma("sp", self.dout["sm"], mnew.v)

    def rglru(self, B, M):
        fw, cfg = self.fw, self.cfg
        RGB, Tb, MC = cfg.RGB, B.Tb, cfg.MC
        with fw.scope():
            Wring = self.wpanel("rgax", 0)
            Wax = fw.sb([128, RGB, 256], BF16, "Wax")
            fw.copy("dve", Wax.v, Wring)
            Wa = Wax[:, :, 0:128]
            Wx = Wax[:, :, 128:256]
            xc = fw.sb([128, Tb], F32, "xc")
            xcb = fw.sb([128, Tb], BF16, "xcb")
            ra = fw.sb([128, Tb], F32, "ra")
            gi = fw.sb([128, Tb], F32, "gi")
            aa = fw.sb([128, Tb], F32, "aa")
            uu = fw.sb([128, Tb], F32, "uu")
            hr = fw.sb([128, Tb], F32, "hr")
            sRH = None
            for r in B.regs:
                if r.kind == "s":
                    sRH = fw.sb([128, RGB, r.ns], F32, "sRH")
                    fw.dma("sp", sRH.v, self.din["rgh"].rearrange("(n p) s -> p n s", p=128))
                    sRHo = fw.sb([128, RGB, r.ns], F32, "sRHo")
            for n in range(RGB):
                for r in B.regs:
                    xv = self.pv(xc.v, r)
                    fw.ts("dve", xv, self.xrv(n, r, 0, r.L), self.vc("cw0", n), self.vc("cb", n), ALU.mult, ALU.add)
                    for j in range(1, 4):
                        fw.stt(xv, self.xrv(n, r, j, r.L), self.vc(f"cw{j}", n), xv, ALU.mult, ALU.add)
                fw.copy("act", xcb[:, 0:Tb], xc[:, 0:Tb])
                pa = self.psum()
                fw.mm(pa[:, 0:Tb], Wa[:, n, :], xcb[:, 0:Tb])
                fw.act(ra[:, 0:Tb], pa[:, 0:Tb], AF.Sigmoid, bias=self.vc("ba", n))
                px = self.psum()
                fw.mm(px[:, 0:Tb], Wx[:, n, :], xcb[:, 0:Tb])
                fw.act(gi[:, 0:Tb], px[:, 0:Tb], AF.Sigmoid, bias=self.vc("bx", n))
                fw.act(aa[:, 0:Tb], ra[:, 0:Tb], AF.Exp, scale=self.dv_c1(n))
                fw.act(uu[:, 0:Tb], ra[:, 0:Tb], AF.Exp, scale=self.dv_c2(n))
                fw.ts("dve", uu[:, 0:Tb], uu[:, 0:Tb], -1.0, 1.0, ALU.mult, ALU.add)
                fw.ts("dve", uu[:, 0:Tb], uu[:, 0:Tb], 1e-30, None, ALU.max)
                fw.act(uu[:, 0:Tb], uu[:, 0:Tb], AF.Sqrt)
                fw.tt("dve", gi[:, 0:Tb], gi[:, 0:Tb], xc[:, 0:Tb], ALU.mult)
                fw.tt("dve", uu[:, 0:Tb], uu[:, 0:Tb], gi[:, 0:Tb], ALU.mult)
                for r in B.regs:
                    if r.kind == "p":
                        c = slice(r.c0, r.c0 + r.L)
                        fw.scan(hr[:, c], aa[:, c], uu[:, c], self.pRH[:, n, :], ALU.mult, ALU.add)
                        fw.copy("dve", self.pRH[:, n, :], hr[:, r.c0 + r.L - 1:r.c0 + r.L])
                    else:
                        for s in range(r.ns):
                            c = slice(r.c0 + s * r.L, r.c0 + (s + 1) * r.L)
                            fw.scan(hr[:, c], aa[:, c], uu[:, c], sRH[:, n, s:s + 1], ALU.mult, ALU.add)
                        fw.copy("dve", sRHo[:, n, :], self.pv(hr.v, r)[:, :, r.L - 1])
                fw.tt("dve", M.yr[:, n, :], hr[:, 0:Tb], M.gr[:, n, :], ALU.mult)
                yield
                if n == 0:
                    self.dump("xc", xc.v, [128, Tb]); self.dump("ra", ra.v, [128, Tb]); self.dump("aa", aa.v, [128, Tb])
                    self.dump("uu", uu.v, [128, Tb]); self.dump("hr", hr.v, [128, Tb])
            for r in B.regs:
                if r.kind == "p":
                    fw.copy("dve", self.pRC.v, self.xrv(None, r, r.L, 3))
                else:
                    fw.dma("sp", self.dout["srgh"].rearrange("(n p) s -> p n s", p=128), sRHo.v)
                    for n in range(RGB):
                        fw.dma("sp", self.dout["srgc"][n * 128:(n + 1) * 128], self.xrv(n, r, r.L, 3), join=True)

    def layer1_mixer(self, B):
        fw, cfg = self.fw, self.cfg
        subs, cur, n = [], [], 0
        for t in B.tiles:
            if cur and (n + t.L > 256 or t.kind == "s" or cur[-1].kind == "s"):
                subs.append(cur)
                cur, n = [], 0
            cur.append(t)
            n += t.L
        subs.append(cur)
        for sub in subs:
            self.rwkv_sub(B, sub)
        self.proj_resid(B, "rwo")

    def rwkv_sub(self, B, sub):
        fw, cfg = self.fw, self.cfg
        NC, LW, LA, LG = cfg.NC, cfg.LW, cfg.LA, cfg.LG
        c0 = sub[0].c0
        n = sum(t.L for t in sub)
        kind = sub[0].kind
        reg = [r for r in B.regs if r.kind == kind][0]
        ns = reg.ns
        if kind == "p":
            off = c0 - reg.c0
            Ls = n
            xv = lambda kc, sh=0: B.hres[:, kc, reg.e0 + 1 + off + sh:reg.e0 + 1 + off + sh + n].re("p (s l) -> p s l", s=1)
        else:
            Ls = reg.L
            xv = lambda kc, sh=0: self.hv(kc, reg, sh)
        s3 = lambda v: v.re("p (s l) -> p s l", s=ns)
        CW = 0.6065306597126334
        with fw.scope():
            S = Blk()
            S.sg = fw.sb([128, NC, n], F32, "sg")
            S.a = fw.sb([128, NC, n], BF16, "a")
            S.k = fw.sb([128, NC, n], BF16, "k")
            S.r = fw.sb([128, NC, n], BF16, "r")
            S.v = fw.sb([128, NC, n], BF16, "v")
            S.g = fw.sb([128, NC, n], BF16, "g")
            with fw.scope():
                xsb = [fw.sb([128, NC, n], BF16, f"xs{i}") for i in range(2)]
                tmps = [fw.sb([128, n], F32, f"xstmp{i}") for i in range(2)]
                lo = fw.sb([128, LG // 128, n], BF16, "lora")

                def mk_xs(j, xs):
                    for kc in range(NC):
                        tmp = tmps[kc % 2]
                        fw.act(s3(tmp[:, 0:n]), xv(kc, -1), AF.Identity, scale=self.vc(f"mu{j}", kc))
                        fw.stt(s3(xs[:, kc, :]), xv(kc), self.dv_1mmu(j, kc), s3(tmp[:, 0:n]), ALU.mult, ALU.add)
                        if kc % 2 == 1:
                            yield

                def lora(xs, n1, n2, R, func1, evac2):
                    rhs = lambda kc: xs[:, kc, :]
                    W1 = self.wpanel(n1, 0)
                    for c in range((R + 127) // 128):
                        w = min(128, R - c * 128)
                        ps = self.psum()
                        for kc in range(NC):
                            fw.mm(ps[0:w, 0:n], W1[:, kc, c * 128:c * 128 + w], rhs(kc), start=(kc == 0), stop=(kc == NC - 1))
                        fw.act(lo[0:w, c, :], ps[0:w, 0:n], func1)
                        self.bg_step()
                    W2 = self.wpanel(n2, 0)
                    KC2 = (R + 127) // 128
                    for m in range(NC):
                        ps = self.psum()
                        for c in range(KC2):
                            w = min(128, R - c * 128)
                            fw.mm(ps[:, 0:n], W2[0:w, c, m * 128:(m + 1) * 128], lo[0:w, c, :], start=(c == 0), stop=(c == KC2 - 1))
                        evac2(m, ps)
                        if m % 2 == 1:
                            self.bg_step()
                jobs = [
                    (1, lambda xs: lora(xs, "w1", "w2", LW, AF.Tanh,
                                        lambda m, ps: fw.act(S.sg[:, m, :], ps[:, 0:n], AF.Sigmoid, bias=self.vc("w0", m)))),
                    (4, lambda xs: lora(xs, "a1", "a2", LA, AF.Copy,
                                        lambda m, ps: fw.act(S.a[:, m, :], ps[:, 0:n], AF.Sigmoid, bias=self.vc("a0", m)))),
                    (5, lambda xs: lora(xs, "g1", "g2", LG, AF.Sigmoid,
                                        lambda m, ps: fw.copy("act", S.g[:, m, :], ps[:, 0:n]))),
                ]
                for j_, wn, dst in ((2, "rwk", S.k), (0, "rwr", S.r), (3, "rwv", S.v)):
                    jobs.append((j_, lambda xs, wn=wn, dst=dst: self.proj_fm(
                        wn, lambda kc: xs[:, kc, :], n, lambda m, ps: fw.copy("act", dst[:, m, :], ps[:, 0:n]))))
                for _ in mk_xs(jobs[0][0], xsb[0]):
                    pass
                for idx, (j_, run) in enumerate(jobs):
                    if idx + 1 < len(jobs):
                        self._bg = mk_xs(jobs[idx + 1][0], xsb[(idx + 1) % 2])
                    run(xsb[idx % 2])
                    if self._bg is not None:
                        for _ in self._bg:
                            pass
                        self._bg = None
            self.chk("l1proj")
            self.dump("sg", S.sg.v, [128, NC, n])
            self.dump("rk", S.k.v, [128, NC, n])
            for m0 in range(0, NC, 2):
                self.rwkv_mpair(B, S, sub, m0, c0, n, CW)

    def rwkv_mpair(self, B, S, sub, m0, c0, n, CW):
        fw, cfg = self.fw, self.cfg
        NC = cfg.NC
        blkf = self.cc("blk")
        with fw.scope():
            FT = fw.sb([128, 2, 4, n], BF16, "FT")
            FTp = fw.sb([128, 2, 2, 2, n], BF16, "FTp")
            self.FTp = FTp
            Ep = fw.sb([128, 2, n], F32, "Ep")
            bonus = fw.sb([128, 2, n], F32, "bonus")
            with fw.scope():
                cs = fw.sb([128, n], F32, "cs")
                Em = fw.sb([128, n], F32, "Em")
                Epm = fw.sb([128, n], F32, "Epm")
                kkp = fw.sb([128, n], F32, "kkp")
                sq = fw.sb([128, n], F32, "sq")
                rn = fw.sb([128, n], F32, "rn")
                tt_ = fw.sb([128, n], F32, "tt")
                km = fw.sb([128, n], F32, "km")
                for ml in range(2):
                    m = m0 + ml
                    sg = S.sg[:, m, :]
                    for t in sub:
                        tc = slice(t.c0 - c0, t.c0 - c0 + t.L)
                        fw.scan(cs[:, tc], self.cc(t.kind + "_rmask", n=t.L), sg[:, tc], 0.0, ALU.mult, ALU.add)
                    fw.act(Ep[:, ml, :], cs[:, 0:n], AF.Exp, scale=-CW)
                    fw.act(Em[:, 0:n], cs[:, 0:n], AF.Exp, scale=CW)
                    fw.tt("dve", Epm[:, 0:n], cs[:, 0:n], sg, ALU.subtract)
                    fw.act(Epm[:, 0:n], Epm[:, 0:n], AF.Exp, scale=-CW)
                    fw.ts("dve", kkp[:, 0:n], S.k[:, m, :], self.vc("kk", m), None, ALU.mult)
                    fw.act(sq[:, 0:n], kkp[:, 0:n], AF.Square)
                    ps = self.psum()
                    fw.mm(ps[:, 0:n], blkf, sq[:, 0:n])
                    fw.ts("dve", rn[:, 0:n], ps[:, 0:n], 1e-18, None, ALU.max)
                    fw.act(rn[:, 0:n], rn[:, 0:n], AF.Ln)
                    fw.act(rn[:, 0:n], rn[:, 0:n], AF.Exp, scale=-0.5)
                    fw.tt("dve", kkp[:, 0:n], kkp[:, 0:n], rn[:, 0:n], ALU.mult)
                    fw.ts("dve", tt_[:, 0:n], S.a[:, m, :], self.vc("ka", m), self.dv_1mka(m), ALU.mult, ALU.add)
                    fw.tt("dve", km[:, 0:n], tt_[:, 0:n], S.k[:, m, :], ALU.mult)
                    fw.stt(FT[:, ml, 0, :], kkp[:, 0:n], -1.0, Epm[:, 0:n], ALU.mult, ALU.mult)
                    fw.tt("dve", FT[:, ml, 1, :], S.r[:, m, :], Ep[:, ml, :], ALU.mult)
                    fw.tt("dve", tt_[:, 0:n], kkp[:, 0:n], S.a[:, m, :], ALU.mult)
                    fw.tt("dve", FT[:, ml, 2, :], tt_[:, 0:n], Em[:, 0:n], ALU.mult)
                    fw.tt("dve", FT[:, ml, 3, :], km[:, 0:n], Em[:, 0:n], ALU.mult)
                    for h2 in range(2):
                        for i_, src_ in enumerate((2, 3)):
                            fw.act(FTp[:, ml, h2, i_, :], FT[:, ml, src_, :], AF.Identity, scale=self.cc("hm")[:, h2:h2 + 1])
                    fw.tt("dve", tt_[:, 0:n], km[:, 0:n], S.r[:, m, :], ALU.mult)
                    fw.ts("dve", sq[:, 0:n], tt_[:, 0:n], self.vc("rk", m), None, ALU.mult)
                    ps = self.psum()
                    fw.mm(ps[:, 0:n], blkf, sq[:, 0:n])
                    fw.tt("dve", bonus[:, ml, :], ps[:, 0:n], S.v[:, m, :], ALU.mult)
            self.chk("l1prep")
            for t in sub:
                with fw.scope():
                    gens = [self.rwkv_chain(B, S, FT, Ep, bonus, t, m0, ml, c0) for ml in range(2)]
                    live = list(gens)
                    while live:
                        for g in list(live):
                            try:
                                next(g)
                            except StopIteration:
                                live.remove(g)
                self.chk("l1tile")

    def rwkv_tile(self, B, S, FT, Ep, bonus, t, m0, c0):
        fw, cfg = self.fw, self.cfg
        L, ns, ls, kind, nsteps = t.L, t.ns, t.ls, t.kind, t.nsteps
        tc = slice(t.c0 - c0, t.c0 - c0 + L)
        bc = slice(t.c0, t.c0 + L)
        hm = self.cc("hm")
        blkf = self.cc("blk")
        psD = self.psD
        bankbf = lambda b: V(b, b.ap.bitcast(BF16))
        with fw.scope():
            tok4 = fw.sb([128, 2, 4, 128], BF16, "tok4")
            BKp = fw.sb([128, 4, 2, 128], BF16, "BKp")
            AT4 = fw.sb([128, 4, 4, 128], BF16, "AT4")
            PX = [fw.sb([128, 4, 256], BF16, f"PX{i}") for i in range(2)]
            Qb = [fw.sb([128, 4, 128], BF16, f"Qb{i}") for i in range(2)]
            X32 = fw.sb([128, 4, 128], F32, "X32")
            AH = fw.sb([128, 4, 64], BF16, "AH")
            ATp = fw.sb([128, 4, 128], BF16, "ATp")
            Ubf = fw.sb([128, 4, 128], BF16, "UV")
            Hbf = fw.sb([128, 2, ns, 64], BF16, "Hbf")
            Hpad = fw.sb([128, 4, ns, 64], BF16, "Hpad")
            yt = fw.sb([128, 128], F32, "yt")
            ysq = fw.sb([128, 128], F32, "ysq")
            yr = fw.sb([128, 128], F32, "yr")
            if kind == "s":
                H32t = fw.sb([128, 2, ns, 64], F32, "H32s")
                fw.dma("sp", H32t.v, self.din["rwS"][m0:m0 + 2].rearrange("m p s v -> p m s v"))
                H32 = H32t.v
                AM = [fw.sb([128, ns, 128], BF16, f"AM{i}") for i in range(2)]
                UVm = [fw.sb([128, ns, 128], BF16, f"UVm{i}") for i in range(2)]
            else:
                H32 = self.pH[:, m0:m0 + 2]
            fw.copy("act", Hbf.v, H32)
            for hh in range(4):
                fw.act(Hpad[:, hh], H32[:, hh // 2], AF.Identity, scale=hm[:, hh % 2:hh % 2 + 1])
            for ml in range(2):
                pb = bankbf(self.psum())
                for i, src in enumerate((FT[:, ml, 0, tc], FT[:, ml, 2, tc], FT[:, ml, 3, tc], S.v[:, m0 + ml, tc])):
                    fw.tr(pb[0:L, i * 128:(i + 1) * 128], src, self.identb.v)
                fw.copy("act", tok4[0:L, ml], pb[0:L, 0:512].re("p (a k) -> p a k", a=4))
            self.chk("t_tr")
            for hh in range(4):
                ml, h2 = hh // 2, hh % 2
                for i, src in enumerate((2, 3)):
                    fw.act(BKp[:, hh, i, 0:L], FT[:, ml, src, tc], AF.Identity, scale=hm[:, h2:h2 + 1])
            self.chk("t_pad")
            pP = self.psum()
            for hh in range(4):
                ml = hh // 2
                bk = self.bank[4 + hh]
                rhsAR = FT[:, ml, 0:2, tc]
                fw.mm(bk[0:L, 0:2 * L].re("p (a l) -> p a l", a=2), BKp[:, hh, 0, 0:L], rhsAR)
                fw.mm(bk[0:L, 2 * L:4 * L].re("p (a l) -> p a l", a=2), BKp[:, hh, 1, 0:L], rhsAR)
                fw.mm(pP[0:L, hh * 128:hh * 128 + L], FT[:, ml, 0, tc], BKp[:, hh, 0, 0:L])
            m2 = self.cc(kind + "_m2").re("p (a l) -> p a l", a=2)[0:L, :, 0:L]
            for rep in range(2):
                src = psD[0:L, :, rep * 2 * L:(rep + 1) * 2 * L].re("p h (a l) -> p h a l", a=2)
                fw.tt("dve", AT4[0:L, :, 2 * rep:2 * rep + 2, 0:L], src, m2.un(1).bc([L, 4, 2, L]), ALU.mult)
            fw.tt("dve", PX[0][0:L, :, 0:L], pP[0:L, 0:512].re("p (h l) -> p h l", h=4)[:, :, 0:L],
                  self.cc(kind + "_strictT", rows=L, n=L).un(1).bc([L, 4, L]), ALU.mult)
            self.chk("t_A")
            pZ = self.psum()
            for hh in range(4):
                ml, h2 = hh // 2, hh % 2
                fw.mm(pZ[0:L, hh * 64:(hh + 1) * 64], AT4[0:L, hh, 2, 0:L], tok4[0:L, ml, 3, h2 * 64:(h2 + 1) * 64])
            fw.copy("act", X32[0:L, :, 64:128], pZ[0:L, 0:256].re("p (h v) -> p h v", h=4))
            for ml in range(2):
                fw.copy("dve", X32[0:L, 2 * ml:2 * ml + 2, 0:64], tok4[0:L, ml, 0, :].re("p (h k) -> p h k", h=2))
            fw.copy("act", PX[0][0:L, :, L:L + 128], X32[0:L])
            self.chk("t_Z")
            for i in range(nsteps):
                cur, nxt = i % 2, (i + 1) % 2
                last = (i == nsteps - 1)
                for hh in range(4):
                    Qi = AT4[0:L, hh, 0, 0:L] if i == 0 else Qb[cur][0:L, hh, 0:L]
                    bk = self.bank[4 + hh]
                    if not last:
                        fw.mm(bk[0:L, 0:L + 128], Qi, PX[cur][0:L, hh, 0:L + 128])
                        fw.mm(bk[0:L, 256:256 + L], PX[cur][0:L, hh, 0:L], Qi)
                    else:
                        fw.mm(bk[0:L, L:L + 128], Qi, PX[cur][0:L, hh, L:L + 128])
                self.chk(f"d_mm{i}")
                for half in range(2):
                    hs = slice(2 * half, 2 * half + 2)
                    pv_ = V(self.bank[4 + 2 * half], psD.ap[0:L, hs, :], ts=self.bank[4 + 2 * half:6 + 2 * half])
                    if not last:
                        for hh in (2 * half, 2 * half + 1):
                            qsrc = self.bank[4 + hh][0:L, 256:256 + L]
                            fw.op("act", lambda: self.nc.scalar.copy(Qb[nxt][0:L, hh, 0:L].ap, qsrc.ap), [qsrc], [Qb[nxt][0:L, hh, 0:L], qsrc])
                        fw.copy("dve", PX[nxt][0:L, hs, 0:L], pv_[:, :, 0:L])
                        fw.tt("dve", PX[nxt][0:L, hs, L:L + 128], X32[0:L, hs, :], pv_[:, :, L:L + 128], ALU.add)
                    fw.tt("dve", X32[0:L, hs, :], X32[0:L, hs, :], pv_[:, :, L:L + 128], ALU.add)
                self.chk(f"d_end{i}")
            self.chk("t_dbl")
            fw.copy("dve", AH[0:L], X32[0:L, :, 0:64])
            for ml in range(2):
                pb = bankbf(self.psum())
                fw.tr(pb[:, 0:L], AH[0:L, 2 * ml:2 * ml + 2, :].re("p h k -> p (h k)"), self.identb[0:L, 0:L])
                for h2 in range(2):
                    fw.ts("dve", ATp[:, 2 * ml + h2, 0:L], pb[:, 0:L], hm[:, h2:h2 + 1], None, ALU.mult)
            self.chk("t_AT")
            pU = self.psum()
            for hh in range(4):
                ml = hh // 2
                if ns == 1:
                    fw.mm(pU[0:L, hh * 64:(hh + 1) * 64], ATp[:, hh, 0:L], Hbf[:, ml, 0, :])
                else:
                    am = AM[hh % 2]
                    fw.tt("dve", am[:, :, 0:L], ATp[:, hh, 0:L].un(1).bc([128, ns, L]),
                          self.colmb.v.re("p (s t) -> p s t", s=ns)[:, :, 0:L], ALU.mult)
                    for s in range(ns):
                        fw.mm(pU[0:L, hh * 64:(hh + 1) * 64], am[:, s, 0:L], Hbf[:, ml, s, :], start=(s == 0), stop=(s == ns - 1))
            fw.tt("dve", Ubf[0:L, :, 0:64], pU[0:L, 0:256].re("p (h v) -> p h v", h=4), X32[0:L, :, 64:128], ALU.add)
            for ml in range(2):
                fw.copy("act", Ubf[0:L, 2 * ml:2 * ml + 2, 64:128], tok4[0:L, ml, 3, :].re("p (h v) -> p h v", h=2))
            self.chk("t_U")
            pY = [self.psum(), self.psum()]
            for hh in range(4):
                ml, h2 = hh // 2, hh % 2
                out = pY[ml][64 * h2:64 * h2 + 64, 0:L]
                fw.mm(out, Ubf[0:L, hh, 0:64], AT4[0:L, hh, 1, 0:L], start=True, stop=False)
                fw.mm(out, Ubf[0:L, hh, 64:128], AT4[0:L, hh, 3, 0:L], start=False, stop=False)
                for s in range(ns):
                    sc = slice(s * ls, (s + 1) * ls)
                    fw.mm(pY[ml][64 * h2:64 * h2 + 64, sc], Hpad[:, hh, s, :], FT[:, ml, 1, tc][:, sc], start=False, stop=(s == ns - 1))
            self.chk("t_Y")
            for hh in range(4):
                ml, h2 = hh // 2, hh % 2
                if ns > 1:
                    uvm = UVm[hh % 2]
                    fw.tt("dve", uvm[0:L], Ubf[0:L, hh, :].un(1).bc([L, ns, 128]),
                          self.cc(kind + "_rowmask", rows=L, n=ns).un(2).bc([L, ns, 128]), ALU.mult)
                for s in range(ns):
                    slot = ml * ns + s
                    bk = self.bank[4 + slot // 8]
                    out = bk[64 * h2:64 * h2 + 64, (slot % 8) * 64:(slot % 8 + 1) * 64]
                    rU = Ubf[0:L, hh, 0:64] if ns == 1 else uvm[0:L, s, 0:64]
                    rV = Ubf[0:L, hh, 64:128] if ns == 1 else uvm[0:L, s, 64:128]
                    fw.mm(out, tok4[0:L, ml, 1, h2 * 64:(h2 + 1) * 64], rU, start=True, stop=False)
                    fw.mm(out, tok4[0:L, ml, 2, h2 * 64:(h2 + 1) * 64], rV, start=False, stop=True)
            nslot = 2 * ns
            for b in range((nslot + 7) // 8):
                w = min(8, nslot - b * 8)
                hv_ = H32.re("p m s v -> p (m s) v")[:, b * 8:b * 8 + w, :]
                fw.tt("dve", hv_, hv_, self.bank[4 + b][:, 0:w * 64].re("p (a v) -> p a v", a=w), ALU.add)
            for ml in range(2):
                wl = Ep[:, ml, tc].re("p (s l) -> p s l", s=ns)[:, :, ls - 1:ls].bc([128, ns, 64])
                fw.tt("dve", H32[:, ml], H32[:, ml], wl, ALU.mult)
            if kind == "s":
                fw.dma("sp", self.dout["srwS"][m0:m0 + 2].rearrange("m p s v -> p m s v"), H32)
            self.chk("t_H")
            for ml in range(2):
                m = m0 + ml
                fw.copy("act", yt[:, 0:L], pY[ml][:, 0:L])
                p1 = self.psum()
                fw.mm(p1[:, 0:L], blkf, yt[:, 0:L])
                fw.stt(yt[:, 0:L], p1[:, 0:L], -1.0 / 64, yt[:, 0:L], ALU.mult, ALU.add)
                fw.act(ysq[:, 0:L], yt[:, 0:L], AF.Square)
                p2 = self.psum()
                fw.mm(p2[:, 0:L], blkf, ysq[:, 0:L])
                fw.act(yr[:, 0:L], p2[:, 0:L], AF.Sqrt, bias=self.epsc("gn"), scale=1.0 / 64)
                fw.recip(yr[:, 0:L], yr[:, 0:L])
                fw.tt("dve", yt[:, 0:L], yt[:, 0:L], yr[:, 0:L], ALU.mult)
                fw.ts("dve", yt[:, 0:L], yt[:, 0:L], self.vc("lxg", m), self.vc("lxb", m), ALU.mult, ALU.add)
                fw.tt("dve", yt[:, 0:L], yt[:, 0:L], bonus[:, ml, tc], ALU.add)
                fw.tt("dve", B.abuf[:, m, bc], yt[:, 0:L], S.g[:, m, tc], ALU.mult)


    def rwkv_chain(self, B, S, FT, Ep, bonus, t, m0, ml, c0):
        fw, cfg = self.fw, self.cfg
        L, ns, ls, kind, nsteps = t.L, t.ns, t.ls, t.kind, t.nsteps
        tc = slice(t.c0 - c0, t.c0 - c0 + L)
        bc = slice(t.c0, t.c0 + L)
        hm = self.cc("hm")
        blkf = self.cc("blk")
        m = m0 + ml
        D0, D1 = self.bank[4 + 2 * ml], self.bank[5 + 2 * ml]
        Db = (D0, D1)
        psD2 = V(D0, self.psD.ap[:, 2 * ml:2 * ml + 2, :], ts=[D0, D1])
        pool = (self.bank[2 * ml], self.bank[2 * ml + 1])
        pi = [0]

        def nextp():
            pi[0] += 1
            return pool[pi[0] % 2]
        bankbf = lambda b: V(b, b.ap.bitcast(BF16))
        if True:
            tok4 = fw.sb([128, 4, 128], BF16, "tok4")
            BKp = self.FTp[:, ml, :, :, tc]
            AT4 = fw.sb([128, 2, 4, 128], BF16, "AT4")
            PX = [fw.sb([128, 2, 256], BF16, f"PX{i}") for i in range(2)]
            Qb = [fw.sb([128, 2, 128], BF16, f"Qb{i}") for i in range(2)]
            X32 = fw.sb([128, 2, 128], F32, "X32")
            AH = fw.sb([128, 2, 64], BF16, "AH")
            ATp = fw.sb([128, 2, 128], BF16, "ATp")
            Ubf = fw.sb([128, 2, 128], BF16, "UV")
            Hbf = fw.sb([128, ns, 64], BF16, "Hbf")
            Hpad = fw.sb([128, 2, ns, 64], BF16, "Hpad")
            yt = fw.sb([128, 128], F32, "yt")
            ysq = fw.sb([128, 128], F32, "ysq")
            yr = fw.sb([128, 128], F32, "yr")
            if kind == "s":
                H32t = fw.sb([128, ns, 64], F32, "H32s")
                fw.dma("sp", H32t.v, self.din["rwS"][m].rearrange("p s v -> p s v"))
                H32 = H32t.v
                AM = fw.sb([128, ns, 128], BF16, "AM")
                UVm = fw.sb([128, ns, 128], BF16, "UVm")
            else:
                H32 = self.pH[:, m]
            fw.copy("act", Hbf.v, H32)
            for h2 in range(2):
                fw.act(Hpad[:, h2], H32, AF.Identity, scale=hm[:, h2:h2 + 1])
            pb = bankbf(nextp())
            for i, src in enumerate((FT[:, ml, 0, tc], FT[:, ml, 2, tc], FT[:, ml, 3, tc], S.v[:, m, tc])):
                fw.tr(pb[0:L, i * 128:(i + 1) * 128], src, self.identb.v)
            fw.copy("act", tok4[0:L], pb[0:L, 0:512].re("p (a k) -> p a k", a=4))
            yield
            pP = nextp()
            rhsAR = FT[:, ml, 0:2, tc]
            for h2 in range(2):
                bk = Db[h2]
                fw.mm(bk[0:L, 0:2 * L].re("p (a l) -> p a l", a=2), BKp[:, h2, 0, 0:L], rhsAR)
                fw.mm(bk[0:L, 2 * L:4 * L].re("p (a l) -> p a l", a=2), BKp[:, h2, 1, 0:L], rhsAR)
                fw.mm(pP[0:L, h2 * 128:h2 * 128 + L], FT[:, ml, 0, tc], BKp[:, h2, 0, 0:L])
            m2 = self.cc(kind + "_m2").re("p (a l) -> p a l", a=2)[0:L, :, 0:L]
            for rep in range(2):
                src = psD2[0:L, :, rep * 2 * L:(rep + 1) * 2 * L].re("p h (a l) -> p h a l", a=2)
                fw.tt("dve", AT4[0:L, :, 2 * rep:2 * rep + 2, 0:L], src, m2.un(1).bc([L, 2, 2, L]), ALU.mult)
            fw.tt("dve", PX[0][0:L, :, 0:L], pP[0:L, 0:256].re("p (h l) -> p h l", h=2)[:, :, 0:L],
                  self.cc(kind + "_strictT", rows=L, n=L).un(1).bc([L, 2, L]), ALU.mult)
            yield
            pZ = nextp()
            for h2 in range(2):
                fw.mm(pZ[0:L, h2 * 64:(h2 + 1) * 64], AT4[0:L, h2, 2, 0:L], tok4[0:L, 3, h2 * 64:(h2 + 1) * 64])
            fw.copy("act", X32[0:L, :, 64:128], pZ[0:L, 0:128].re("p (h v) -> p h v", h=2))
            fw.copy("dve", X32[0:L, :, 0:64], tok4[0:L, 0, :].re("p (h k) -> p h k", h=2))
            fw.copy("act", PX[0][0:L, :, L:L + 128], X32[0:L])
            yield
            for i in range(nsteps):
                cur, nxt = i % 2, (i + 1) % 2
                last = (i == nsteps - 1)
                for h2 in range(2):
                    Qi = AT4[0:L, h2, 0, 0:L] if i == 0 else Qb[cur][0:L, h2, 0:L]
                    bk = Db[h2]
                    if not last:
                        fw.mm(bk[0:L, 0:L + 128], Qi, PX[cur][0:L, h2, 0:L + 128])
                        fw.mm(bk[0:L, 256:256 + L], PX[cur][0:L, h2, 0:L], Qi)
                    else:
                        fw.mm(bk[0:L, L:L + 128], Qi, PX[cur][0:L, h2, L:L + 128])
                if not last:
                    for h2 in range(2):
                        qsrc = Db[h2][0:L, 256:256 + L]
                        qdst = Qb[nxt][0:L, h2, 0:L]
                        fw.op("act", lambda qdst=qdst, qsrc=qsrc: self.nc.scalar.copy(qdst.ap, qsrc.ap), [qsrc], [qdst, qsrc])
                    fw.copy("dve", PX[nxt][0:L, :, 0:L], psD2[0:L, :, 0:L])
                    fw.tt("dve", PX[nxt][0:L, :, L:L + 128], PX[cur][0:L, :, L:L + 128], psD2[0:L, :, L:L + 128], ALU.add)
                else:
                    fw.tt("dve", X32[0:L], PX[cur][0:L, :, L:L + 128], psD2[0:L, :, L:L + 128], ALU.add)
                yield
            fw.copy("dve", AH[0:L], X32[0:L, :, 0:64])
            pb = bankbf(nextp())
            fw.tr(pb[:, 0:L], AH[0:L].re("p h k -> p (h k)"), self.identb[0:L, 0:L])
            for h2 in range(2):
                fw.act(ATp[:, h2, 0:L], pb[:, 0:L], AF.Identity, scale=hm[:, h2:h2 + 1])
            yield
            pU = nextp()
            for h2 in range(2):
                if ns == 1:
                    fw.mm(pU[0:L, h2 * 64:(h2 + 1) * 64], ATp[:, h2, 0:L], Hbf[:, 0, :])
                else:
                    fw.tt("dve", AM[:, :, 0:L], ATp[:, h2, 0:L].un(1).bc([128, ns, L]),
                          self.colmb.v.re("p (s t) -> p s t", s=ns)[:, :, 0:L], ALU.mult)
                    for s in range(ns):
                        fw.mm(pU[0:L, h2 * 64:(h2 + 1) * 64], AM[:, s, 0:L], Hbf[:, s, :], start=(s == 0), stop=(s == ns - 1))
            fw.tt("dve", Ubf[0:L, :, 0:64], pU[0:L, 0:128].re("p (h v) -> p h v", h=2), X32[0:L, :, 64:128], ALU.add)
            fw.copy("act", Ubf[0:L, :, 64:128], tok4[0:L, 3, :].re("p (h v) -> p h v", h=2))
            yield
            pY = nextp()
            for h2 in range(2):
                out = pY[64 * h2:64 * h2 + 64, 0:L]
                fw.mm(out, Ubf[0:L, h2, 0:64], AT4[0:L, h2, 1, 0:L], start=True, stop=False)
                fw.mm(out, Ubf[0:L, h2, 64:128], AT4[0:L, h2, 3, 0:L], start=False, stop=False)
                for s in range(ns):
                    sc = slice(s * ls, (s + 1) * ls)
                    fw.mm(pY[64 * h2:64 * h2 + 64, sc], Hpad[:, h2, s, :], FT[:, ml, 1, tc][:, sc], start=False, stop=(s == ns - 1))
            yield
            for h2 in range(2):
                if ns > 1:
                    fw.tt("dve", UVm[0:L], Ubf[0:L, h2, :].un(1).bc([L, ns, 128]),
                          self.cc(kind + "_rowmask", rows=L, n=ns).un(2).bc([L, ns, 128]), ALU.mult)
                for s in range(ns):
                    bk = Db[s // 8]
                    out = bk[64 * h2:64 * h2 + 64, (s % 8) * 64:(s % 8 + 1) * 64]
                    rU = Ubf[0:L, h2, 0:64] if ns == 1 else UVm[0:L, s, 0:64]
                    rV = Ubf[0:L, h2, 64:128] if ns == 1 else UVm[0:L, s, 64:128]
                    fw.mm(out, tok4[0:L, 1, h2 * 64:(h2 + 1) * 64], rU, start=True, stop=False)
                    fw.mm(out, tok4[0:L, 2, h2 * 64:(h2 + 1) * 64], rV, start=False, stop=True)
            for b in range((ns + 7) // 8):
                w = min(8, ns - b * 8)
                hv_ = H32[:, b * 8:b * 8 + w, :]
                fw.tt("dve", hv_, hv_, Db[b][:, 0:w * 64].re("p (a v) -> p a v", a=w), ALU.add)
            wl = Ep[:, ml, tc].re("p (s l) -> p s l", s=ns)[:, :, ls - 1:ls].bc([128, ns, 64])
            fw.tt("dve", H32, H32, wl, ALU.mult)
            if kind == "s":
                fw.dma("sp", self.dout["srwS"][m].rearrange("p s v -> p s v"), H32)
            yield
            fw.copy("act", yt[:, 0:L], pY[:, 0:L])
            p1 = nextp()
            fw.mm(p1[:, 0:L], blkf, yt[:, 0:L])
            fw.stt(yt[:, 0:L], p1[:, 0:L], -1.0 / 64, yt[:, 0:L], ALU.mult, ALU.add)
            fw.act(ysq[:, 0:L], yt[:, 0:L], AF.Square)
            p2 = nextp()
            fw.mm(p2[:, 0:L], blkf, ysq[:, 0:L])
            fw.act(yr[:, 0:L], p2[:, 0:L], AF.Ln, bias=self.epsc("gn"), scale=1.0 / 64)
            fw.act(yr[:, 0:L], yr[:, 0:L], AF.Exp, scale=-0.5)
            fw.tt("dve", yt[:, 0:L], yt[:, 0:L], yr[:, 0:L], ALU.mult)
            fw.act(yt[:, 0:L], yt[:, 0:L], AF.Identity, bias=self.vc("lxb", m), scale=self.vc("lxg", m))
            fw.tt("dve", yt[:, 0:L], yt[:, 0:L], bonus[:, ml, tc], ALU.add)
            fw.tt("dve", B.abuf[:, m, bc], yt[:, 0:L], S.g[:, m, tc], ALU.mult)
            yield


def build_program(cfg, debug=()):
    nc0 = bass.Bass("TRN2", target_bir_lowering=False)
    p0 = Prog(nc0, cfg, plan=None, debug=debug)
    with nc0.allow_non_contiguous_dma(reason="small strided state/vector transfers"):
        p0.build()
    nc = bass.Bass("TRN2", target_bir_lowering=False)
    p = Prog(nc, cfg, plan=p0.plan, debug=debug)
    with nc.allow_non_contiguous_dma(reason="small strided state/vector transfers"):
        p.build()
    return nc, p


def prep_shared(cfg, P):
    cst, _, colm = make_consts(cfg)
    m = {"cst": cst, "colm": colm, "vec": make_vecs(cfg, P)}
    m.update(host_weights(cfg, P))
    bif = np.asarray(P["b_if_ab"][0], np.float32)
    m["bif"] = np.ascontiguousarray(np.stack([bif[:cfg.NHA], bif[cfg.NHA:]], 1))
    return m


def prep_core(cfg, P, xp_seq, sl):
    D, NS, NC = cfg.D, cfg.NS, cfg.NC
    f = lambda a: np.asarray(a, np.float32)
    m = {}
    xfull = np.concatenate([f(P["meta_tokens"]), f(xp_seq)], 0)
    m["xp"] = np.ascontiguousarray(xfull.T)
    xs = f(P["x_sample"])[sl]
    m["xs"] = np.ascontiguousarray(xs.reshape(NS * cfg.LS, D).T)
    C = f(P["state_mlstm_C"])[0, sl]
    n = f(P["state_mlstm_n"])[0, sl]
    m["cext"] = np.ascontiguousarray(np.concatenate([C, n[..., None]], -1))
    m["m0"] = np.ascontiguousarray(f(P["state_mlstm_m"])[0, sl].T)
    m["rgh"] = np.ascontiguousarray(f(P["state_rglru_h"])[0, sl].T)
    m["rgc"] = np.ascontiguousarray(f(P["state_rglru_conv"])[0, sl].transpose(2, 0, 1))
    S = f(P["state_rwkv_S"])[0, sl]
    H = S.transpose(0, 1, 3, 2).reshape(NS, NC, 2, 64, 64)
    m["rwS"] = np.ascontiguousarray(H.transpose(1, 2, 3, 0, 4).reshape(NC, 128, NS, 64))
    m["rwx"] = np.ascontiguousarray(f(P["state_rwkv_shift"])[0, sl].T)
    return m


def unpack_core(cfg, r):
    NS, NC, NHA, HDA = cfg.NS, cfg.NC, cfg.NHA, cfg.HDA
    o = {}
    o["yp"] = r["yp"].T[16:]
    o["ys"] = r["ys"].T.reshape(NS, cfg.LS, cfg.D)
    for pre, nb in (("p", 1), ("s", NS)):
        c = r[pre + "c"]
        o[pre + "C"] = c[..., :HDA]
        o[pre + "n"] = c[..., HDA]
        o[pre + "m"] = r[pre + "m"].T
        o[pre + "rgh"] = r[pre + "rgh"].T
        o[pre + "rgc"] = r[pre + "rgc"].transpose(1, 2, 0)
        H = r[pre + "rwS"].reshape(NC, 2, 64, nb, 64)
        o[pre + "S"] = H.transpose(3, 0, 1, 4, 2).reshape(nb, NC * 2, 64, 64)
        o[pre + "x"] = r[pre + "rwx"].T
    return o


_CACHE = {}


def kernel(**inputs):
    cfg = Cfg()
    P = inputs
    if "prog" not in _CACHE:
        _CACHE["prog"] = build_program(cfg)
    nc, prog = _CACHE["prog"]
    shared = prep_shared(cfg, P)
    in_maps = []
    for c in range(8):
        m = dict(shared)
        m.update(prep_core(cfg, P, np.asarray(P["x_prompt"])[c // 2], slice(c * cfg.NS, (c + 1) * cfg.NS)))
        in_maps.append(m)
    res = run_bass_kernel_spmd(nc, in_maps, core_ids=list(range(8)))
    outs = [unpack_core(cfg, r) for r in res.results]
    cat = lambda k, cores: np.ascontiguousarray(np.concatenate([outs[c][k] for c in cores], 0)).astype(np.float32)
    pc = [0, 2, 4, 6]
    ac = list(range(8))
    yp = np.stack([outs[c]["yp"] for c in pc], 0).astype(np.float32)
    ys = cat("ys", ac)
    res_t = [yp, ys]
    for pre, cores in (("p", pc), ("s", ac)):
        for k in ("C", "n", "m", "rgh", "rgc", "S", "x"):
            res_t.append(cat(pre + k, cores)[None])
    return tuple(res_t)
```

```python
import os
import numpy as np
import concourse.bass as bass
import concourse.mybir as mybir
from concourse.bass_utils import run_bass_kernel_spmd

F32 = mybir.dt.float32
BF16 = mybir.dt.bfloat16
AF = mybir.ActivationFunctionType
ALU = mybir.AluOpType
AX = mybir.AxisListType

LN_EPS = 1e-5
GN_EPS = 64e-5
NEG = -30000.0


class Cfg:
    def __init__(self, D=2048, NHA=4, nchunks=16, NS=16, LS=8, DEPTH=2, lora_w=96, lora_a=96, lora_g=256):
        self.D = D
        self.NC = D // 128
        self.MIXA = D // 2
        self.NHA = NHA
        self.HDA = self.MIXA // NHA
        self.HKC = self.HDA // 128
        self.MC = self.MIXA // 128
        self.RGW = D // 2
        self.RGB = self.RGW // 128
        self.NHC = D // 64
        self.DFF = 4 * D
        self.FC = self.DFF // 128
        self.nchunks = nchunks
        self.TP = 16 + 128 * nchunks
        self.NS = NS
        self.LS = LS
        self.TS = NS * LS
        self.LW, self.LA, self.LG = lora_w, lora_a, lora_g
        self.ALPHA = (2.0 * DEPTH) ** 0.25
        tiles = [("p", 0, 16)] + [("p", 16 + 128 * i, 128) for i in range(nchunks)]
        blocks = []
        cur = []
        n = 0
        for t in tiles:
            if n + t[2] > 512:
                blocks.append(cur)
                cur, n = [], 0
            cur.append(t)
            n += t[2]
        if n + self.TS > 512:
            blocks.append(cur)
            cur = []
        cur.append(("s", 0, self.TS))
        blocks.append(cur)
        self.blocks = blocks


class T:
    __slots__ = ("ap", "name", "w", "r", "dsem", "dcount")

    def __init__(self, ap, name):
        self.ap, self.name = ap, name
        self.w, self.r = {}, {}
        self.dsem, self.dcount = None, 0

    def __getitem__(self, k):
        return V(self, self.ap[k])

    @property
    def v(self):
        return V(self, self.ap)


class V:
    __slots__ = ("t", "ap", "ts")

    def __init__(self, t, ap, ts=None):
        self.t, self.ap, self.ts = t, ap, ts

    def __getitem__(self, k):
        return V(self.t, self.ap[k], self.ts)

    def bc(self, shape):
        return V(self.t, self.ap.to_broadcast(list(shape)), self.ts)

    def re(self, pat, **kw):
        return V(self.t, self.ap.rearrange(pat, **kw), self.ts)

    def un(self, axis):
        return V(self.t, self.ap.unsqueeze(axis), self.ts)

    def tiles(self):
        return self.ts if self.ts is not None else [self.t]


def _ap(x):
    return x.ap if isinstance(x, V) else x


class Fw:
    ENG = ("pe", "dve", "act", "pool", "sp")

    def __init__(self, nc, dry=False):
        self.nc, self.dry = nc, dry
        self.eng = {"pe": nc.tensor, "dve": nc.vector, "act": nc.scalar, "pool": nc.gpsimd, "sp": nc.sync}
        self.cnt = {e: 0 for e in self.ENG}
        self.waited = {e: {} for e in self.ENG}
        self.dsems = {}
        self.sem = {}
        if not dry:
            self.sem = {e: nc.alloc_semaphore("s_" + e) for e in self.ENG}
        self.n_inst = 0
        self._uid = 0
        self.dma_keys = {}
        self.free_dsems = []
        self.scopes = []
        self.freed = {}
        self.min_free = 1 << 30

    def scope(self):
        fw = self

        class _S:
            def __enter__(s):
                s.tiles = []
                s.guards = []
                fw.scopes.append(s)
                return s

            def __exit__(s, *a):
                fw.scopes.pop()
                for t in s.tiles:
                    for d in (t.w, t.r):
                        for k, v in d.items():
                            if fw.freed.get(k, 0) < v:
                                fw.freed[k] = v
                    if t.dsem is not None:
                        fw.free_dsems.append(t.dsem)
                for g in reversed(s.guards):
                    g.__exit__(None, None, None)
                return False
        return _S()

    def sb(self, shape, dtype=F32, name=None):
        self._uid += 1
        name = (name or "sb") + f"_{self._uid}"
        if self.scopes:
            g = self.nc.sbuf_tensor(name, list(shape), dtype)
            h = g.__enter__()
            t = T(h.ap(), name)
            t.w = dict(self.freed)
            self.scopes[-1].tiles.append(t)
            self.scopes[-1].guards.append(g)
        else:
            t = T(self.nc.alloc_sbuf_tensor(name, list(shape), dtype).ap(), name)
        self.min_free = min(self.min_free, self.nc.sbuf_bytes_remaining)
        return t

    def ps(self, shape, dtype=F32, name=None):
        self._uid += 1
        name = (name or "ps") + f"_{self._uid}"
        return self.nc.alloc_psum_tensor(name, list(shape), dtype).ap()

    def _collect(self, reads, writes, eng, skip=None):
        deps = {}
        for t in reads:
            for k, v in t.w.items():
                if k != skip and deps.get(k, 0) < v:
                    deps[k] = v
        for t in writes:
            for k, v in t.w.items():
                if (k == eng and eng == "pe") or k == skip:
                    continue
                if deps.get(k, 0) < v:
                    deps[k] = v
            for k, v in t.r.items():
                if k == skip:
                    continue
                if deps.get(k, 0) < v:
                    deps[k] = v
        return deps

    def _waits(self, eng, deps):
        e = self.eng[eng]
        wd = self.waited[eng]
        for k, v in deps.items():
            if wd.get(k, 0) >= v:
                continue
            sem = self.sem[k] if k in self.sem else self.dsems[k]
            e.wait_ge(sem, v)
            wd[k] = v

    def op(self, eng, fn, ins, outs):
        self.n_inst += 1
        if self.dry:
            return
        reads = []
        for x in ins:
            if isinstance(x, V):
                for t in x.tiles():
                    if t not in reads:
                        reads.append(t)
        writes = []
        for x in outs:
            for t in x.tiles():
                if t not in writes:
                    writes.append(t)
        deps = self._collect(reads, writes, eng)
        self._waits(eng, deps)
        ins_ = fn()
        self.cnt[eng] += 1
        idx = self.cnt[eng]
        ins_.then_inc(self.sem[eng], 1)
        for t in writes:
            t.w = {eng: idx}
            t.r = {}
        for t in reads:
            if t not in writes:
                t.r[eng] = idx

    def dma(self, q, out, in_, join=False):
        self.n_inst += 1
        if self.dry:
            return
        reads = in_.tiles() if isinstance(in_, V) else []
        writes = out.tiles() if isinstance(out, V) else []
        st = (writes[0] if writes else reads[0])
        if st.dsem is None:
            if self.free_dsems:
                st.dsem = self.free_dsems.pop()
            else:
                st.dsem = f"d{len(self.dsems)}"
                self.dsems[st.dsem] = self.nc.alloc_semaphore(st.dsem)
                self.dma_keys[st.dsem] = 0
            prev = self.dma_keys[st.dsem]
            if prev > 0:
                self._waits(q, {st.dsem: prev})
        deps = self._collect(reads, writes, q, skip=(st.dsem if join else None))
        self._waits(q, deps)
        ins_ = self.eng[q].dma_start(out=_ap(out), in_=_ap(in_))
        self.dma_keys[st.dsem] += 16
        cnt = self.dma_keys[st.dsem]
        ins_.then_inc(self.dsems[st.dsem], 16)
        for t in writes:
            if join and st.dsem in t.w:
                t.w[st.dsem] = cnt
            else:
                t.w = {st.dsem: cnt}
                t.r = {}
        for t in reads:
            t.r[st.dsem] = cnt

    def final_wait(self, eng="sp"):
        if self.dry:
            return
        for k, c in self.dma_keys.items():
            if c > 0:
                self.eng[eng].wait_ge(self.dsems[k], c)

    def _e(self, eng):
        return self.eng[eng]

    def tt(self, eng, out, a, b, op):
        self.op(eng, lambda: self._e(eng).tensor_tensor(out.ap, a.ap, b.ap, op), [a, b], [out])

    def ts(self, eng, out, a, s1, s2, op0, op1=None):
        if op1 is None:
            self.op(eng, lambda: self._e(eng).tensor_scalar(out.ap, a.ap, _ap(s1), None, op0), [a, s1], [out])
        else:
            self.op(eng, lambda: self._e(eng).tensor_scalar(out.ap, a.ap, _ap(s1), _ap(s2), op0, op1),
                    [a, s1, s2], [out])

    def stt(self, out, a, s, b, op0, op1):
        self.op("dve", lambda: self.nc.vector.scalar_tensor_tensor(out.ap, a.ap, _ap(s), b.ap, op0, op1),
                [a, s, b], [out])

    def scan(self, out, d0, d1, init, op0, op1):
        self.op("dve", lambda: self.nc.vector.tensor_tensor_scan(out.ap, d0.ap, d1.ap, _ap(init), op0, op1),
                [d0, d1, init], [out])

    def copy(self, eng, out, a):
        if eng == "act":
            self.op(eng, lambda: self.nc.scalar.copy(out.ap, a.ap), [a], [out])
        else:
            self.op(eng, lambda: self._e(eng).tensor_copy(out.ap, a.ap), [a], [out])

    def act(self, out, a, func, bias=0.0, scale=1.0):
        self.op("act", lambda: self.nc.scalar.activation(out.ap, a.ap, func, bias=_ap(bias), scale=_ap(scale)),
                [a, bias, scale], [out])

    def recip(self, out, a):
        self.op("dve", lambda: self.nc.vector.reciprocal(out.ap, a.ap), [a], [out])

    def memset(self, eng, out, val):
        self.op(eng, lambda: self._e(eng).memset(out.ap, val), [], [out])

    def mm(self, out, lhsT, rhs, start=True, stop=True):
        self.op("pe", lambda: self.nc.tensor.matmul(out.ap, lhsT.ap, rhs.ap, start=start, stop=stop),
                [lhsT, rhs], [out])

    def tr(self, out, a, ident):
        self.op("pe", lambda: self.nc.tensor.transpose(out.ap, a.ap, ident.ap), [a, ident], [out])


def _panels(W, pw):
    K, N = W.shape
    assert K % 128 == 0 and N % pw == 0
    return np.ascontiguousarray(W.reshape(K // 128, 128, N // pw, pw).transpose(2, 1, 0, 3))


def _fm(vec):
    v = np.asarray(vec, np.float32).reshape(-1)
    return np.ascontiguousarray(v.reshape(-1, 128).T)


def make_consts(cfg):
    cols = {}
    parts = []

    def add(name, arr):
        arr = np.asarray(arr, np.float32)
        assert arr.shape[0] == 128
        cols[name] = (sum(p.shape[1] for p in parts), arr.shape[1])
        parts.append(arr)

    I = np.arange(128)
    add("ident", np.eye(128))
    add("ones", np.ones((128, 128)))
    add("blk", (I[:, None] // 64 == I[None, :] // 64).astype(np.float32))
    add("hm", np.stack([(I < 64), (I >= 64)], 1).astype(np.float32))
    for kind, ns, L in (("p", 1, 128), ("s", cfg.NS, cfg.LS)):
        seg = I // L
        same = seg[:, None] == seg[None, :]
        le = I[:, None] <= I[None, :]
        lt = I[:, None] < I[None, :]
        add(kind + "_maskb", np.where(same & le, 0.0, NEG))
        strict = (same & lt).astype(np.float32)
        incl = (same & le).astype(np.float32)
        add(kind + "_m2", np.concatenate([strict, incl], 1))
        add(kind + "_strictT", strict.T.copy())
        start = (I % L == 0)
        add(kind + "_rmask", np.tile(np.where(start, 0.0, 1.0)[None, :], (128, 1)))
        add(kind + "_rbias", np.tile(np.where(start, -1e30, 0.0)[None, :], (128, 1)))
        rowmask = (seg[:, None] == np.arange(ns)[None, :]).astype(np.float32)
        add(kind + "_rowmask", rowmask)
    seg = I // cfg.LS
    rowmask = (seg[:, None] == np.arange(cfg.NS)[None, :]).astype(np.float32)
    colmask = np.tile(rowmask.T.reshape(1, cfg.NS * 128), (128, 1)).astype(np.float32)
    return np.concatenate(parts, 1), cols, colmask


def vec_cols(cfg):
    NC, MC, RGB = cfg.NC, cfg.MC, cfg.RGB
    names = []
    for l in range(2):
        names += [(f"ln1g{l}", NC), (f"ln1b{l}", NC), (f"ln2g{l}", NC), (f"ln2b{l}", NC)]
    names += [("mng", MC)] + [(f"cw{j}", RGB) for j in range(4)] + [("cb", RGB), ("ba", RGB), ("bx", RGB), ("lam", RGB)]
    names += [(f"mu{j}", NC) for j in range(6)]
    names += [(n, NC) for n in ("w0", "a0", "kk", "ka", "rk", "lxg", "lxb")]
    cols, o = {}, 0
    for n, c in names:
        cols[n] = (o, c)
        o += c
    return cols, o


def make_vecs(cfg, p):
    cols, n = vec_cols(cfg)
    out = np.zeros((128, n), np.float32)

    def put(name, vec):
        o, c = cols[name]
        a = _fm(vec)
        assert a.shape == (128, c), (name, a.shape, c)
        out[:, o:o + c] = a

    for l in range(2):
        put(f"ln1g{l}", p["ln1_g"][l]); put(f"ln1b{l}", p["ln1_b"][l])
        put(f"ln2g{l}", p["ln2_g"][l]); put(f"ln2b{l}", p["ln2_b"][l])
    put("mng", p["mlstm_norm_g"][0])
    for j in range(4):
        put(f"cw{j}", p["rg_conv_w"][0, j])
    put("cb", p["rg_conv_b"][0]); put("ba", p["rg_ba"][0]); put("bx", p["rg_bx"][0]); put("lam", p["rg_lambda"][0])
    for j in range(6):
        put(f"mu{j}", p["rw_mu"][0, j])
    put("w0", p["rw_w0"][0]); put("a0", p["rw_a0"][0]); put("kk", p["rw_kk"][0]); put("ka", p["rw_ka"][0])
    put("rk", np.asarray(p["rw_rk"][0]).reshape(-1)); put("lxg", p["rw_lnx_g"][0]); put("lxb", p["rw_lnx_b"][0])
    return out


def weight_specs(cfg):
    D, NC, MIXA, RGB, DFF, FC = cfg.D, cfg.NC, cfg.MIXA, cfg.RGB, cfg.DFF, cfg.FC
    PW = 256
    s = {}
    for n in ("wq", "wk", "wv", "wo", "wxr", "wgr"):
        s[n] = [MIXA // PW, 128, NC, PW]
    s["wig"] = [1, 128, NC, cfg.NHA]
    s["wfg"] = [1, 128, NC, cfg.NHA]
    s["rgax"] = [1, 128, RGB, 256]
    s["wout"] = [D // PW, 128, NC, PW]
    KH = min(FC, 32)
    for l in range(2):
        s[f"wup{l}"] = [DFF // PW, 128, NC, PW]
        s[f"wdn{l}"] = [NC * (FC // KH), 128, KH, 128]
    for n in ("rwr", "rwk", "rwv", "rwo"):
        s[n] = [D // PW, 128, NC, PW]
    s["w1"] = [1, 128, NC, cfg.LW]
    s["a1"] = [1, 128, NC, cfg.LA]
    s["g1"] = [1, 128, NC, cfg.LG]
    s["w2"] = [1, cfg.LW, 1, D]
    s["a2"] = [1, cfg.LA, 1, D]
    s["g2"] = [1, 128, cfg.LG // 128, D]
    return s


def host_weights(cfg, p):
    MIXA, RGW, NHA, FC, NC = cfg.MIXA, cfg.RGW, cfg.NHA, cfg.FC, cfg.NC
    PW = 256
    W = np.asarray(p["w_in_ab"][0], np.float32)
    o = 0
    out = {}
    for n in ("wq", "wk", "wv", "wo"):
        out[n] = _panels(W[:, o:o + MIXA], PW)
        o += MIXA
    out["wig"] = _panels(W[:, o:o + NHA], NHA)
    o += NHA
    out["wfg"] = _panels(W[:, o:o + NHA], NHA)
    o += NHA
    out["wxr"] = _panels(W[:, o:o + RGW], PW)
    o += RGW
    out["wgr"] = _panels(W[:, o:o + RGW], PW)
    o += RGW
    out["rgax"] = np.ascontiguousarray(np.concatenate([np.asarray(p["rg_wa"][0], np.float32).transpose(1, 0, 2),
                                                       np.asarray(p["rg_wx"][0], np.float32).transpose(1, 0, 2)], -1))[None]
    out["wout"] = _panels(np.asarray(p["w_out_ab"][0], np.float32), PW)
    KH = min(FC, 32)
    for l in range(2):
        out[f"wup{l}"] = _panels(np.asarray(p["w_up"][l], np.float32), PW)
        wd = _panels(np.asarray(p["w_down"][l], np.float32), 128)
        wd = wd.reshape(NC, 128, FC // KH, KH, 128).transpose(0, 2, 1, 3, 4)
        out[f"wdn{l}"] = np.ascontiguousarray(wd.reshape(NC * (FC // KH), 128, KH, 128))
    for n, k in (("rwr", "rw_wr"), ("rwk", "rw_wk"), ("rwv", "rw_wv"), ("rwo", "rw_wo")):
        out[n] = _panels(np.asarray(p[k][0], np.float32), PW)
    out["w1"] = _panels(np.asarray(p["rw_w1"][0], np.float32), cfg.LW)
    out["a1"] = _panels(np.asarray(p["rw_a1"][0], np.float32), cfg.LA)
    out["g1"] = _panels(np.asarray(p["rw_g1"][0], np.float32), cfg.LG)
    out["w2"] = np.ascontiguousarray(np.asarray(p["rw_w2"][0], np.float32))[None, :, None, :]
    out["a2"] = np.ascontiguousarray(np.asarray(p["rw_a2"][0], np.float32))[None, :, None, :]
    out["g2"] = _panels(np.asarray(p["rw_g2"][0], np.float32), cfg.D)
    specs = weight_specs(cfg)
    for n in out:
        assert list(out[n].shape) == specs[n], (n, out[n].shape, specs[n])
    return out


def io_specs(cfg):
    D, NS, NHA, HDA, RGW, NC = cfg.D, cfg.NS, cfg.NHA, cfg.HDA, cfg.RGW, cfg.NC
    ins = {
        "xp": [D, cfg.TP], "xs": [D, cfg.TS],
        "cext": [NS, NHA, HDA, HDA + 1], "m0": [NHA, NS],
        "rgh": [RGW, NS], "rgc": [RGW, NS, 3],
        "rwS": [NC, 128, NS, 64], "rwx": [D, NS],
        "bif": [NHA, 2], "colm": [128, NS * 128],
    }
    outs = {
        "yp": [D, cfg.TP], "ys": [D, cfg.TS],
        "pc": [1, NHA, HDA, HDA + 1], "pm": [NHA, 1], "prgh": [RGW, 1], "prgc": [RGW, 1, 3],
        "prwS": [NC, 128, 1, 64], "prwx": [D, 1],
        "sc": [NS, NHA, HDA, HDA + 1], "sm": [NHA, NS], "srgh": [RGW, NS], "srgc": [RGW, NS, 3],
        "srwS": [NC, 128, NS, 64], "srwx": [D, NS],
    }
    return ins, outs


class Tile:
    def __init__(self, kind, tok0, L, c0, cfg):
        self.kind, self.tok0, self.L, self.c0 = kind, tok0, L, c0
        if kind == "p":
            self.ns, self.ls = 1, L
        else:
            self.ns, self.ls = cfg.NS, cfg.LS
        self.nsteps = int(np.ceil(np.log2(self.ls)))


class Reg:
    pass


class StopBuild(Exception):
    pass


STOP = None


class Blk:
    pass


class Prog:
    WELEMS = 4096

    def __init__(self, nc, cfg, plan=None, debug=()):
        self.nc, self.cfg = nc, cfg
        self.dry = plan is None
        self.fw = Fw(nc, dry=self.dry)
        self.plan = plan if plan is not None else []
        self.wk = 0
        self.wissued = 0
        self.debug = debug
        self.dbg_out = {}
        self.wspec = weight_specs(cfg)

    def declare(self):
        nc, cfg = self.nc, self.cfg
        ins, outs = io_specs(cfg)
        self.din, self.dout = {}, {}
        for n, s in ins.items():
            self.din[n] = nc.dram_tensor(n, s, F32, kind="ExternalInput").ap()
        cst, self.ccols, _ = make_consts(cfg)
        self.ncst = cst.shape[1]
        self.din["cst"] = nc.dram_tensor("cst", [128, self.ncst], F32, kind="ExternalInput").ap()
        self.vcols, self.nvec = vec_cols(cfg)
        self.din["vec"] = nc.dram_tensor("vec", [128, self.nvec], F32, kind="ExternalInput").ap()
        self.dw = {}
        for n, s in self.wspec.items():
            self.dw[n] = nc.dram_tensor(n, s, F32, kind="ExternalInput").ap()
        for n, s in outs.items():
            self.dout[n] = nc.dram_tensor(n, s, F32, kind="ExternalOutput").ap()

    def wpanel(self, name, pan):
        fw = self.fw
        shp = self.wspec[name]
        parts, KC, pw = shp[1], shp[2], shp[3]
        assert KC * pw <= self.WELEMS, (name, KC, pw)
        k = self.wk
        self.wk += 1
        if self.dry:
            self.plan.append((name, pan))
        else:
            assert self.plan[k] == (name, pan), (k, self.plan[k], name, pan)
            while self.wissued < min(len(self.plan), k + 4):
                j = self.wissued
                nm, pn = self.plan[j]
                s2 = self.wspec[nm]
                buf = self.wbufs[j % 4]
                dst = buf[0:s2[1], 0:s2[2] * s2[3]]
                fw.dma("pool", dst, self.dw[nm][pn].rearrange("p k w -> p (k w)"))
                self.wissued += 1
        buf = self.wbufs[k % 4]
        return buf[0:parts, 0:KC * pw].re("p (k w) -> p k w", k=KC)

    def psum(self, wide=False):
        if wide:
            self.psw = getattr(self, "psw", 0) + 1
            return self.bank[self.psw % 8]
        b = self.bank[self.psi % 4]
        self.psi += 1
        return b

    def cc(self, name, rows=128, c0=0, n=None):
        o, w = self.ccols[name]
        n = w - c0 if n is None else n
        return self.CST[0:rows, o + c0:o + c0 + n]

    def vc(self, name, j=0, rows=128):
        o, w = self.vcols[name]
        return self.VEC[0:rows, o + j:o + j + 1]

    def dump(self, name, v, shape):
        if name not in self.debug:
            return
        fw = self.fw
        key = name
        i = 0
        while key in self.dbg_out:
            i += 1
            key = f"{name}_{i}"
        self.dbg_out[key] = self.nc.dram_tensor("dbg_" + key, list(shape), F32, kind="ExternalOutput").ap()
        tmp = fw.sb(list(shape), F32, "dbg")
        fw.copy("dve", tmp.v, v)
        fw.dma("sp", self.dbg_out[key], tmp.v)

    def build(self):
        cfg, fw, nc = self.cfg, self.fw, self.nc
        self.declare()
        NC = cfg.NC
        self.CST = fw.sb([128, self.ncst], F32, "cst")
        self.VEC = fw.sb([128, self.nvec], F32, "vec")
        fw.dma("sp", self.CST.v, self.din["cst"])
        fw.dma("sp", self.VEC.v, self.din["vec"])
        self.wbufs = [fw.sb([128, self.WELEMS], BF16, f"wbuf{i}") for i in range(4)]
        psA = fw.ps([128, 4, 512], F32, "psA")
        psD = fw.ps([128, 4, 512], F32, "psD")
        self.bank = [T(psA[:, i, :], f"bankA{i}") for i in range(4)] + [T(psD[:, i, :], f"bankD{i}") for i in range(4)]
        self.psD = V(self.bank[4], psD, ts=self.bank[4:8])
        self.psi = 0
        self.identb = fw.sb([128, 128], BF16, "identb")
        fw.copy("dve", self.identb.v, self.cc("ident"))
        self.onesb = fw.sb([128, 128], BF16, "onesb")
        fw.copy("dve", self.onesb.v, self.cc("ones"))
        self.colmb = fw.sb([128, cfg.NS * 128], BF16, "colmb")
        fw.dma("pool", self.colmb.v, self.din["colm"])
        try:
            self.derived_vecs()
            self.init_states()
            self.chk("init")
            for bi, blk in enumerate(cfg.blocks):
                self.run_block(bi, blk)
        except StopBuild:
            pass
        self.write_prompt_states()
        fw.final_wait("sp")

    def chk(self, name):
        if not hasattr(self, "phase_log"):
            self.phase_log = []
        self.phase_log.append((name, self.fw.cnt["pe"]))
        if STOP == name:
            raise StopBuild()

    def derived_vecs(self):
        cfg, fw = self.cfg, self.fw
        RGB, NC, NHA = cfg.RGB, cfg.NC, cfg.NHA
        self.DV = fw.sb([128, 2 * RGB + 7 * NC], F32, "dv")
        o, _ = self.vcols["lam"]
        lam = self.VEC[:, o:o + RGB]
        t = self.DV[:, 0:RGB]
        fw.act(t, lam, AF.Exp, scale=-1.0)
        fw.act(t, t, AF.Ln, bias=1.0)
        fw.ts("dve", self.DV[:, RGB:2 * RGB], t, -16.0, None, ALU.mult)
        fw.ts("dve", t, t, -8.0, None, ALU.mult)
        b = 2 * RGB
        for j in range(6):
            o, _ = self.vcols[f"mu{j}"]
            fw.ts("dve", self.DV[:, b + j * NC:b + (j + 1) * NC], self.VEC[:, o:o + NC], -1.0, 1.0, ALU.mult, ALU.add)
        b2 = b + 6 * NC
        o, _ = self.vcols["ka"]
        fw.ts("dve", self.DV[:, b2:b2 + NC], self.VEC[:, o:o + NC], -1.0, 1.0, ALU.mult, ALU.add)
        self.dv_c1 = lambda n: self.DV[:, n:n + 1]
        self.dv_c2 = lambda n: self.DV[:, RGB + n:RGB + n + 1]
        self.dv_1mmu = lambda j, c: self.DV[:, b + j * NC + c:b + j * NC + c + 1]
        self.dv_1mka = lambda c: self.DV[:, b2 + c:b2 + c + 1]
        self.EPS = fw.sb([128, 2], F32, "eps")
        fw.memset("dve", self.EPS[:, 0:1], LN_EPS)
        fw.memset("dve", self.EPS[:, 1:2], GN_EPS)
        self.epsc = lambda k: self.EPS[:, 0:1] if k == "ln" else self.EPS[:, 1:2]
        self.BIF = fw.sb([NHA, 3], F32, "bif")
        fw.dma("sp", self.BIF[:, 0:2], self.din["bif"])
        fw.ts("dve", self.BIF[:, 2:3], self.BIF[:, 1:2], -1.0, None, ALU.mult)
        self.SEL = fw.sb([NHA, NHA, 128], F32, "sel")
        for h in range(NHA):
            fw.copy("dve", self.SEL[:, h, :], self.cc("ident", rows=NHA, c0=h, n=1).bc([NHA, 128]))

    def init_states(self):
        cfg, fw = self.cfg, self.fw
        NHA, HKC, HDA, RGB, NC = cfg.NHA, cfg.HKC, cfg.HDA, cfg.RGB, cfg.NC
        self.pC = fw.sb([128, NHA, HKC, HDA + 1], F32, "pC")
        fw.memset("dve", self.pC.v, 0.0)
        self.pM = fw.sb([NHA, 1], F32, "pM")
        fw.memset("dve", self.pM.v, 0.0)
        self.sM = fw.sb([NHA, cfg.NS], F32, "sM")
        fw.dma("sp", self.sM.v, self.din["m0"])
        self.pRH = fw.sb([128, RGB, 1], F32, "pRH")
        fw.memset("dve", self.pRH.v, 0.0)
        self.pRC = fw.sb([128, RGB, 1, 3], F32, "pRC")
        fw.memset("dve", self.pRC.v, 0.0)
        self.pH = fw.sb([128, NC, 1, 64], F32, "pH")
        fw.memset("dve", self.pH.v, 0.0)
        self.pSH = fw.sb([128, NC, 1], F32, "pSH")
        fw.memset("dve", self.pSH.v, 0.0)

    def write_prompt_states(self):
        fw, do = self.fw, self.dout
        fw.dma("sp", do["pc"][0].rearrange("h (kc p) v -> p h kc v", p=128), self.pC.v)
        fw.dma("sp", do["pm"], self.pM.v)
        fw.dma("sp", do["prgh"].rearrange("(n p) o -> p n o", p=128), self.pRH.v)
        fw.dma("sp", do["prgc"].rearrange("(n p) o j -> p n o j", p=128), self.pRC.v)
        fw.dma("sp", do["prwS"].rearrange("m p o v -> p m o v"), self.pH.v)
        fw.dma("sp", do["prwx"].rearrange("(c p) o -> p c o", p=128), self.pSH.v)

    def run_block(self, bi, blk):
        cfg, fw = self.cfg, self.fw
        NC = cfg.NC
        B = Blk()
        B.tiles = []
        c = 0
        for (kind, tok0, L) in blk:
            B.tiles.append(Tile(kind, tok0, L, c, cfg))
            c += L
        B.Tb = c
        B.regs = []
        e = 0
        pt = [t for t in B.tiles if t.kind == "p"]
        if pt:
            r = Reg()
            r.kind, r.c0, r.ns, r.L, r.tok0 = "p", pt[0].c0, 1, sum(t.L for t in pt), pt[0].tok0
            r.e0 = e
            e += 1 + r.L
            B.regs.append(r)
        st = [t for t in B.tiles if t.kind == "s"]
        if st:
            r = Reg()
            r.kind, r.c0, r.ns, r.L, r.tok0 = "s", st[0].c0, cfg.NS, cfg.LS, 0
            r.e0 = e
            e += r.ns * (1 + r.L)
            B.regs.append(r)
        B.EXT = e
        for r in B.regs:
            r.n = r.ns * r.L
        with fw.scope():
            B.hres = fw.sb([128, NC, B.EXT], F32, "hres")
            B.abuf = fw.sb([128, NC, B.Tb], BF16, "abuf")
            self.B = B
            for r in B.regs:
                if r.kind == "p":
                    src = self.din["xp"][:, r.tok0:r.tok0 + r.L].rearrange("(c p) t -> p c t", p=128)
                    fw.dma("sp", B.hres[:, :, r.e0 + 1:r.e0 + 1 + r.L], src)
                else:
                    for kc in range(NC):
                        src = self.din["xs"][kc * 128:(kc + 1) * 128, :].rearrange("p (s t) -> p s t", s=r.ns)
                        fw.dma("sp", self.hv(kc, r), src, join=True)
                        src2 = self.din["rwx"][kc * 128:(kc + 1) * 128, :]
                        fw.dma("sp", self.hv(kc, r, -1)[:, :, 0], src2, join=True)
            self.chk("loadx")
            self.to_bf16(B)
            self.chk("bf16")
            self.layer0_mixer(B)
            self.chk("wout")
            self.layernorm(B, "ln1g0", "ln1b0", bf=True)
            self.chk("ln1")
            self.mlp(B, 0)
            self.chk("mlp0")
            self.layernorm(B, "ln2g0", "ln2b0", bf=False)
            self.chk("ln2")
            for r in B.regs:
                if r.kind == "p":
                    fw.copy("dve", B.hres[:, :, r.e0:r.e0 + 1], self.pSH.v)
                    fw.copy("dve", self.pSH.v, B.hres[:, :, r.e0 + r.L:r.e0 + r.L + 1])
                else:
                    for kc in range(NC):
                        dst = self.dout["srwx"][kc * 128:(kc + 1) * 128, :]
                        fw.dma("sp", dst, self.hv(kc, r)[:, :, r.L - 1], join=True)
            self.layer1_mixer(B)
            self.chk("l1mix")
            self.layernorm(B, "ln1g1", "ln1b1", bf=True)
            self.mlp(B, 1)
            self.layernorm(B, "ln2g1", "ln2b1", bf=False)
            for r in B.regs:
                if r.kind == "p":
                    dst = self.dout["yp"][:, r.tok0:r.tok0 + r.L].rearrange("(c p) t -> p c t", p=128)
                    fw.dma("sp", dst, B.hres[:, :, r.e0 + 1:r.e0 + 1 + r.L])
                else:
                    for kc in range(NC):
                        dst = self.dout["ys"][kc * 128:(kc + 1) * 128, :].rearrange("p (s t) -> p s t", s=r.ns)
                        fw.dma("sp", dst, self.hv(kc, r), join=True)

    def hv(self, kc, r, shift=0):
        B = self.B
        v = B.hres[:, kc, r.e0:r.e0 + r.ns * (1 + r.L)].re("p (s l) -> p s l", s=r.ns)
        return v[:, :, 1 + shift:1 + shift + r.L]

    def pv(self, v2d, r):
        return v2d[:, r.c0:r.c0 + r.n].re("p (s l) -> p s l", s=r.ns)

    def to_bf16(self, B):
        fw = self.fw
        for kc in range(self.cfg.NC):
            for r in B.regs:
                fw.copy("act" if kc % 2 else "dve", self.pv(B.abuf[:, kc, :], r), self.hv(kc, r))

    def proj_fm(self, wname, rhs_fn, N, evac):
        fw = self.fw
        npan, parts, KC, pw = self.wspec[wname]
        for pan in range(npan):
            W = self.wpanel(wname, pan)
            for mi in range(pw // 128):
                ps = self.psum(wide=(getattr(self, "_bg", None) is None))
                for kc in range(KC):
                    fw.mm(ps[:, 0:N], W[:, kc, mi * 128:(mi + 1) * 128], rhs_fn(kc), start=(kc == 0), stop=(kc == KC - 1))
                evac(pan * (pw // 128) + mi, ps)
            self.bg_step()

    def bg_step(self):
        g = getattr(self, "_bg", None)
        if g is not None:
            try:
                next(g)
            except StopIteration:
                self._bg = None

    def proj_resid(self, B, wname, N0=0, N=None):
        fw, cfg = self.fw, self.cfg

        def evac(m, ps):
            for r in B.regs:
                hv = self.hv(m, r)
                fw.stt(hv, hv, cfg.ALPHA, self.pv(ps[:, 0:B.Tb], r), ALU.mult, ALU.add)
        MC = cfg.MC
        if wname == "wout":
            yr = self.M.yr
            self.proj_fm(wname, lambda kc: B.abuf[:, kc, :] if kc < MC else yr[:, kc - MC, :], B.Tb, evac)
        else:
            self.proj_fm(wname, lambda kc: B.abuf[:, kc, :], B.Tb, evac)

    def layernorm(self, B, gname, bname, bf):
        fw, cfg = self.fw, self.cfg
        NC, Tb, D = cfg.NC, B.Tb, cfg.D
        onesf = self.cc("ones")
        with fw.scope():
            mu = fw.sb([128, Tb], F32, "lnmu")
            sq = [fw.sb([128, Tb], F32, f"lnsq{i}") for i in range(2)]
            ps1 = self.psum()
            for r in B.regs:
                for kc in range(NC):
                    fw.mm(self.pv(ps1[:, 0:Tb], r), onesf, self.hv(kc, r), start=(kc == 0), stop=(kc == NC - 1))
            fw.ts("dve", mu[:, 0:Tb], ps1[:, 0:Tb], 1.0 / D, None, ALU.mult)
            ps2 = self.psum()
            for r in B.regs:
                for kc in range(NC):
                    hv = self.hv(kc, r)
                    fw.tt("dve", hv, hv, self.pv(mu.v, r), ALU.subtract)
                    s = sq[kc % 2]
                    fw.act(self.pv(s.v, r), hv, AF.Square)
                    fw.mm(self.pv(ps2[:, 0:Tb], r), onesf, self.pv(s.v, r), start=(kc == 0), stop=(kc == NC - 1))
            rs = mu
            fw.act(rs[:, 0:Tb], ps2[:, 0:Tb], AF.Ln, bias=self.epsc("ln"), scale=1.0 / D)
            fw.act(rs[:, 0:Tb], rs[:, 0:Tb], AF.Exp, scale=-0.5)
            for kc in range(NC):
                for r in B.regs:
                    hv = self.hv(kc, r)
                    fw.tt("dve", hv, hv, self.pv(rs.v, r), ALU.mult)
                    if bf:
                        fw.act(self.pv(B.abuf[:, kc, :], r), hv, AF.Identity, bias=self.vc(bname, kc), scale=self.vc(gname, kc))
                    fw.ts("dve", hv, hv, self.vc(gname, kc), self.vc(bname, kc), ALU.mult, ALU.add)

    def mlp(self, B, l):
        fw, cfg = self.fw, self.cfg
        NC, FC, Tb = cfg.NC, cfg.FC, B.Tb
        with fw.scope():
            hid = fw.sb([128, FC, Tb], BF16, "hid")
            tmp = [fw.sb([128, Tb], F32, f"mlptmp{i}") for i in range(2)]

            def evac(f, ps):
                t = tmp[f % 2]
                fw.act(t[:, 0:Tb], ps[:, 0:Tb], AF.Relu)
                fw.tt("dve", hid[:, f, :], t[:, 0:Tb], t[:, 0:Tb], ALU.mult)
            self.proj_fm(f"wup{l}", lambda kc: B.abuf[:, kc, :], Tb, evac)
            npan, parts, KH, pw = self.wspec[f"wdn{l}"]
            nh = FC // KH
            for m in range(NC):
                ps = self.psum()
                for hf in range(nh):
                    W = self.wpanel(f"wdn{l}", m * nh + hf)
                    for kk in range(KH):
                        f = hf * KH + kk
                        fw.mm(ps[:, 0:Tb], W[:, kk, :], hid[:, f, :], start=(f == 0), stop=(f == FC - 1))
                for r in B.regs:
                    hv = self.hv(m, r)
                    fw.stt(hv, hv, cfg.ALPHA, self.pv(ps[:, 0:Tb], r), ALU.mult, ALU.add)

    def layer0_mixer(self, B):
        fw, cfg = self.fw, self.cfg
        NC, MC, NHA, HDA, HKC, RGB, Tb = cfg.NC, cfg.MC, cfg.NHA, cfg.HDA, cfg.HKC, cfg.RGB, B.Tb
        nt = len(B.tiles)
        with fw.scope():
            M = Blk()
            self.M = M
            M.qT = fw.sb([128, MC, Tb], BF16, "qT")
            M.kT = fw.sb([128, MC, Tb], BF16, "kT")
            M.oT = fw.sb([128, MC, Tb], BF16, "oT")
            M.kTok = fw.sb([128, nt, cfg.MIXA], BF16, "kTok")
            M.vTok = fw.sb([128, nt, NHA, HDA + 1], BF16, "vTok")
            M.gr = fw.sb([128, RGB, Tb], BF16, "gr")
            M.XE = sum(r.ns * (3 + r.L) for r in B.regs)
            M.xr = fw.sb([128, RGB, M.XE], F32, "xr")
            M.ig = fw.sb([NHA, Tb], F32, "ig")
            M.lf = fw.sb([NHA, Tb], F32, "lf")
            xo = 0
            for r in B.regs:
                r.x0 = xo
                xo += r.ns * (3 + r.L)
            rhs = lambda kc: B.abuf[:, kc, :]
            sc = float(HDA) ** -0.5
            fw.memset("dve", M.vTok[:, :, :, HDA:HDA + 1], 1.0)
            for r in B.regs:
                if r.kind == "p":
                    fw.copy("dve", self.xrv(None, r, 0, 3), self.pRC.v)
                else:
                    for n in range(RGB):
                        src = self.din["rgc"][n * 128:(n + 1) * 128]
                        fw.dma("sp", self.xrv(n, r, 0, 3), src, join=True)
            def xev(m, ps):
                for r in B.regs:
                    fw.copy("act", self.xrv(m, r, 3, r.L), self.pv(ps[:, 0:Tb], r))
            self.proj_fm("wxr", rhs, Tb, xev)
            with fw.scope():
                g1 = fw.sb([128, Tb], F32, "g1")
                g2 = fw.sb([128, Tb], F32, "g2")

                def gev(m, ps):
                    fw.act(g1[:, 0:Tb], ps[:, 0:Tb], AF.Square)
                    fw.ts("dve", g1[:, 0:Tb], g1[:, 0:Tb], 0.044715, 1.0, ALU.mult, ALU.add)
                    fw.tt("dve", g1[:, 0:Tb], g1[:, 0:Tb], ps[:, 0:Tb], ALU.mult)
                    fw.act(g2[:, 0:Tb], g1[:, 0:Tb], AF.Sigmoid, scale=1.5957691216)
                    fw.tt("dve", M.gr[:, m, :], g2[:, 0:Tb], ps[:, 0:Tb], ALU.mult)
                self.proj_fm("wgr", rhs, Tb, gev)
            M.yr = fw.sb([128, RGB, Tb], BF16, "yr")
            rg = self.rglru(B, M)
            self._bg = rg
            self.proj_fm("wq", rhs, Tb, lambda m, ps: fw.copy("act", M.qT[:, m, :], ps[:, 0:Tb]))
            self.proj_fm_tok("wk", B, lambda m, ps: fw.act(M.kT[:, m, :], ps[:, 0:Tb], AF.Copy, scale=sc),
                             lambda ti, L, col0, w, ps: fw.act(M.kTok[0:L, ti, col0:col0 + w], ps[0:L, 0:w], AF.Copy, scale=sc))
            def vev(ti, L, col0, w, ps):
                c = col0
                while c < col0 + w:
                    h, dv = c // HDA, c % HDA
                    ww = min(HDA - dv, col0 + w - c)
                    fw.copy("act", M.vTok[0:L, ti, h, dv:dv + ww], ps[0:L, c - col0:c - col0 + ww])
                    c += ww
            self.proj_fm_tok("wv", B, None, vev)
            self.proj_fm("wo", rhs, Tb, lambda m, ps: fw.act(M.oT[:, m, :], ps[:, 0:Tb], AF.Sigmoid))
            for nm, dst in (("wig", M.ig), ("wfg", M.lf)):
                W = self.wpanel(nm, 0)
                ps = self.psum()
                for kc in range(NC):
                    fw.mm(ps[0:NHA, 0:Tb], W[:, kc, 0:NHA], rhs(kc), start=(kc == 0), stop=(kc == NC - 1))
                if nm == "wig":
                    fw.act(dst[:, 0:Tb], ps[0:NHA, 0:Tb], AF.Identity, bias=self.BIF[:, 0:1])
                else:
                    fw.act(dst[:, 0:Tb], ps[0:NHA, 0:Tb], AF.Exp, bias=self.BIF[:, 2:3], scale=-1.0)
                    fw.act(dst[:, 0:Tb], dst[:, 0:Tb], AF.Ln, bias=1.0)
                    fw.ts("dve", dst[:, 0:Tb], dst[:, 0:Tb], -1.0, None, ALU.mult)
            self._bg = None
            for _ in rg:
                pass
            self.dump("qT", M.qT.v, [128, MC, Tb])
            self.dump("kT", M.kT.v, [128, MC, Tb])
            self.dump("ig", M.ig.v, [NHA, Tb])
            self.dump("lf", M.lf.v, [NHA, Tb])
            self.chk("inproj")
            for ti, t in enumerate(B.tiles):
                self.mlstm_tile(B, M, ti, t)
                self.chk(f"mlstm{ti}")
            self.chk("mlstm")
            self.chk("rglru")
            self.dump("cat", B.abuf.v, [128, NC, Tb])
            self.proj_resid(B, "wout")

    def xrv(self, n, r, j0, w):
        M = self.M
        if n is None:
            v = M.xr[:, :, r.x0:r.x0 + r.ns * (3 + r.L)].re("p n (s l) -> p n s l", s=r.ns)
            return v[:, :, :, j0:j0 + w]
        v = M.xr[:, n, r.x0:r.x0 + r.ns * (3 + r.L)].re("p (s l) -> p s l", s=r.ns)
        return v[:, :, j0:j0 + w]

    def proj_fm_tok(self, wname, B, evac_fm, evac_tok):
        fw = self.fw
        npan, parts, KC, pw = self.wspec[wname]
        for pan in range(npan):
            W = self.wpanel(wname, pan)
            if evac_fm is not None:
                for mi in range(pw // 128):
                    ps = self.psum()
                    for kc in range(KC):
                        fw.mm(ps[:, 0:B.Tb], W[:, kc, mi * 128:(mi + 1) * 128], B.abuf[:, kc, :], start=(kc == 0), stop=(kc == KC - 1))
                    evac_fm(pan * (pw // 128) + mi, ps)
            for ti, t in enumerate(B.tiles):
                ps = self.psum()
                for kc in range(KC):
                    fw.mm(ps[0:t.L, 0:pw], B.abuf[:, kc, t.c0:t.c0 + t.L], W[:, kc, :], start=(kc == 0), stop=(kc == KC - 1))
                evac_tok(ti, t.L, pan * pw, pw, ps)
            self.bg_step()

    def mlstm_tile(self, B, M, ti, t):
        fw, cfg = self.fw, self.cfg
        NHA, HDA, HKC = cfg.NHA, cfg.HDA, cfg.HKC
        L, c0, ns, ls, kind = t.L, t.c0, t.ns, t.ls, t.kind
        cs = slice(c0, c0 + L)
        onesf = self.cc("ones")
        mprev = self.pM if kind == "p" else self.sM
        with fw.scope():
            G = fw.sb([NHA, 6, 128], F32, "G")
            Fv, gv, Mv, iv, ev, wv = (G[:, i, 0:L] for i in range(6))
            s3 = lambda v: v.re("h (s l) -> h s l", s=ns)
            fw.scan(Fv, self.cc(kind + "_rmask", rows=NHA, n=L), M.lf[:, cs], 0.0, ALU.mult, ALU.add)
            fw.tt("dve", gv, M.ig[:, cs], Fv, ALU.subtract)
            fw.scan(Mv, self.cc(kind + "_rbias", rows=NHA, n=L), gv, -1e30, ALU.add, ALU.max)
            fw.tt("dve", s3(Mv), s3(Mv), mprev[:, 0:ns].un(2).bc([NHA, ns, ls]), ALU.max)
            fw.tt("dve", s3(iv), s3(Mv), mprev[:, 0:ns].un(2).bc([NHA, ns, ls]), ALU.subtract)
            fw.act(iv, iv, AF.Exp, scale=-1.0)
            fw.tt("dve", ev, Fv, Mv, ALU.add)
            mnew = fw.sb([NHA, ns], F32, "mnew")
            fw.copy("dve", mnew.v, s3(ev)[:, :, ls - 1])
            fw.act(ev, ev, AF.Exp, scale=-1.0)
            fw.tt("dve", s3(wv), s3(gv), s3(Mv)[:, :, ls - 1:ls].bc([NHA, ns, ls]), ALU.subtract)
            fw.act(wv, wv, AF.Exp)
            pc = self.psum()
            identf = self.cc("ident", rows=NHA, n=NHA)
            fw.mm(pc[0:L, 0:NHA], gv, identf)
            fw.mm(pc[0:L, NHA:2 * NHA], wv, identf)
            cols = fw.sb([128, 2 * NHA], F32, "cols")
            fw.copy("act", cols[0:L, :], pc[0:L, 0:2 * NHA])
            BCs = fw.sb([128, 3, 128], F32, "BCs")
            DT = fw.sb([128, 128], F32, "DT")
            sTd = fw.sb([128, 128], BF16, "sTd")
            qTs = fw.sb([128, HKC, 128], BF16, "qTs")
            hT = fw.sb([128, HKC, 128], F32, "hT")
            sq = fw.sb([128, HKC, 128], F32, "hsq")
            dn = fw.sb([128, 128], F32, "dn")
            mu = fw.sb([128, 128], F32, "hmu")
            kw = fw.sb([128, HDA], BF16, "kw")
            wm = fw.sb([128, ns], F32, "wm")
            Cb = fw.sb([128, HKC, HDA], BF16, "Cb")
            nbc = fw.sb([128, HKC, 128], BF16, "nbc")
            Cs = [fw.sb([128, HKC, HDA + 1], F32, f"Cs{i}") for i in range(2)] if kind == "s" else None
            for h in range(NHA):
                pb = self.psum()
                fw.mm(pb[:, 0:3 * L].re("p (a l) -> p a l", a=3), self.SEL[:, h, :], G[:, 2:5, 0:L])
                fw.copy("act", BCs[:, :, 0:L], pb[:, 0:3 * L].re("p (a l) -> p a l", a=3))
                fw.stt(DT[0:L, 0:L], BCs[0:L, 0, 0:L], -1.0, self.cc(kind + "_maskb", rows=L, n=L), ALU.mult, ALU.add)
                fw.act(DT[0:L, 0:L], DT[0:L, 0:L], AF.Exp, bias=cols[0:L, h:h + 1])
                p2 = self.psum()
                for kc in range(HKC):
                    fw.mm(p2[0:L, 0:L], M.kT[:, h * HKC + kc, cs], M.qT[:, h * HKC + kc, cs], start=(kc == 0), stop=(kc == HKC - 1))
                fw.tt("dve", sTd[0:L, 0:L], p2[0:L, 0:L], DT[0:L, 0:L], ALU.mult)
                for kc in range(HKC):
                    fw.tt("dve", qTs[:, kc, 0:L], M.qT[:, h * HKC + kc, cs], BCs[:, 1, 0:L], ALU.mult)
                psn = [self.bank[4 + c] for c in range(HKC)]
                psd = self.bank[4 + HKC]
                for c in range(HKC):
                    fw.mm(psn[c][:, 0:L], M.vTok[0:L, ti, h, c * 128:(c + 1) * 128], sTd[0:L, 0:L], start=True, stop=False)
                fw.mm(psd[:, 0:L], self.onesb[0:L, :], sTd[0:L, 0:L], start=True, stop=False)
                fw.ts("dve", wm[0:L, 0:ns], self.cc(kind + "_rowmask", rows=L, n=ns), cols[0:L, NHA + h:NHA + h + 1], None, ALU.mult)
                for s in range(ns):
                    sc = slice(s * ls, (s + 1) * ls)
                    if kind == "s":
                        Cst = Cs[(h * ns + s) % 2]
                        fw.dma("sp", Cst.v, self.din["cext"][s, h].rearrange("(kc p) v -> p kc v", p=128))
                        Cv = Cst.v
                    else:
                        Cv = self.pC[:, h]
                    fw.copy("act", Cb.v, Cv[:, :, 0:HDA])
                    fw.copy("dve", nbc.v, Cv[:, :, HDA:HDA + 1].bc([128, HKC, 128]))
                    last = (s == ns - 1)
                    for c in range(HKC):
                        for kc in range(HKC):
                            fw.mm(psn[c][:, sc], Cb[:, kc, c * 128:(c + 1) * 128], qTs[:, kc, sc], start=False, stop=(last and kc == HKC - 1))
                    for kc in range(HKC):
                        fw.mm(psd[:, sc], nbc[:, kc, :], qTs[:, kc, sc], start=False, stop=(last and kc == HKC - 1))
                    fw.ts("dve", kw[0:L, :], M.kTok[0:L, ti, h * HDA:(h + 1) * HDA], wm[0:L, s:s + 1], None, ALU.mult)
                    dec = BCs[:, 1, (s + 1) * ls - 1:(s + 1) * ls]
                    for kc in range(HKC):
                        pu = self.psum()
                        fw.mm(pu[:, 0:HDA + 1], kw[0:L, kc * 128:(kc + 1) * 128], M.vTok[0:L, ti, h, :])
                        fw.stt(Cv[:, kc, :], Cv[:, kc, :], dec, pu[:, 0:HDA + 1], ALU.mult, ALU.add)
                    if kind == "s":
                        fw.dma("sp", self.dout["sc"][s, h].rearrange("(kc p) v -> p kc v", p=128), Cv)
                fw.act(dn[:, 0:L], psd[:, 0:L], AF.Abs)
                fw.tt("dve", dn[:, 0:L], dn[:, 0:L], BCs[:, 2, 0:L], ALU.max)
                fw.recip(dn[:, 0:L], dn[:, 0:L])
                for c in range(HKC):
                    fw.tt("dve", hT[:, c, 0:L], psn[c][:, 0:L], dn[:, 0:L], ALU.mult)
                p3 = self.psum()
                for c in range(HKC):
                    fw.mm(p3[:, 0:L], onesf, hT[:, c, 0:L], start=(c == 0), stop=(c == HKC - 1))
                fw.ts("dve", mu[:, 0:L], p3[:, 0:L], 1.0 / HDA, None, ALU.mult)
                p4 = self.psum()
                for c in range(HKC):
                    fw.tt("dve", hT[:, c, 0:L], hT[:, c, 0:L], mu[:, 0:L], ALU.subtract)
                    fw.act(sq[:, c, 0:L], hT[:, c, 0:L], AF.Square)
                    fw.mm(p4[:, 0:L], onesf, sq[:, c, 0:L], start=(c == 0), stop=(c == HKC - 1))
                fw.act(mu[:, 0:L], p4[:, 0:L], AF.Ln, bias=self.epsc("ln"), scale=1.0 / HDA)
                fw.act(mu[:, 0:L], mu[:, 0:L], AF.Exp, scale=-0.5)
                for c in range(HKC):
                    m = h * HKC + c
                    fw.tt("dve", hT[:, c, 0:L], hT[:, c, 0:L], mu[:, 0:L], ALU.mult)
                    fw.stt(B.abuf[:, m, cs], hT[:, c, 0:L], self.vc("mng", m), M.oT[:, m, cs], ALU.mult, ALU.mult)
            if kind == "p":
                fw.copy("dve", self.pM.v, mnew.v)
            else:
                fw.dma("sp", self.dout["sm"], mnew.v)

    def rglru(self, B, M):
        fw, cfg = self.fw, self.cfg
        RGB, Tb, MC = cfg.RGB, B.Tb, cfg.MC
        with fw.scope():
            Wring = self.wpanel("rgax", 0)
            Wax = fw.sb([128, RGB, 256], BF16, "Wax")
            fw.copy("dve", Wax.v, Wring)
            Wa = Wax[:, :, 0:128]
            Wx = Wax[:, :, 128:256]
            xc = fw.sb([128, Tb], F32, "xc")
            xcb = fw.sb([128, Tb], BF16, "xcb")
            ra = fw.sb([128, Tb], F32, "ra")
            gi = fw.sb([128, Tb], F32, "gi")
            aa = fw.sb([128, Tb], F32, "aa")
            uu = fw.sb([128, Tb], F32, "uu")
            hr = fw.sb([128, Tb], F32, "hr")
            sRH = None
            for r in B.regs:
                if r.kind == "s":
                    sRH = fw.sb([128, RGB, r.ns], F32, "sRH")
                    fw.dma("sp", sRH.v, self.din["rgh"].rearrange("(n p) s -> p n s", p=128))
                    sRHo = fw.sb([128, RGB, r.ns], F32, "sRHo")
            for n in range(RGB):
                for r in B.regs:
                    xv = self.pv(xc.v, r)
                    fw.ts("dve", xv, self.xrv(n, r, 0, r.L), self.vc("cw0", n), self.vc("cb", n), ALU.mult, ALU.add)
                    for j in range(1, 4):
                        fw.stt(xv, self.xrv(n, r, j, r.L), self.vc(f"cw{j}", n), xv, ALU.mult, ALU.add)
                fw.copy("act", xcb[:, 0:Tb], xc[:, 0:Tb])
                pa = self.psum()
                fw.mm(pa[:, 0:Tb], Wa[:, n, :], xcb[:, 0:Tb])
                fw.act(ra[:, 0:Tb], pa[:, 0:Tb], AF.Sigmoid, bias=self.vc("ba", n))
                px = self.psum()
                fw.mm(px[:, 0:Tb], Wx[:, n, :], xcb[:, 0:Tb])
                fw.act(gi[:, 0:Tb], px[:, 0:Tb], AF.Sigmoid, bias=self.vc("bx", n))
                fw.act(aa[:, 0:Tb], ra[:, 0:Tb], AF.Exp, scale=self.dv_c1(n))
                fw.act(uu[:, 0:Tb], ra[:, 0:Tb], AF.Exp, scale=self.dv_c2(n))
                fw.ts("dve", uu[:, 0:Tb], uu[:, 0:Tb], -1.0, 1.0, ALU.mult, ALU.add)
                fw.ts("dve", uu[:, 0:Tb], uu[:, 0:Tb], 1e-30, None, ALU.max)
                fw.act(uu[:, 0:Tb], uu[:, 0:Tb], AF.Sqrt)
                fw.tt("dve", gi[:, 0:Tb], gi[:, 0:Tb], xc[:, 0:Tb], ALU.mult)
                fw.tt("dve", uu[:, 0:Tb], uu[:, 0:Tb], gi[:, 0:Tb], ALU.mult)
                for r in B.regs:
                    if r.kind == "p":
                        c = slice(r.c0, r.c0 + r.L)
                        fw.scan(hr[:, c], aa[:, c], uu[:, c], self.pRH[:, n, :], ALU.mult, ALU.add)
                        fw.copy("dve", self.pRH[:, n, :], hr[:, r.c0 + r.L - 1:r.c0 + r.L])
                    else:
                        for s in range(r.ns):
                            c = slice(r.c0 + s * r.L, r.c0 + (s + 1) * r.L)
                            fw.scan(hr[:, c], aa[:, c], uu[:, c], sRH[:, n, s:s + 1], ALU.mult, ALU.add)
                        fw.copy("dve", sRHo[:, n, :], self.pv(hr.v, r)[:, :, r.L - 1])
                fw.tt("dve", M.yr[:, n, :], hr[:, 0:Tb], M.gr[:, n, :], ALU.mult)
                yield
                if n == 0:
                    self.dump("xc", xc.v, [128, Tb]); self.dump("ra", ra.v, [128, Tb]); self.dump("aa", aa.v, [128, Tb])
                    self.dump("uu", uu.v, [128, Tb]); self.dump("hr", hr.v, [128, Tb])
            for r in B.regs:
                if r.kind == "p":
                    fw.copy("dve", self.pRC.v, self.xrv(None, r, r.L, 3))
                else:
                    fw.dma("sp", self.dout["srgh"].rearrange("(n p) s -> p n s", p=128), sRHo.v)
                    for n in range(RGB):
                        fw.dma("sp", self.dout["srgc"][n * 128:(n + 1) * 128], self.xrv(n, r, r.L, 3), join=True)

    def layer1_mixer(self, B):
        fw, cfg = self.fw, self.cfg
        subs, cur, n = [], [], 0
        for t in B.tiles:
            if cur and (n + t.L > 256 or t.kind == "s" or cur[-1].kind == "s"):
                subs.append(cur)
                cur, n = [], 0
            cur.append(t)
            n += t.L
        subs.append(cur)
        for sub in subs:
            self.rwkv_sub(B, sub)
        self.proj_resid(B, "rwo")

    def rwkv_sub(self, B, sub):
        fw, cfg = self.fw, self.cfg
        NC, LW, LA, LG = cfg.NC, cfg.LW, cfg.LA, cfg.LG
        c0 = sub[0].c0
        n = sum(t.L for t in sub)
        kind = sub[0].kind
        reg = [r for r in B.regs if r.kind == kind][0]
        ns = reg.ns
        if kind == "p":
            off = c0 - reg.c0
            Ls = n
            xv = lambda kc, sh=0: B.hres[:, kc, reg.e0 + 1 + off + sh:reg.e0 + 1 + off + sh + n].re("p (s l) -> p s l", s=1)
        else:
            Ls = reg.L
            xv = lambda kc, sh=0: self.hv(kc, reg, sh)
        s3 = lambda v: v.re("p (s l) -> p s l", s=ns)
        CW = 0.6065306597126334
        with fw.scope():
            S = Blk()
            S.sg = fw.sb([128, NC, n], F32, "sg")
            S.a = fw.sb([128, NC, n], BF16, "a")
            S.k = fw.sb([128, NC, n], BF16, "k")
            S.r = fw.sb([128, NC, n], BF16, "r")
            S.v = fw.sb([128, NC, n], BF16, "v")
            S.g = fw.sb([128, NC, n], BF16, "g")
            with fw.scope():
                xsb = [fw.sb([128, NC, n], BF16, f"xs{i}") for i in range(2)]
                tmps = [fw.sb([128, n], F32, f"xstmp{i}") for i in range(2)]
                lo = fw.sb([128, LG // 128, n], BF16, "lora")

                def mk_xs(j, xs):
                    for kc in range(NC):
                        tmp = tmps[kc % 2]
                        fw.act(s3(tmp[:, 0:n]), xv(kc, -1), AF.Identity, scale=self.vc(f"mu{j}", kc))
                        fw.stt(s3(xs[:, kc, :]), xv(kc), self.dv_1mmu(j, kc), s3(tmp[:, 0:n]), ALU.mult, ALU.add)
                        if kc % 2 == 1:
                            yield

                def lora(xs, n1, n2, R, func1, evac2):
                    rhs = lambda kc: xs[:, kc, :]
                    W1 = self.wpanel(n1, 0)
                    for c in range((R + 127) // 128):
                        w = min(128, R - c * 128)
                        ps = self.psum()
                        for kc in range(NC):
                            fw.mm(ps[0:w, 0:n], W1[:, kc, c * 128:c * 128 + w], rhs(kc), start=(kc == 0), stop=(kc == NC - 1))
                        fw.act(lo[0:w, c, :], ps[0:w, 0:n], func1)
                        self.bg_step()
                    W2 = self.wpanel(n2, 0)
                    KC2 = (R + 127) // 128
                    for m in range(NC):
                        ps = self.psum()
                        for c in range(KC2):
                            w = min(128, R - c * 128)
                            fw.mm(ps[:, 0:n], W2[0:w, c, m * 128:(m + 1) * 128], lo[0:w, c, :], start=(c == 0), stop=(c == KC2 - 1))
                        evac2(m, ps)
                        if m % 2 == 1:
                            self.bg_step()
                jobs = [
                    (1, lambda xs: lora(xs, "w1", "w2", LW, AF.Tanh,
                                        lambda m, ps: fw.act(S.sg[:, m, :], ps[:, 0:n], AF.Sigmoid, bias=self.vc("w0", m)))),
                    (4, lambda xs: lora(xs, "a1", "a2", LA, AF.Copy,
                                        lambda m, ps: fw.act(S.a[:, m, :], ps[:, 0:n], AF.Sigmoid, bias=self.vc("a0", m)))),
                    (5, lambda xs: lora(xs, "g1", "g2", LG, AF.Sigmoid,
                                        lambda m, ps: fw.copy("act", S.g[:, m, :], ps[:, 0:n]))),
                ]
                for j_, wn, dst in ((2, "rwk", S.k), (0, "rwr", S.r), (3, "rwv", S.v)):
                    jobs.append((j_, lambda xs, wn=wn, dst=dst: self.proj_fm(
                        wn, lambda kc: xs[:, kc, :], n, lambda m, ps: fw.copy("act", dst[:, m, :], ps[:, 0:n]))))
                for _ in mk_xs(jobs[0][0], xsb[0]):
                    pass
                for idx, (j_, run) in enumerate(jobs):
                    if idx + 1 < len(jobs):
                        self._bg = mk_xs(jobs[idx + 1][0], xsb[(idx + 1) % 2])
                    run(xsb[idx % 2])
                    if self._bg is not None:
                        for _ in self._bg:
                            pass
                        self._bg = None
            self.chk("l1proj")
            self.dump("sg", S.sg.v, [128, NC, n])
            self.dump("rk", S.k.v, [128, NC, n])
            for m0 in range(0, NC, 2):
                self.rwkv_mpair(B, S, sub, m0, c0, n, CW)

    def rwkv_mpair(self, B, S, sub, m0, c0, n, CW):
        fw, cfg = self.fw, self.cfg
        NC = cfg.NC
        blkf = self.cc("blk")
        with fw.scope():
            FT = fw.sb([128, 2, 4, n], BF16, "FT")
            FTp = fw.sb([128, 2, 2, 2, n], BF16, "FTp")
            self.FTp = FTp
            Ep = fw.sb([128, 2, n], F32, "Ep")
            bonus = fw.sb([128, 2, n], F32, "bonus")
            with fw.scope():
                cs = fw.sb([128, n], F32, "cs")
                Em = fw.sb([128, n], F32, "Em")
                Epm = fw.sb([128, n], F32, "Epm")
                kkp = fw.sb([128, n], F32, "kkp")
                sq = fw.sb([128, n], F32, "sq")
                rn = fw.sb([128, n], F32, "rn")
                tt_ = fw.sb([128, n], F32, "tt")
                km = fw.sb([128, n], F32, "km")
                for ml in range(2):
                    m = m0 + ml
                    sg = S.sg[:, m, :]
                    for t in sub:
                        tc = slice(t.c0 - c0, t.c0 - c0 + t.L)
                        fw.scan(cs[:, tc], self.cc(t.kind + "_rmask", n=t.L), sg[:, tc], 0.0, ALU.mult, ALU.add)
                    fw.act(Ep[:, ml, :], cs[:, 0:n], AF.Exp, scale=-CW)
                    fw.act(Em[:, 0:n], cs[:, 0:n], AF.Exp, scale=CW)
                    fw.tt("dve", Epm[:, 0:n], cs[:, 0:n], sg, ALU.subtract)
                    fw.act(Epm[:, 0:n], Epm[:, 0:n], AF.Exp, scale=-CW)
                    fw.ts("dve", kkp[:, 0:n], S.k[:, m, :], self.vc("kk", m), None, ALU.mult)
                    fw.act(sq[:, 0:n], kkp[:, 0:n], AF.Square)
                    ps = self.psum()
                    fw.mm(ps[:, 0:n], blkf, sq[:, 0:n])
                    fw.ts("dve", rn[:, 0:n], ps[:, 0:n], 1e-18, None, ALU.max)
                    fw.act(rn[:, 0:n], rn[:, 0:n], AF.Ln)
                    fw.act(rn[:, 0:n], rn[:, 0:n], AF.Exp, scale=-0.5)
                    fw.tt("dve", kkp[:, 0:n], kkp[:, 0:n], rn[:, 0:n], ALU.mult)
                    fw.ts("dve", tt_[:, 0:n], S.a[:, m, :], self.vc("ka", m), self.dv_1mka(m), ALU.mult, ALU.add)
                    fw.tt("dve", km[:, 0:n], tt_[:, 0:n], S.k[:, m, :], ALU.mult)
                    fw.stt(FT[:, ml, 0, :], kkp[:, 0:n], -1.0, Epm[:, 0:n], ALU.mult, ALU.mult)
                    fw.tt("dve", FT[:, ml, 1, :], S.r[:, m, :], Ep[:, ml, :], ALU.mult)
                    fw.tt("dve", tt_[:, 0:n], kkp[:, 0:n], S.a[:, m, :], ALU.mult)
                    fw.tt("dve", FT[:, ml, 2, :], tt_[:, 0:n], Em[:, 0:n], ALU.mult)
                    fw.tt("dve", FT[:, ml, 3, :], km[:, 0:n], Em[:, 0:n], ALU.mult)
                    for h2 in range(2):
                        for i_, src_ in enumerate((2, 3)):
                            fw.act(FTp[:, ml, h2, i_, :], FT[:, ml, src_, :], AF.Identity, scale=self.cc("hm")[:, h2:h2 + 1])
                    fw.tt("dve", tt_[:, 0:n], km[:, 0:n], S.r[:, m, :], ALU.mult)
                    fw.ts("dve", sq[:, 0:n], tt_[:, 0:n], self.vc("rk", m), None, ALU.mult)
                    ps = self.psum()
                    fw.mm(ps[:, 0:n], blkf, sq[:, 0:n])
                    fw.tt("dve", bonus[:, ml, :], ps[:, 0:n], S.v[:, m, :], ALU.mult)
            self.chk("l1prep")
            for t in sub:
                with fw.scope():
                    gens = [self.rwkv_chain(B, S, FT, Ep, bonus, t, m0, ml, c0) for ml in range(2)]
                    live = list(gens)
                    while live:
                        for g in list(live):
                            try:
                                next(g)
                            except StopIteration:
                                live.remove(g)
                self.chk("l1tile")

    def rwkv_tile(self, B, S, FT, Ep, bonus, t, m0, c0):
        fw, cfg = self.fw, self.cfg
        L, ns, ls, kind, nsteps = t.L, t.ns, t.ls, t.kind, t.nsteps
        tc = slice(t.c0 - c0, t.c0 - c0 + L)
        bc = slice(t.c0, t.c0 + L)
        hm = self.cc("hm")
        blkf = self.cc("blk")
        psD = self.psD
        bankbf = lambda b: V(b, b.ap.bitcast(BF16))
        with fw.scope():
            tok4 = fw.sb([128, 2, 4, 128], BF16, "tok4")
            BKp = fw.sb([128, 4, 2, 128], BF16, "BKp")
            AT4 = fw.sb([128, 4, 4, 128], BF16, "AT4")
            PX = [fw.sb([128, 4, 256], BF16, f"PX{i}") for i in range(2)]
            Qb = [fw.sb([128, 4, 128], BF16, f"Qb{i}") for i in range(2)]
            X32 = fw.sb([128, 4, 128], F32, "X32")
            AH = fw.sb([128, 4, 64], BF16, "AH")
            ATp = fw.sb([128, 4, 128], BF16, "ATp")
            Ubf = fw.sb([128, 4, 128], BF16, "UV")
            Hbf = fw.sb([128, 2, ns, 64], BF16, "Hbf")
            Hpad = fw.sb([128, 4, ns, 64], BF16, "Hpad")
            yt = fw.sb([128, 128], F32, "yt")
            ysq = fw.sb([128, 128], F32, "ysq")
            yr = fw.sb([128, 128], F32, "yr")
            if kind == "s":
                H32t = fw.sb([128, 2, ns, 64], F32, "H32s")
                fw.dma("sp", H32t.v, self.din["rwS"][m0:m0 + 2].rearrange("m p s v -> p m s v"))
                H32 = H32t.v
                AM = [fw.sb([128, ns, 128], BF16, f"AM{i}") for i in range(2)]
                UVm = [fw.sb([128, ns, 128], BF16, f"UVm{i}") for i in range(2)]
            else:
                H32 = self.pH[:, m0:m0 + 2]
            fw.copy("act", Hbf.v, H32)
            for hh in range(4):
                fw.act(Hpad[:, hh], H32[:, hh // 2], AF.Identity, scale=hm[:, hh % 2:hh % 2 + 1])
            for ml in range(2):
                pb = bankbf(self.psum())
                for i, src in enumerate((FT[:, ml, 0, tc], FT[:, ml, 2, tc], FT[:, ml, 3, tc], S.v[:, m0 + ml, tc])):
                    fw.tr(pb[0:L, i * 128:(i + 1) * 128], src, self.identb.v)
                fw.copy("act", tok4[0:L, ml], pb[0:L, 0:512].re("p (a k) -> p a k", a=4))
            self.chk("t_tr")
            for hh in range(4):
                ml, h2 = hh // 2, hh % 2
                for i, src in enumerate((2, 3)):
                    fw.act(BKp[:, hh, i, 0:L], FT[:, ml, src, tc], AF.Identity, scale=hm[:, h2:h2 + 1])
            self.chk("t_pad")
            pP = self.psum()
            for hh in range(4):
                ml = hh // 2
                bk = self.bank[4 + hh]
                rhsAR = FT[:, ml, 0:2, tc]
                fw.mm(bk[0:L, 0:2 * L].re("p (a l) -> p a l", a=2), BKp[:, hh, 0, 0:L], rhsAR)
                fw.mm(bk[0:L, 2 * L:4 * L].re("p (a l) -> p a l", a=2), BKp[:, hh, 1, 0:L], rhsAR)
                fw.mm(pP[0:L, hh * 128:hh * 128 + L], FT[:, ml, 0, tc], BKp[:, hh, 0, 0:L])
            m2 = self.cc(kind + "_m2").re("p (a l) -> p a l", a=2)[0:L, :, 0:L]
            for rep in range(2):
                src = psD[0:L, :, rep * 2 * L:(rep + 1) * 2 * L].re("p h (a l) -> p h a l", a=2)
                fw.tt("dve", AT4[0:L, :, 2 * rep:2 * rep + 2, 0:L], src, m2.un(1).bc([L, 4, 2, L]), ALU.mult)
            fw.tt("dve", PX[0][0:L, :, 0:L], pP[0:L, 0:512].re("p (h l) -> p h l", h=4)[:, :, 0:L],
                  self.cc(kind + "_strictT", rows=L, n=L).un(1).bc([L, 4, L]), ALU.mult)
            self.chk("t_A")
            pZ = self.psum()
            for hh in range(4):
                ml, h2 = hh // 2, hh % 2
                fw.mm(pZ[0:L, hh * 64:(hh + 1) * 64], AT4[0:L, hh, 2, 0:L], tok4[0:L, ml, 3, h2 * 64:(h2 + 1) * 64])
            fw.copy("act", X32[0:L, :, 64:128], pZ[0:L, 0:256].re("p (h v) -> p h v", h=4))
            for ml in range(2):
                fw.copy("dve", X32[0:L, 2 * ml:2 * ml + 2, 0:64], tok4[0:L, ml, 0, :].re("p (h k) -> p h k", h=2))
            fw.copy("act", PX[0][0:L, :, L:L + 128], X32[0:L])
            self.chk("t_Z")
            for i in range(nsteps):
                cur, nxt = i % 2, (i + 1) % 2
                last = (i == nsteps - 1)
                for hh in range(4):
                    Qi = AT4[0:L, hh, 0, 0:L] if i == 0 else Qb[cur][0:L, hh, 0:L]
                    bk = self.bank[4 + hh]
                    if not last:
                        fw.mm(bk[0:L, 0:L + 128], Qi, PX[cur][0:L, hh, 0:L + 128])
                        fw.mm(bk[0:L, 256:256 + L], PX[cur][0:L, hh, 0:L], Qi)
                    else:
                        fw.mm(bk[0:L, L:L + 128], Qi, PX[cur][0:L, hh, L:L + 128])
                self.chk(f"d_mm{i}")
                for half in range(2):
                    hs = slice(2 * half, 2 * half + 2)
                    pv_ = V(self.bank[4 + 2 * half], psD.ap[0:L, hs, :], ts=self.bank[4 + 2 * half:6 + 2 * half])
                    if not last:
                        for hh in (2 * half, 2 * half + 1):
                            qsrc = self.bank[4 + hh][0:L, 256:256 + L]
                            fw.op("act", lambda: self.nc.scalar.copy(Qb[nxt][0:L, hh, 0:L].ap, qsrc.ap), [qsrc], [Qb[nxt][0:L, hh, 0:L], qsrc])
                        fw.copy("dve", PX[nxt][0:L, hs, 0:L], pv_[:, :, 0:L])
                        fw.tt("dve", PX[nxt][0:L, hs, L:L + 128], X32[0:L, hs, :], pv_[:, :, L:L + 128], ALU.add)
                    fw.tt("dve", X32[0:L, hs, :], X32[0:L, hs, :], pv_[:, :, L:L + 128], ALU.add)
                self.chk(f"d_end{i}")
            self.chk("t_dbl")
            fw.copy("dve", AH[0:L], X32[0:L, :, 0:64])
            for ml in range(2):
                pb = bankbf(self.psum())
                fw.tr(pb[:, 0:L], AH[0:L, 2 * ml:2 * ml + 2, :].re("p h k -> p (h k)"), self.identb[0:L, 0:L])
                for h2 in range(2):
                    fw.ts("dve", ATp[:, 2 * ml + h2, 0:L], pb[:, 0:L], hm[:, h2:h2 + 1], None, ALU.mult)
            self.chk("t_AT")
            pU = self.psum()
            for hh in range(4):
                ml = hh // 2
                if ns == 1:
                    fw.mm(pU[0:L, hh * 64:(hh + 1) * 64], ATp[:, hh, 0:L], Hbf[:, ml, 0, :])
                else:
                    am = AM[hh % 2]
                    fw.tt("dve", am[:, :, 0:L], ATp[:, hh, 0:L].un(1).bc([128, ns, L]),
                          self.colmb.v.re("p (s t) -> p s t", s=ns)[:, :, 0:L], ALU.mult)
                    for s in range(ns):
                        fw.mm(pU[0:L, hh * 64:(hh + 1) * 64], am[:, s, 0:L], Hbf[:, ml, s, :], start=(s == 0), stop=(s == ns - 1))
            fw.tt("dve", Ubf[0:L, :, 0:64], pU[0:L, 0:256].re("p (h v) -> p h v", h=4), X32[0:L, :, 64:128], ALU.add)
            for ml in range(2):
                fw.copy("act", Ubf[0:L, 2 * ml:2 * ml + 2, 64:128], tok4[0:L, ml, 3, :].re("p (h v) -> p h v", h=2))
            self.chk("t_U")
            pY = [self.psum(), self.psum()]
            for hh in range(4):
                ml, h2 = hh // 2, hh % 2
                out = pY[ml][64 * h2:64 * h2 + 64, 0:L]
                fw.mm(out, Ubf[0:L, hh, 0:64], AT4[0:L, hh, 1, 0:L], start=True, stop=False)
                fw.mm(out, Ubf[0:L, hh, 64:128], AT4[0:L, hh, 3, 0:L], start=False, stop=False)
                for s in range(ns):
                    sc = slice(s * ls, (s + 1) * ls)
                    fw.mm(pY[ml][64 * h2:64 * h2 + 64, sc], Hpad[:, hh, s, :], FT[:, ml, 1, tc][:, sc], start=False, stop=(s == ns - 1))
            self.chk("t_Y")
            for hh in range(4):
                ml, h2 = hh // 2, hh % 2
                if ns > 1:
                    uvm = UVm[hh % 2]
                    fw.tt("dve", uvm[0:L], Ubf[0:L, hh, :].un(1).bc([L, ns, 128]),
                          self.cc(kind + "_rowmask", rows=L, n=ns).un(2).bc([L, ns, 128]), ALU.mult)
                for s in range(ns):
                    slot = ml * ns + s
                    bk = self.bank[4 + slot // 8]
                    out = bk[64 * h2:64 * h2 + 64, (slot % 8) * 64:(slot % 8 + 1) * 64]
                    rU = Ubf[0:L, hh, 0:64] if ns == 1 else uvm[0:L, s, 0:64]
                    rV = Ubf[0:L, hh, 64:128] if ns == 1 else uvm[0:L, s, 64:128]
                    fw.mm(out, tok4[0:L, ml, 1, h2 * 64:(h2 + 1) * 64], rU, start=True, stop=False)
                    fw.mm(out, tok4[0:L, ml, 2, h2 * 64:(h2 + 1) * 64], rV, start=False, stop=True)
            nslot = 2 * ns
            for b in range((nslot + 7) // 8):
                w = min(8, nslot - b * 8)
                hv_ = H32.re("p m s v -> p (m s) v")[:, b * 8:b * 8 + w, :]
                fw.tt("dve", hv_, hv_, self.bank[4 + b][:, 0:w * 64].re("p (a v) -> p a v", a=w), ALU.add)
            for ml in range(2):
                wl = Ep[:, ml, tc].re("p (s l) -> p s l", s=ns)[:, :, ls - 1:ls].bc([128, ns, 64])
                fw.tt("dve", H32[:, ml], H32[:, ml], wl, ALU.mult)
            if kind == "s":
                fw.dma("sp", self.dout["srwS"][m0:m0 + 2].rearrange("m p s v -> p m s v"), H32)
            self.chk("t_H")
            for ml in range(2):
                m = m0 + ml
                fw.copy("act", yt[:, 0:L], pY[ml][:, 0:L])
                p1 = self.psum()
                fw.mm(p1[:, 0:L], blkf, yt[:, 0:L])
                fw.stt(yt[:, 0:L], p1[:, 0:L], -1.0 / 64, yt[:, 0:L], ALU.mult, ALU.add)
                fw.act(ysq[:, 0:L], yt[:, 0:L], AF.Square)
                p2 = self.psum()
                fw.mm(p2[:, 0:L], blkf, ysq[:, 0:L])
                fw.act(yr[:, 0:L], p2[:, 0:L], AF.Sqrt, bias=self.epsc("gn"), scale=1.0 / 64)
                fw.recip(yr[:, 0:L], yr[:, 0:L])
                fw.tt("dve", yt[:, 0:L], yt[:, 0:L], yr[:, 0:L], ALU.mult)
                fw.ts("dve", yt[:, 0:L], yt[:, 0:L], self.vc("lxg", m), self.vc("lxb", m), ALU.mult, ALU.add)
                fw.tt("dve", yt[:, 0:L], yt[:, 0:L], bonus[:, ml, tc], ALU.add)
                fw.tt("dve", B.abuf[:, m, bc], yt[:, 0:L], S.g[:, m, tc], ALU.mult)


    def rwkv_chain(self, B, S, FT, Ep, bonus, t, m0, ml, c0):
        fw, cfg = self.fw, self.cfg
        L, ns, ls, kind, nsteps = t.L, t.ns, t.ls, t.kind, t.nsteps
        tc = slice(t.c0 - c0, t.c0 - c0 + L)
        bc = slice(t.c0, t.c0 + L)
        hm = self.cc("hm")
        blkf = self.cc("blk")
        m = m0 + ml
        D0, D1 = self.bank[4 + 2 * ml], self.bank[5 + 2 * ml]
        Db = (D0, D1)
        psD2 = V(D0, self.psD.ap[:, 2 * ml:2 * ml + 2, :], ts=[D0, D1])
        pool = (self.bank[2 * ml], self.bank[2 * ml + 1])
        pi = [0]

        def nextp():
            pi[0] += 1
            return pool[pi[0] % 2]
        bankbf = lambda b: V(b, b.ap.bitcast(BF16))
        if True:
            tok4 = fw.sb([128, 4, 128], BF16, "tok4")
            BKp = self.FTp[:, ml, :, :, tc]
            AT4 = fw.sb([128, 2, 4, 128], BF16, "AT4")
            PX = [fw.sb([128, 2, 256], BF16, f"PX{i}") for i in range(2)]
            Qb = [fw.sb([128, 2, 128], BF16, f"Qb{i}") for i in range(2)]
            X32 = fw.sb([128, 2, 128], F32, "X32")
            AH = fw.sb([128, 2, 64], BF16, "AH")
            ATp = fw.sb([128, 2, 128], BF16, "ATp")
            Ubf = fw.sb([128, 2, 128], BF16, "UV")
            Hbf = fw.sb([128, ns, 64], BF16, "Hbf")
            Hpad = fw.sb([128, 2, ns, 64], BF16, "Hpad")
            yt = fw.sb([128, 128], F32, "yt")
            ysq = fw.sb([128, 128], F32, "ysq")
            yr = fw.sb([128, 128], F32, "yr")
            if kind == "s":
                H32t = fw.sb([128, ns, 64], F32, "H32s")
                fw.dma("sp", H32t.v, self.din["rwS"][m].rearrange("p s v -> p s v"))
                H32 = H32t.v
                AM = fw.sb([128, ns, 128], BF16, "AM")
                UVm = fw.sb([128, ns, 128], BF16, "UVm")
            else:
                H32 = self.pH[:, m]
            fw.copy("act", Hbf.v, H32)
            for h2 in range(2):
                fw.act(Hpad[:, h2], H32, AF.Identity, scale=hm[:, h2:h2 + 1])
            pb = bankbf(nextp())
            for i, src in enumerate((FT[:, ml, 0, tc], FT[:, ml, 2, tc], FT[:, ml, 3, tc], S.v[:, m, tc])):
                fw.tr(pb[0:L, i * 128:(i + 1) * 128], src, self.identb.v)
            fw.copy("act", tok4[0:L], pb[0:L, 0:512].re("p (a k) -> p a k", a=4))
            yield
            pP = nextp()
            rhsAR = FT[:, ml, 0:2, tc]
            for h2 in range(2):
                bk = Db[h2]
                fw.mm(bk[0:L, 0:2 * L].re("p (a l) -> p a l", a=2), BKp[:, h2, 0, 0:L], rhsAR)
                fw.mm(bk[0:L, 2 * L:4 * L].re("p (a l) -> p a l", a=2), BKp[:, h2, 1, 0:L], rhsAR)
                fw.mm(pP[0:L, h2 * 128:h2 * 128 + L], FT[:, ml, 0, tc], BKp[:, h2, 0, 0:L])
            m2 = self.cc(kind + "_m2").re("p (a l) -> p a l", a=2)[0:L, :, 0:L]
            for rep in range(2):
                src = psD2[0:L, :, rep * 2 * L:(rep + 1) * 2 * L].re("p h (a l) -> p h a l", a=2)
                fw.tt("dve", AT4[0:L, :, 2 * rep:2 * rep + 2, 0:L], src, m2.un(1).bc([L, 2, 2, L]), ALU.mult)
            fw.tt("dve", PX[0][0:L, :, 0:L], pP[0:L, 0:256].re("p (h l) -> p h l", h=2)[:, :, 0:L],
                  self.cc(kind + "_strictT", rows=L, n=L).un(1).bc([L, 2, L]), ALU.mult)
            yield
            pZ = nextp()
            for h2 in range(2):
                fw.mm(pZ[0:L, h2 * 64:(h2 + 1) * 64], AT4[0:L, h2, 2, 0:L], tok4[0:L, 3, h2 * 64:(h2 + 1) * 64])
            fw.copy("act", X32[0:L, :, 64:128], pZ[0:L, 0:128].re("p (h v) -> p h v", h=2))
            fw.copy("dve", X32[0:L, :, 0:64], tok4[0:L, 0, :].re("p (h k) -> p h k", h=2))
            fw.copy("act", PX[0][0:L, :, L:L + 128], X32[0:L])
            yield
            for i in range(nsteps):
                cur, nxt = i % 2, (i + 1) % 2
                last = (i == nsteps - 1)
                for h2 in range(2):
                    Qi = AT4[0:L, h2, 0, 0:L] if i == 0 else Qb[cur][0:L, h2, 0:L]
                    bk = Db[h2]
                    if not last:
                        fw.mm(bk[0:L, 0:L + 128], Qi, PX[cur][0:L, h2, 0:L + 128])
                        fw.mm(bk[0:L, 256:256 + L], PX[cur][0:L, h2, 0:L], Qi)
                    else:
                        fw.mm(bk[0:L, L:L + 128], Qi, PX[cur][0:L, h2, L:L + 128])
                if not last:
                    for h2 in range(2):
                        qsrc = Db[h2][0:L, 256:256 + L]
                        qdst = Qb[nxt][0:L, h2, 0:L]
                        fw.op("act", lambda qdst=qdst, qsrc=qsrc: self.nc.scalar.copy(qdst.ap, qsrc.ap), [qsrc], [qdst, qsrc])
                    fw.copy("dve", PX[nxt][0:L, :, 0:L], psD2[0:L, :, 0:L])
                    fw.tt("dve", PX[nxt][0:L, :, L:L + 128], PX[cur][0:L, :, L:L + 128], psD2[0:L, :, L:L + 128], ALU.add)
                else:
                    fw.tt("dve", X32[0:L], PX[cur][0:L, :, L:L + 128], psD2[0:L, :, L:L + 128], ALU.add)
                yield
            fw.copy("dve", AH[0:L], X32[0:L, :, 0:64])
            pb = bankbf(nextp())
            fw.tr(pb[:, 0:L], AH[0:L].re("p h k -> p (h k)"), self.identb[0:L, 0:L])
            for h2 in range(2):
                fw.act(ATp[:, h2, 0:L], pb[:, 0:L], AF.Identity, scale=hm[:, h2:h2 + 1])
            yield
            pU = nextp()
            for h2 in range(2):
                if ns == 1:
                    fw.mm(pU[0:L, h2 * 64:(h2 + 1) * 64], ATp[:, h2, 0:L], Hbf[:, 0, :])
                else:
                    fw.tt("dve", AM[:, :, 0:L], ATp[:, h2, 0:L].un(1).bc([128, ns, L]),
                          self.colmb.v.re("p (s t) -> p s t", s=ns)[:, :, 0:L], ALU.mult)
                    for s in range(ns):
                        fw.mm(pU[0:L, h2 * 64:(h2 + 1) * 64], AM[:, s, 0:L], Hbf[:, s, :], start=(s == 0), stop=(s == ns - 1))
            fw.tt("dve", Ubf[0:L, :, 0:64], pU[0:L, 0:128].re("p (h v) -> p h v", h=2), X32[0:L, :, 64:128], ALU.add)
            fw.copy("act", Ubf[0:L, :, 64:128], tok4[0:L, 3, :].re("p (h v) -> p h v", h=2))
            yield
            pY = nextp()
            for h2 in range(2):
                out = pY[64 * h2:64 * h2 + 64, 0:L]
                fw.mm(out, Ubf[0:L, h2, 0:64], AT4[0:L, h2, 1, 0:L], start=True, stop=False)
                fw.mm(out, Ubf[0:L, h2, 64:128], AT4[0:L, h2, 3, 0:L], start=False, stop=False)
                for s in range(ns):
                    sc = slice(s * ls, (s + 1) * ls)
                    fw.mm(pY[64 * h2:64 * h2 + 64, sc], Hpad[:, h2, s, :], FT[:, ml, 1, tc][:, sc], start=False, stop=(s == ns - 1))
            yield
            for h2 in range(2):
                if ns > 1:
                    fw.tt("dve", UVm[0:L], Ubf[0:L, h2, :].un(1).bc([L, ns, 128]),
                          self.cc(kind + "_rowmask", rows=L, n=ns).un(2).bc([L, ns, 128]), ALU.mult)
                for s in range(ns):
                    bk = Db[s // 8]
                    out = bk[64 * h2:64 * h2 + 64, (s % 8) * 64:(s % 8 + 1) * 64]
                    rU = Ubf[0:L, h2, 0:64] if ns == 1 else UVm[0:L, s, 0:64]
                    rV = Ubf[0:L, h2, 64:128] if ns == 1 else UVm[0:L, s, 64:128]
                    fw.mm(out, tok4[0:L, 1, h2 * 64:(h2 + 1) * 64], rU, start=True, stop=False)
                    fw.mm(out, tok4[0:L, 2, h2 * 64:(h2 + 1) * 64], rV, start=False, stop=True)
            for b in range((ns + 7) // 8):
                w = min(8, ns - b * 8)
                hv_ = H32[:, b * 8:b * 8 + w, :]
                fw.tt("dve", hv_, hv_, Db[b][:, 0:w * 64].re("p (a v) -> p a v", a=w), ALU.add)
            wl = Ep[:, ml, tc].re("p (s l) -> p s l", s=ns)[:, :, ls - 1:ls].bc([128, ns, 64])
            fw.tt("dve", H32, H32, wl, ALU.mult)
            if kind == "s":
                fw.dma("sp", self.dout["srwS"][m].rearrange("p s v -> p s v"), H32)
            yield
            fw.copy("act", yt[:, 0:L], pY[:, 0:L])
            p1 = nextp()
            fw.mm(p1[:, 0:L], blkf, yt[:, 0:L])
            fw.stt(yt[:, 0:L], p1[:, 0:L], -1.0 / 64, yt[:, 0:L], ALU.mult, ALU.add)
            fw.act(ysq[:, 0:L], yt[:, 0:L], AF.Square)
            p2 = nextp()
            fw.mm(p2[:, 0:L], blkf, ysq[:, 0:L])
            fw.act(yr[:, 0:L], p2[:, 0:L], AF.Ln, bias=self.epsc("gn"), scale=1.0 / 64)
            fw.act(yr[:, 0:L], yr[:, 0:L], AF.Exp, scale=-0.5)
            fw.tt("dve", yt[:, 0:L], yt[:, 0:L], yr[:, 0:L], ALU.mult)
            fw.act(yt[:, 0:L], yt[:, 0:L], AF.Identity, bias=self.vc("lxb", m), scale=self.vc("lxg", m))
            fw.tt("dve", yt[:, 0:L], yt[:, 0:L], bonus[:, ml, tc], ALU.add)
            fw.tt("dve", B.abuf[:, m, bc], yt[:, 0:L], S.g[:, m, tc], ALU.mult)
            yield


def build_program(cfg, debug=()):
    nc0 = bass.Bass("TRN2", target_bir_lowering=False)
    p0 = Prog(nc0, cfg, plan=None, debug=debug)
    with nc0.allow_non_contiguous_dma(reason="small strided state/vector transfers"):
        p0.build()
    nc = bass.Bass("TRN2", target_bir_lowering=False)
    p = Prog(nc, cfg, plan=p0.plan, debug=debug)
    with nc.allow_non_contiguous_dma(reason="small strided state/vector transfers"):
        p.build()
    return nc, p


def prep_shared(cfg, P):
    cst, _, colm = make_consts(cfg)
    m = {"cst": cst, "colm": colm, "vec": make_vecs(cfg, P)}
    m.update(host_weights(cfg, P))
    bif = np.asarray(P["b_if_ab"][0], np.float32)
    m["bif"] = np.ascontiguousarray(np.stack([bif[:cfg.NHA], bif[cfg.NHA:]], 1))
    return m


def prep_core(cfg, P, xp_seq, sl):
    D, NS, NC = cfg.D, cfg.NS, cfg.NC
    f = lambda a: np.asarray(a, np.float32)
    m = {}
    xfull = np.concatenate([f(P["meta_tokens"]), f(xp_seq)], 0)
    m["xp"] = np.ascontiguousarray(xfull.T)
    xs = f(P["x_sample"])[sl]
    m["xs"] = np.ascontiguousarray(xs.reshape(NS * cfg.LS, D).T)
    C = f(P["state_mlstm_C"])[0, sl]
    n = f(P["state_mlstm_n"])[0, sl]
    m["cext"] = np.ascontiguousarray(np.concatenate([C, n[..., None]], -1))
    m["m0"] = np.ascontiguousarray(f(P["state_mlstm_m"])[0, sl].T)
    m["rgh"] = np.ascontiguousarray(f(P["state_rglru_h"])[0, sl].T)
    m["rgc"] = np.ascontiguousarray(f(P["state_rglru_conv"])[0, sl].transpose(2, 0, 1))
    S = f(P["state_rwkv_S"])[0, sl]
    H = S.transpose(0, 1, 3, 2).reshape(NS, NC, 2, 64, 64)
    m["rwS"] = np.ascontiguousarray(H.transpose(1, 2, 3, 0, 4).reshape(NC, 128, NS, 64))
    m["rwx"] = np.ascontiguousarray(f(P["state_rwkv_shift"])[0, sl].T)
    return m


def unpack_core(cfg, r):
    NS, NC, NHA, HDA = cfg.NS, cfg.NC, cfg.NHA, cfg.HDA
    o = {}
    o["yp"] = r["yp"].T[16:]
    o["ys"] = r["ys"].T.reshape(NS, cfg.LS, cfg.D)
    for pre, nb in (("p", 1), ("s", NS)):
        c = r[pre + "c"]
        o[pre + "C"] = c[..., :HDA]
        o[pre + "n"] = c[..., HDA]
        o[pre + "m"] = r[pre + "m"].T
        o[pre + "rgh"] = r[pre + "rgh"].T
        o[pre + "rgc"] = r[pre + "rgc"].transpose(1, 2, 0)
        H = r[pre + "rwS"].reshape(NC, 2, 64, nb, 64)
        o[pre + "S"] = H.transpose(3, 0, 1, 4, 2).reshape(nb, NC * 2, 64, 64)
        o[pre + "x"] = r[pre + "rwx"].T
    return o


_CACHE = {}


def kernel(**inputs):
    cfg = Cfg()
    P = inputs
    if "prog" not in _CACHE:
        _CACHE["prog"] = build_program(cfg)
    nc, prog = _CACHE["prog"]
    shared = prep_shared(cfg, P)
    in_maps = []
    for c in range(8):
        m = dict(shared)
        m.update(prep_core(cfg, P, np.asarray(P["x_prompt"])[c // 2], slice(c * cfg.NS, (c + 1) * cfg.NS)))
        in_maps.append(m)
    res = run_bass_kernel_spmd(nc, in_maps, core_ids=list(range(8)))
    outs = [unpack_core(cfg, r) for r in res.results]
    cat = lambda k, cores: np.ascontiguousarray(np.concatenate([outs[c][k] for c in cores], 0)).astype(np.float32)
    pc = [0, 2, 4, 6]
    ac = list(range(8))
    yp = np.stack([outs[c]["yp"] for c in pc], 0).astype(np.float32)
    ys = cat("ys", ac)
    res_t = [yp, ys]
    for pre, cores in (("p", pc), ("s", ac)):
        for k in ("C", "n", "m", "rgh", "rgc", "S", "x"):
            res_t.append(cat(pre + k, cores)[None])
    return tuple(res_t)
```
